# Optimizing a Trainium2 kernel written in Bass

```python
import math
import jax
import jax.numpy as jnp
from jax import lax
import numpy as np

D_MODEL = 1024
BATCH = 2
SEQ = 8192
DEPTH = 1
DEC_BATCH = 16
DEC_SEQ = 16
PAST_LEN = 1024

CHUNK = 64
MIX_WIDTH = D_MODEL
GDN_HEAD_DIM = 128
GDN_WIDTH = MIX_WIDTH // 2
GDN_HEADS = GDN_WIDTH // GDN_HEAD_DIM
GDN_CONV = 4
GDN_QKV = 3 * GDN_WIDTH
GDN_PROJ = GDN_QKV + GDN_WIDTH + 2 * GDN_HEADS
RWKV_HEAD_DIM = 64
RWKV_WIDTH = MIX_WIDTH - GDN_WIDTH
RWKV_HEADS = RWKV_WIDTH // RWKV_HEAD_DIM
DECAY_LORA = 64
AAA_LORA = 64
GATE_LORA = 160
RWKV_PROJ = 3 * RWKV_WIDTH + DECAY_LORA + AAA_LORA + GATE_LORA
P_IN = GDN_PROJ + RWKV_PROJ
FFN_DIM = 2816
FFN_CONV = 3
NORM_EPS = 1e-6
L2_EPS = 1e-6
RWKV_LN_EPS = 64e-5

kernel_name = 'hymba_gdn_rwkv7_convffn_stream_step'


def rms_norm(x, w, eps=NORM_EPS):
    x32 = x.astype(jnp.float32)
    y = x32 * lax.rsqrt(jnp.mean(x32 * x32, axis=-1, keepdims=True) + eps)
    return (y * w.astype(jnp.float32)).astype(x.dtype)


def l2_normalize(x):
    return x * lax.rsqrt(jnp.sum(x * x, axis=-1, keepdims=True) + L2_EPS)


def causal_dwconv(x, buf, w):
    width, chans = w.shape
    xp = jnp.concatenate([buf.astype(x.dtype), x], axis=1)
    y = lax.conv_general_dilated(xp, w.astype(x.dtype)[:, None, :], window_strides=(1,), padding='VALID',
                                 dimension_numbers=('NWC', 'WIO', 'NWC'), feature_group_count=chans)
    return y, xp[:, xp.shape[1] - (width - 1):]


def gdn_chunked(q, k, v, log_g, beta, s0):
    bsz, t_len, n_h, dk = q.shape
    dv = v.shape[-1]
    c = min(CHUNK, t_len)
    n = t_len // c

    def blocks(a):
        a = a.reshape((bsz, n, c) + a.shape[2:])
        return jnp.moveaxis(jnp.moveaxis(a, 1, 0), 3, 2)

    qb, kb, vb, bb = blocks(q), blocks(k), blocks(v), blocks(beta)
    gb = jnp.cumsum(blocks(log_g), axis=-1)
    causal = jnp.tril(jnp.ones((c, c), dtype=bool))
    strict = jnp.tril(jnp.ones((c, c), dtype=bool), -1)
    decay = jnp.exp(jnp.where(causal, gb[..., :, None] - gb[..., None, :], -jnp.inf))
    kbeta = kb * bb[..., None]
    lmat = jnp.where(strict, jnp.einsum('nbhid,nbhjd->nbhij', kbeta, kb) * decay, 0.0)
    a_mat = lmat + jnp.eye(c, dtype=lmat.dtype)
    rhs = jnp.concatenate([vb * bb[..., None], kbeta * jnp.exp(gb)[..., None]], axis=-1)
    sol = lax.linalg.triangular_solve(a_mat, rhs, left_side=True, lower=True, unit_diagonal=True)
    value, kcd = sol[..., :dv], sol[..., dv:]
    attn = jnp.einsum('nbhid,nbhjd->nbhij', qb, kb) * decay
    q_dec = qb * jnp.exp(gb)[..., None]
    k_dec = kb * jnp.exp(gb[..., -1:] - gb)[..., None]
    g_last = jnp.exp(gb[..., -1])

    def step(s, xs):
        val_c, kcd_c, q_c, k_c, attn_c, gl_c = xs
        v_new = val_c - jnp.einsum('bhcd,bhde->bhce', kcd_c, s)
        o_c = jnp.einsum('bhcd,bhde->bhce', q_c, s) + jnp.einsum('bhij,bhje->bhie', attn_c, v_new)
        s = s * gl_c[..., None, None] + jnp.einsum('bhcd,bhce->bhde', k_c, v_new)
        return s, o_c

    s_fin, o = lax.scan(step, s0, (value, kcd, q_dec, k_dec, attn, g_last))
    o = jnp.transpose(o, (1, 0, 3, 2, 4)).reshape(bsz, t_len, n_h, dv)
    return o, s_fin


def rwkv7_recurrence(r, log_w, k, v, a_vec, b_vec, s0):
    def step(s, xs):
        r_t, lw_t, k_t, v_t, a_t, b_t = xs
        sa = jnp.einsum('bhvk,bhk->bhv', s, a_t)
        s = (s * jnp.exp(lw_t)[:, :, None, :] + sa[..., :, None] * b_t[..., None, :]
             + v_t[..., :, None] * k_t[..., None, :])
        return s, jnp.einsum('bhvk,bhk->bhv', s, r_t)

    xs = (jnp.moveaxis(r, 1, 0), jnp.moveaxis(log_w, 1, 0), jnp.moveaxis(k, 1, 0),
          jnp.moveaxis(v, 1, 0), jnp.moveaxis(a_vec, 1, 0), jnp.moveaxis(b_vec, 1, 0))
    s_fin, y = lax.scan(step, s0, xs)
    return jnp.moveaxis(y, 0, 1), s_fin


def hybrid_layer(x, gdn_conv_buf, s_gdn, shift_buf, s_rwkv, ffn_buf,
                 norm_mix, w_in, gdn_conv_w, gdn_a_log, gdn_dt_bias, gdn_norm,
                 rwkv_mu, rwkv_w0, rwkv_w2, rwkv_a0, rwkv_a2, rwkv_g2, rwkv_k_k, rwkv_k_a, rwkv_r_k,
                 rwkv_ln_w, rwkv_ln_b, w_out, norm_ffn, w_up, ffn_conv_w, w_down):
    f32 = jnp.float32
    bsz, t_len, _ = x.shape
    h = rms_norm(x, norm_mix)
    p = h @ w_in
    p_gdn, p_rwkv = p[..., :GDN_PROJ], p[..., GDN_PROJ:]

    qkv, gdn_conv_new = causal_dwconv(p_gdn[..., :GDN_QKV], gdn_conv_buf, gdn_conv_w)
    qkv = jax.nn.silu(qkv.astype(f32))
    q, k, v = jnp.split(qkv, 3, axis=-1)
    gh = lambda a: a.reshape(bsz, t_len, GDN_HEADS, GDN_HEAD_DIM)
    q = l2_normalize(gh(q)) * (GDN_HEAD_DIM ** -0.5)
    k = l2_normalize(gh(k))
    v = gh(v)
    z = p_gdn[..., GDN_QKV:GDN_QKV + GDN_WIDTH].astype(f32)
    b_logit = p_gdn[..., GDN_QKV + GDN_WIDTH:GDN_QKV + GDN_WIDTH + GDN_HEADS].astype(f32)
    a_logit = p_gdn[..., GDN_QKV + GDN_WIDTH + GDN_HEADS:].astype(f32)
    beta = jax.nn.sigmoid(b_logit)
    log_g = -jnp.exp(gdn_a_log.astype(f32)) * jax.nn.softplus(a_logit + gdn_dt_bias.astype(f32))
    o_gdn, s_gdn_new = gdn_chunked(q, k, v, log_g, beta, s_gdn.astype(f32))
    o_gdn = (rms_norm(o_gdn, gdn_norm) * jax.nn.silu(gh(z))).reshape(bsz, t_len, GDN_WIDTH)

    ps, shift_new = causal_dwconv(p_rwkv, shift_buf, jnp.stack([rwkv_mu, 1.0 - rwkv_mu]))
    ps = ps.astype(f32)
    o1, o2, o3 = RWKV_WIDTH, 2 * RWKV_WIDTH, 3 * RWKV_WIDTH
    o4, o5 = o3 + DECAY_LORA, o3 + DECAY_LORA + AAA_LORA
    r, kr, vr = ps[..., :o1], ps[..., o1:o2], ps[..., o2:o3]
    wd, ad, gd = ps[..., o3:o4], ps[..., o4:o5], ps[..., o5:]
    w = -jax.nn.softplus(-(rwkv_w0.astype(f32) + jnp.tanh(wd) @ rwkv_w2.astype(f32))) - 0.5
    log_w = -jnp.exp(w)
    a = jax.nn.sigmoid(rwkv_a0.astype(f32) + ad @ rwkv_a2.astype(f32))
    g = jax.nn.sigmoid(gd) @ rwkv_g2.astype(f32)
    rh = lambda t: t.reshape(bsz, t_len, RWKV_HEADS, RWKV_HEAD_DIM)
    kk = l2_normalize(rh(kr * rwkv_k_k.astype(f32)))
    kr = kr * (1.0 + (a - 1.0) * rwkv_k_a.astype(f32))
    y, s_rwkv_new = rwkv7_recurrence(rh(r), rh(log_w), rh(kr), rh(vr), -kk, kk * rh(a), s_rwkv.astype(f32))
    y_mean = jnp.mean(y, axis=-1, keepdims=True)
    y_c = y - y_mean
    y_n = y_c * lax.rsqrt(jnp.mean(y_c * y_c, axis=-1, keepdims=True) + RWKV_LN_EPS)
    y_n = y_n.reshape(bsz, t_len, RWKV_WIDTH) * rwkv_ln_w.astype(f32) + rwkv_ln_b.astype(f32)
    bonus = jnp.sum(rh(r) * rh(kr) * rwkv_r_k.astype(f32), axis=-1, keepdims=True) * rh(vr)
    o_rwkv = (y_n + bonus.reshape(bsz, t_len, RWKV_WIDTH)) * g

    o = jnp.concatenate([o_gdn, o_rwkv], axis=-1).astype(x.dtype) @ w_out
    x = x + o

    h2 = rms_norm(x, norm_ffn)
    up = h2 @ w_up
    gate, ffn_new = causal_dwconv(up[..., :FFN_DIM], ffn_buf, ffn_conv_w)
    x = x + (jax.nn.silu(gate) * up[..., FFN_DIM:]) @ w_down
    new_states = (gdn_conv_new.astype(gdn_conv_buf.dtype), s_gdn_new.astype(s_gdn.dtype),
                  shift_new.astype(shift_buf.dtype), s_rwkv_new.astype(s_rwkv.dtype),
                  ffn_new.astype(ffn_buf.dtype))
    return x, new_states


def setup_inputs(seed: int = 0) -> dict:
    key = jax.random.key(seed)
    ks = iter(jax.random.split(key, 40))
    nrm = lambda shape, scale=1.0: scale * jax.random.normal(next(ks), shape, jnp.float32)
    uni = lambda shape, lo, hi: jax.random.uniform(next(ks), shape, jnp.float32, lo, hi)
    L = DEPTH
    dt = jnp.exp(uni((L, GDN_HEADS), math.log(1e-3), math.log(1e-1)))
    return {
        'x_prompt': nrm((BATCH, SEQ, D_MODEL)),
        'x_sample': nrm((DEC_BATCH, DEC_SEQ, D_MODEL)),
        'state_gdn_conv': nrm((L, DEC_BATCH, GDN_CONV - 1, GDN_QKV)),
        'state_gdn': nrm((L, DEC_BATCH, GDN_HEADS, GDN_HEAD_DIM, GDN_HEAD_DIM), 0.3),
        'state_rwkv_shift': nrm((L, DEC_BATCH, 1, RWKV_PROJ)),
        'state_rwkv': nrm((L, DEC_BATCH, RWKV_HEADS, RWKV_HEAD_DIM, RWKV_HEAD_DIM), 0.3),
        'state_ffn_conv': nrm((L, DEC_BATCH, FFN_CONV - 1, FFN_DIM)),
        'norm_mix': 1.0 + nrm((L, D_MODEL), 0.02),
        'w_in': nrm((L, D_MODEL, P_IN), D_MODEL ** -0.5),
        'gdn_conv_w': nrm((L, GDN_CONV, GDN_QKV), GDN_CONV ** -0.5),
        'gdn_a_log': jnp.log(uni((L, GDN_HEADS), 1.0, 16.0)),
        'gdn_dt_bias': dt + jnp.log(-jnp.expm1(-dt)),
        'gdn_norm': 1.0 + nrm((L, GDN_HEAD_DIM), 0.02),
        'rwkv_mu': uni((L, RWKV_PROJ), 0.0, 1.0),
        'rwkv_w0': uni((L, RWKV_WIDTH), -6.0, -1.0),
        'rwkv_w2': nrm((L, DECAY_LORA, RWKV_WIDTH), 0.1),
        'rwkv_a0': nrm((L, RWKV_WIDTH), 0.1),
        'rwkv_a2': nrm((L, AAA_LORA, RWKV_WIDTH), 0.1),
        'rwkv_g2': nrm((L, GATE_LORA, RWKV_WIDTH), 0.1),
        'rwkv_k_k': 0.85 + nrm((L, RWKV_WIDTH), 0.02),
        'rwkv_k_a': 1.0 + nrm((L, RWKV_WIDTH), 0.02),
        'rwkv_r_k': nrm((L, RWKV_HEADS, RWKV_HEAD_DIM), 0.1),
        'rwkv_ln_w': 1.0 + nrm((L, RWKV_WIDTH), 0.02),
        'rwkv_ln_b': nrm((L, RWKV_WIDTH), 0.02),
        'w_out': nrm((L, MIX_WIDTH, D_MODEL), MIX_WIDTH ** -0.5),
        'norm_ffn': 1.0 + nrm((L, D_MODEL), 0.02),
        'w_up': nrm((L, D_MODEL, 2 * FFN_DIM), D_MODEL ** -0.5),
        'ffn_conv_w': nrm((L, FFN_CONV, FFN_DIM), FFN_CONV ** -0.5),
        'w_down': nrm((L, FFN_DIM, D_MODEL), FFN_DIM ** -0.5),
        'norm_final': 1.0 + nrm((D_MODEL,), 0.02),
    }


def reference(x_prompt, x_sample, state_gdn_conv, state_gdn, state_rwkv_shift, state_rwkv, state_ffn_conv,
              norm_mix, w_in, gdn_conv_w, gdn_a_log, gdn_dt_bias, gdn_norm,
              rwkv_mu, rwkv_w0, rwkv_w2, rwkv_a0, rwkv_a2, rwkv_g2, rwkv_k_k, rwkv_k_a, rwkv_r_k,
              rwkv_ln_w, rwkv_ln_b, w_out, norm_ffn, w_up, ffn_conv_w, w_down, norm_final):
    caches = (state_gdn_conv, state_gdn, state_rwkv_shift, state_rwkv, state_ffn_conv)
    n_prompt = x_prompt.shape[0]
    new_p = ([], [], [], [], [])
    new_s = ([], [], [], [], [])
    y_p, y_s = x_prompt, x_sample
    for layer in range(DEPTH):
        params = (norm_mix[layer], w_in[layer], gdn_conv_w[layer], gdn_a_log[layer], gdn_dt_bias[layer],
                  gdn_norm[layer], rwkv_mu[layer], rwkv_w0[layer], rwkv_w2[layer], rwkv_a0[layer],
                  rwkv_a2[layer], rwkv_g2[layer], rwkv_k_k[layer], rwkv_k_a[layer], rwkv_r_k[layer],
                  rwkv_ln_w[layer], rwkv_ln_b[layer], w_out[layer], norm_ffn[layer], w_up[layer],
                  ffn_conv_w[layer], w_down[layer])
        zero_states = [jnp.zeros((n_prompt,) + c.shape[2:], c.dtype) for c in caches]
        y_p, st_p = hybrid_layer(y_p, *zero_states, *params)
        y_s, st_s = hybrid_layer(y_s, *[c[layer] for c in caches], *params)
        for i in range(5):
            new_p[i].append(st_p[i])
            new_s[i].append(st_s[i])
    y_prompt = rms_norm(y_p, norm_final)
    y_sample = rms_norm(y_s, norm_final)
    return (y_prompt, y_sample,
            jnp.stack(new_p[0]), jnp.stack(new_s[0]),
            jnp.stack(new_p[1]), jnp.stack(new_s[1]),
            jnp.stack(new_p[2]), jnp.stack(new_s[2]),
            jnp.stack(new_p[3]), jnp.stack(new_s[3]),
            jnp.stack(new_p[4]), jnp.stack(new_s[4]))
```

```python
from contextlib import ExitStack
import numpy as np
import concourse.bass as bass
import concourse.mybir as mybir
from concourse.bass_utils import run_bass_kernel_spmd

F32 = mybir.dt.float32
BF16 = mybir.dt.bfloat16
I32 = mybir.dt.int32
AF = mybir.ActivationFunctionType
ALU = mybir.AluOpType

NCORES = 8
D = 1024
SEQ = 8192
NSAMP = 16
TS = 16
NTA = SEQ + 128
NBLK_A = SEQ // 512
NTB = 2048
FF = 2816
NFC = 22
C0 = -0.6065306597126334


class Buf:
    __slots__ = ("t", "w", "r", "name", "track", "excl")

    def __init__(self, t, name="", track=True):
        self.excl = False
        self.t = t
        self.w = None
        self.r = []
        self.name = name
        self.track = track


class V:
    __slots__ = ("b", "ap")

    def __init__(self, b, ap):
        self.b = b
        self.ap = ap


def v(buf, *idx):
    if not idx:
        return V(buf, buf.t[:])
    if len(idx) == 1:
        return V(buf, buf.t[idx[0]])
    return V(buf, buf.t[idx])


class Sync:
    def __init__(self, nc, n_dma_sems=14, self_sync=True):
        self.nc = nc
        self.stack = ExitStack()
        self.self_sync = self_sync
        self.engs = {}
        for nm, e in (("pe", nc.tensor), ("act", nc.scalar), ("dve", nc.vector),
                      ("pool", nc.gpsimd), ("sp", nc.sync)):
            sem = self.stack.enter_context(nc.semaphore("s_" + nm))
            self.engs[nm] = dict(e=e, sem=sem, cnt=0, waited={}, name=nm)
        self.dma_sems = []
        for i in range(n_dma_sems):
            sem = self.stack.enter_context(nc.semaphore("s_dma%d" % i))
            self.dma_sems.append(dict(sem=sem, cnt=0, name="dma%d" % i))
        self.dma_rr = {"hw": 0, "sw": 0}
        self.sems = {k: e["sem"] for k, e in self.engs.items()}
        for d in self.dma_sems:
            self.sems[d["name"]] = d["sem"]
        self.n_ins = 0
        self.n_wait = 0

    def _wait(self, E, tok):
        key, val = tok
        if E["waited"].get(key, 0) >= val:
            return
        if key == E["name"] and (not self.self_sync or key == "pe"):
            return
        E["e"].wait_ge(self.sems[key], val)
        E["waited"][key] = val
        self.n_wait += 1

    def _deps(self, E, reads, writes):
        reads = [b for b in reads if b.track]
        writes = [b for b in writes if b.track]
        for b in reads:
            if b.w is not None:
                self._wait(E, b.w)
        for b in writes:
            if b.w is not None:
                self._wait(E, b.w)
            for tok in b.r:
                self._wait(E, tok)

    def _mark(self, tok, reads, writes):
        reads = [b for b in reads if b.track]
        writes = [b for b in writes if b.track]
        for b in reads:
            if len(b.r) > 24:
                best = {}
                for k, val in b.r:
                    if best.get(k, 0) < val:
                        best[k] = val
                b.r = list(best.items())
            b.r.append(tok)
        for b in writes:
            b.w = tok
            b.r = []

    def op(self, eng, fn, reads=(), writes=(), inc=True):
        E = self.engs[eng]
        if eng != "pe":
            ex = [b for b in reads if b.excl]
            if ex:
                writes = list(writes) + ex
        self._deps(E, reads, writes)
        ins = fn(E["e"])
        self.n_ins += 1
        if inc:
            E["cnt"] += 1
            ins.then_inc(E["sem"], 1)
            tok = (eng, E["cnt"])
        else:
            tok = (eng, E["cnt"] + 1)
        self._mark(tok, reads, writes)
        return ins

    def dma(self, eng, out_ap, in_ap, reads=(), writes=(), **kw):
        E = self.engs[eng]
        self._deps(E, reads, writes)
        kind = "sw" if eng == "pool" else "hw"
        pool_ = self.dma_sems[0:4] if kind == "sw" else self.dma_sems[4:]
        d = pool_[self.dma_rr[kind] % len(pool_)]
        self.dma_rr[kind] += 1
        if d["cnt"] > 0:
            self._wait(E, (d["name"], d["cnt"]))
        ins = E["e"].dma_start(out=out_ap, in_=in_ap, **kw)
        d["cnt"] += 16
        ins.then_inc(d["sem"], 16)
        tok = (d["name"], d["cnt"])
        self.n_ins += 1
        self._mark(tok, reads, writes)
        return ins

    def wait_all_dma(self, eng):
        E = self.engs[eng]
        for d in self.dma_sems:
            if d["cnt"] > 0:
                self._wait(E, (d["name"], d["cnt"]))

    def finish(self, eng="sp"):
        E = self.engs[eng]
        for d in self.dma_sems:
            if d["cnt"] > 0:
                self._wait(E, (d["name"], d["cnt"]))
        for k, e in self.engs.items():
            if k != eng and e["cnt"] > 0:
                self._wait(E, (k, e["cnt"]))


class K:
    def __init__(self, nc):
        self.nc = nc
        self.S = Sync(nc)
        self.st = self.S.stack
        self.psf_i = 0
        self.psb_i = 0

    def sb(self, name, shape, dtype=F32, stack=None):
        t = (stack or self.st).enter_context(self.nc.sbuf_tensor(name, list(shape), dtype))
        return Buf(t, name)

    def ps(self, name, shape, dtype=F32):
        t = self.st.enter_context(self.nc.psum_tensor(name, list(shape), dtype))
        b = Buf(t, name)
        b.excl = True
        return b

    @staticmethod
    def _sc(x):
        return x.ap if isinstance(x, V) else x

    @staticmethod
    def _rb(*xs):
        return [x.b for x in xs if isinstance(x, V)]

    def tt(self, eng, out, a, b, op):
        self.S.op(eng, lambda e: e.tensor_tensor(out=out.ap, in0=a.ap, in1=b.ap, op=op),
                  reads=self._rb(a, b), writes=[out.b])

    def ts(self, eng, out, a, s1, op0, s2=None, op1=None):
        kw = {}
        if op1 is not None:
            kw["op1"] = op1
        self.S.op(eng, lambda e: e.tensor_scalar(out=out.ap, in0=a.ap, scalar1=self._sc(s1),
                                                 scalar2=self._sc(s2), op0=op0, **kw),
                  reads=self._rb(a, s1, s2), writes=[out.b])

    def stt(self, eng, out, a, s, b, op0, op1):
        self.S.op(eng, lambda e: e.scalar_tensor_tensor(out=out.ap, in0=a.ap, scalar=self._sc(s),
                                                        in1=b.ap, op0=op0, op1=op1),
                  reads=self._rb(a, s, b), writes=[out.b])

    def act(self, out, a, func, bias=None, scale=None, accum=None):
        kw = {}
        if bias is not None:
            kw["bias"] = self._sc(bias)
        if scale is not None:
            kw["scale"] = self._sc(scale)
        if accum is not None:
            kw["accum_out"] = accum.ap
        w = [out.b] + ([accum.b] if accum is not None else [])
        self.S.op("act", lambda e: e.activation(out=out.ap, in_=a.ap, func=func, **kw),
                  reads=self._rb(a, bias, scale), writes=w)

    def rsqrt(self, out, a, scale, eps):
        npart = out.ap.shape[0]
        eb, ec = self.epsv[eps]
        self.act(out, a, AF.Sqrt, bias=V(eb, eb.t[0:npart, ec:ec + 1]), scale=scale)
        self.S.op("dve", lambda e: e.reciprocal(out=out.ap, in_=out.ap), reads=[out.b], writes=[out.b])

    def cp(self, eng, out, a):
        if eng == "act":
            self.S.op("act", lambda e: e.copy(out=out.ap, in_=a.ap), reads=[a.b], writes=[out.b])
        else:
            self.S.op(eng, lambda e: e.tensor_copy(out=out.ap, in_=a.ap), reads=[a.b], writes=[out.b])

    def memset(self, eng, out, val):
        self.S.op(eng, lambda e: e.memset(out.ap, val), writes=[out.b])

    def asel(self, out, a, pattern, cmp, base, cm, fill=0.0):
        self.S.op("pool", lambda e: e.affine_select(out=out.ap, in_=a.ap, pattern=pattern, compare_op=cmp,
                                                    fill=fill, base=base, channel_multiplier=cm),
                  reads=[a.b], writes=[out.b])

    def mm(self, out, lhsT, rhs, start=True, stop=True, inc=None):
        if inc is None:
            inc = stop
        self.S.op("pe", lambda e: e.matmul(out.ap, lhsT.ap, rhs.ap, start=start, stop=stop),
                  reads=[lhsT.b, rhs.b], writes=[out.b], inc=inc)

    def tr(self, out, a, ident, inc=True):
        self.S.op("pe", lambda e: e.transpose(out.ap, a.ap, ident.ap), reads=[a.b, ident.b],
                  writes=[out.b], inc=inc)

    def scan(self, out, ones, a):
        self.S.op("dve", lambda e: e.tensor_tensor_scan(out=out.ap, data0=ones.ap, data1=a.ap, initial=0.0,
                                                       op0=ALU.mult, op1=ALU.add),
                  reads=[ones.b, a.b], writes=[out.b])

    def dma(self, eng, out, a, **kw):
        self.S.dma(eng, out.ap, a.ap, reads=[a.b], writes=[out.b], **kw)

    def psf(self):
        b = self.PSF[self.psf_i % len(self.PSF)]
        self.psf_i += 1
        return b

    def psb(self):
        b = self.PSB[self.psb_i % len(self.PSB)]
        self.psb_i += 1
        return b


def build():
    nc = bass.Bass("TRN2", target_bir_lowering=False)
    k = K(nc)
    S = k.S

    def din(name, shape, dt=F32):
        return Buf(nc.dram_tensor(name, list(shape), dt, kind="ExternalInput"), name, track=False)

    def dout(name, shape, dt=F32):
        return Buf(nc.dram_tensor(name, list(shape), dt, kind="ExternalOutput"), name, track=False)

    xA = din("xA", [NTA, D])
    xB = din("xB", [2 + NTB + 32, D])
    win = din("win", [D, 1186])
    ppar = din("ppar", [128, 40])
    wa2 = din("wa2", [128, 128])
    g2a = din("g2a", [128, 128])
    g2b = din("g2b", [32, 128])
    nrm = din("nrm", [3, D])
    wout = din("wout", [D, D])
    wup = din("wup", [D, 2 * FF])
    wdn = din("wdn", [FF, D])
    sgc = din("sgc", [128, 3, 8, 3])
    ssh = din("ssh", [128, 6, 8])
    sgs = din("sgs", [128, 8, 128])
    srs = din("srs", [128, 8, 64])
    sfc = din("sfc", [128, NFC, 2, 2])
    fcw = din("fcw", [128, NFC, 3])
    offs = din("offs", [1, 4], I32)
    flag = din("flag", [128, 1])

    yB = dout("yB", [NTB + 32, D])
    og_p = dout("og_p", [9, 128])
    og_s = dout("og_s", [72, 128])
    osh_p = dout("osh_p", [6, 128])
    osh_s = dout("osh_s", [48, 128])
    ogs_p = dout("ogs_p", [1, 128, 128])
    ogs_s = dout("ogs_s", [8, 128, 128])
    ors_p = dout("ors_p", [1, 2, 64, 64])
    ors_s = dout("ors_s", [8, 2, 64, 64])
    ofc_p = dout("ofc_p", [2, FF])
    ofc_s = dout("ofc_s", [4, FF])

    import os
    KDBG = os.environ.get("KDBG", "")
    if KDBG:
        dbg = dout("dbg", [5, 256, 2048], BF16)
    agin = Buf(nc.dram_tensor("agin", [5, 256, 2048], BF16), "agin", track=False)
    agout = Buf(nc.dram_tensor("agout", [5, 1024, 2048], BF16), "agout", track=False)
    cc_sem = k.st.enter_context(nc.semaphore("cc"))
    cc_n = [0]

    def exchange(piece):
        S.wait_all_dma("pool")
        nc.gpsimd.collective_compute("AllGather", ALU.bypass, replica_groups=[[0, 1, 2, 3], [4, 5, 6, 7]],
                                     ins=[agin.t.ap()[piece].opt()], outs=[agout.t.ap()[piece].opt()]).then_inc(cc_sem, 1)
        cc_n[0] += 1

    k.PSF = [k.ps("psf%d" % i, [128, 512], F32) for i in range(6)]
    k.PSB = [k.ps("psb%d" % i, [128, 1024], BF16) for i in range(2)]

    ones = k.sb("ones", [128, 512])
    onesb = k.sb("onesb", [128, 128], BF16)
    oblk = k.sb("oblk", [128, 128], BF16)
    ident = k.sb("ident", [128, 128])
    identb = k.sb("identb", [128, 128], BF16)
    tmpc = k.sb("tmpc", [128, 128])
    k.memset("pool", v(ones), 1.0)
    k.asel(v(ident), v(ones, (slice(None), slice(0, 128))), [[-1, 128]], ALU.is_equal, 0, 1)
    k.cp("pool", v(identb), v(ident))
    k.cp("pool", v(onesb), v(ones, (slice(None), slice(0, 128))))
    o128 = v(ones, (slice(None), slice(0, 128)))
    o3 = V(ones, ones.t[:, 0:128].rearrange("p (h x) -> p h x", x=64))
    t3 = V(tmpc, tmpc.t[:, :].rearrange("p (h x) -> p h x", x=64))
    k.asel(t3, o3, [[-64, 2], [0, 64]], ALU.is_ge, 0, 1)
    k.asel(t3, t3, [[64, 2], [0, 64]], ALU.is_ge, 63, -1)
    k.cp("pool", v(oblk), v(tmpc))

    MK = []
    bd = k.sb("bd16", [128, 128])
    b3o = V(ones, ones.t[:, 0:128].rearrange("p (q x) -> p q x", x=16))
    b3 = V(bd, bd.t[:, :].rearrange("p (q x) -> p q x", x=16))
    k.asel(b3, b3o, [[-16, 8], [0, 16]], ALU.is_ge, 0, 1)
    k.asel(b3, b3, [[16, 8], [0, 16]], ALU.is_ge, 15, -1)
    for mi in range(2):
        m = {}
        for nm, pat, cmp, cm in (("Ls", [[-1, 128]], ALU.is_gt, 1), ("Us", [[1, 128]], ALU.is_gt, -1),
                                 ("Ui", [[1, 128]], ALU.is_ge, -1)):
            t = k.sb("m%s%d" % (nm, mi), [128, 128])
            k.asel(v(t), o128, pat, cmp, 0, cm)
            if mi == 1:
                k.tt("pool", v(t), v(t), v(bd), ALU.mult)
            m[nm] = t
        for nm in ("Ls", "Us"):
            t = k.sb("mn%s%d" % (nm, mi), [128, 128])
            k.ts("pool", v(t), v(m[nm]), -1.0, ALU.mult)
            m["n" + nm] = t
        MK.append(m)
    colm = k.sb("colm", [128, 8, 128], BF16)
    on4 = V(ones, ones.t[:, 0:128].rearrange("p (r x) -> p r x", x=16))
    for q in range(8):
        k.asel(V(colm, colm.t[:, q, :].rearrange("p (r x) -> p r x", x=16)), on4, [[1, 8], [0, 16]],
               ALU.is_equal, -q, 0)
    rowm = k.sb("rowm", [128, 8])
    k.asel(v(rowm), v(ones, (slice(None), slice(0, 8))), [[-16, 8]], ALU.is_ge, 0, 1)
    k.asel(v(rowm), v(rowm), [[16, 8]], ALU.is_ge, 15, -1)

    epst = k.sb("epst", [128, 2])
    k.memset("pool", v(epst, (slice(None), slice(0, 1))), 1e-6)
    k.memset("pool", v(epst, (slice(None), slice(1, 2))), 64e-5)
    k.epsv = {1e-6: (epst, 0), 64e-5: (epst, 1)}
    pp = k.sb("pp", [128, 40])
    k.dma("sp", v(pp), v(ppar))
    PC = lambda i: v(pp, (slice(None), slice(i, i + 1)))
    der = k.sb("der", [128, 4])
    DC = lambda i: v(der, (slice(None), slice(i, i + 1)))
    k.act(DC(0), PC(12), AF.Exp)
    k.ts("dve", DC(0), DC(0), -1.0, ALU.mult)
    k.ts("dve", DC(1), PC(24), -1.0, ALU.mult, 1.0, ALU.add)
    flg = k.sb("flg", [128, 1])
    k.dma("sp", v(flg), v(flag))
    nbc = k.sb("nbc", [128, 3, D])
    for i in range(3):
        k.S.dma("sp", nbc.t[:, i, :], nrm.t[i:i + 1, :].partition_broadcast(128), reads=[nrm], writes=[nbc])

    import os
    KSTOP = os.environ.get("KSTOP", "")
    if KSTOP == "C0":
        S.finish("sp"); k.st.close(); return nc
    stA = ExitStack()
    A = lambda name, shape, dt=F32: k.sb(name, shape, dt, stack=stA)
    winb = A("winb", [128, 8, 1280], BF16)
    k.memset("pool", v(winb, (slice(None), slice(None), slice(1152, 1280))), 0.0)
    wv = win.t[:, :].rearrange("(kc p) c -> p kc c", p=128)
    for kc in range(8):
        S.dma("pool", winb.t[:, kc, 0:1185], wv[:, kc, 0:1185], reads=[win], writes=[winb])
    with nc.allow_non_contiguous_dma(reason="single weight column"):
        S.dma("pool", winb.t[:, :, 1216:1217], wv[:, :, 1185:1186], reads=[win], writes=[winb])
    wa2b = A("wa2b", [128, 128], BF16)
    g2ab = A("g2ab", [128, 128], BF16)
    g2bb = A("g2bb", [32, 128], BF16)
    k.dma("pool", v(wa2b), v(wa2))
    k.dma("pool", v(g2ab), v(g2a))
    k.dma("pool", v(g2bb), v(g2b))

    if KSTOP == "C1":
        S.finish("sp"); stA.close(); k.st.close(); return nc
    xt = [A("xt%d" % i, [128, D]) for i in range(2)]
    hb = [A("hb%d" % i, [128, D], BF16) for i in range(2)]
    ss = A("ss", [128, 2])
    hT = A("hT", [128, 8, 512], BF16)
    PB = {}
    for nm in ("q", "k", "v", "r", "kr", "vr", "wa", "g0", "g1"):
        t_ = A("P%s" % nm, [128, 3 + 512])
        k.memset("pool", v(t_), 0.0)
        PB[nm] = [t_, t_]
    PZ = A("PZ", [128, 512])
    W = {}
    for nm in ("cq", "ck", "cv", "t0", "t1", "t2", "t3", "betab", "lg", "gb", "eg", "bg", "kn", "qn",
               "sr", "skr", "svr", "gg", "krp", "Ep", "Ee", "Em", "Eh", "yg", "yr", "sz", "yc"):
        W[nm] = A("W" + nm, [128, 512])
    for a_, b_ in (("swa", "cq"), ("sg0", "ck"), ("sg1", "cv"), ("sig", "lg"), ("av", "bg"), ("kk", "eg"),
                   ("bv", "betab"), ("cs", "kn"), ("cse", "qn")):
        W[a_] = W[b_]
    WB = {}
    for nm in ("sqb", "knb", "kbb", "qdT", "kbgT", "vbT", "kdT", "T7", "sg0b", "sg1b", "atok", "vrb", "Bh", "Kh",
               "rk2", "ob", "qnb"):
        WB[nm] = A("B" + nm, [128, 512], BF16)
    BKt = A("BKt", [128, 4, 2, 128], BF16)
    ARt = A("ARt", [128, 4, 2, 128], BF16)
    gcol = A("gcol", [128, 4])
    glast = A("glast", [128, 8])
    gend = A("gend", [128, 8])

    CH = []
    for u in range(3):
        d = {}
        for nm in ("Nb", "NTb", "XTb", "Pb", "PTb", "Pb2", "PTb2", "Aak", "ArbT", "ArkT", "M2T"):
            d[nm] = A("c%s%d" % (nm, u), [128, 128], BF16)
        for nm in ("XT", "e1a", "e1b", "Dm", "DTm", "DTi"):
            d[nm] = A("c%s%d" % (nm, u), [128, 128])
        CH.append(d)
    kbg_t = A("kbg_t", [128, 128], BF16)
    Vb_t = A("Vb_t", [128, 128], BF16)
    kd_t = A("kd_t", [128, 128], BF16)
    AxTg = A("AxTg", [128, 128], BF16)
    vnew = A("vnew", [128, 128], BF16)
    padz = {}
    for nm in ("atokp", "Vp", "Bhp", "Khp", "Up"):
        t = A("pd" + nm, [128, 384], BF16)
        k.memset("pool", v(t), 0.0)
        padz[nm] = t
    PADV = lambda t, h: v(t, (slice(None), slice(h * 128, h * 128 + 128)))
    DIAG = lambda t: V(t, t.t[:, 0:384].rearrange("p (h x) -> p h x", x=192)[:, :, 0:64])
    AxTr = A("AxTr", [128, 128], BF16)

    def dcopy(eng, t, src, c0):
        for h in range(2):
            k.cp(eng, v(t, (slice(None), slice(h * 192, h * 192 + 64))),
                 v(src, (slice(None), slice(c0 + h * 64, c0 + h * 64 + 64))))

    Sg32 = A("Sg32", [128, 128]); Sgb = A("Sgb", [128, 128], BF16)
    Hr32 = A("Hr32", [128, 128]); Hrb = A("Hrb", [128, 128], BF16)
    k.memset("pool", v(Sg32), 0.0); k.memset("pool", v(Sgb), 0.0)
    k.memset("pool", v(Hr32), 0.0); k.memset("pool", v(Hrb), 0.0)
    S0g32 = A("S0g32", [128, 8, 128]); S0gb = A("S0gb", [128, 8, 128], BF16)
    H0r32 = A("H0r32", [128, 8, 128]); H0rb = A("H0rb", [128, 8, 128], BF16)
    mskA = A("mskA", [128, 8, 128], BF16)
    mskQ = A("mskQ", [128, 8, 128], BF16)
    mskK = A("mskK", [128, 2, 256], BF16)
    mskK2 = A("mskK2", [128, 2, 256], BF16)
    Sfin = A("Sfin", [128, 2, 128])
    COs = {8: A("CO8", [128, 3, 8, 3]), 1: A("CO1", [128, 3, 1, 3])}
    CO2s = {8: A("CO28", [128, 6, 8]), 1: A("CO21", [128, 6, 1])}
    COt = A("COt", [128, 128])
    HTt = A("HTt", [128, 128])

    def in_proj_block(tok0, NT, par, ncs, tcs, halo_src):
        ntile = NT // 128
        for i in range(ntile):
            xti = xt[i % 2]; hbi = hb[i % 2]
            k.dma("sp" if i % 2 == 0 else "act", v(xti), v(xA, (slice(tok0 + i * 128, tok0 + (i + 1) * 128), slice(None))))
            sc = v(ss, (slice(None), slice(i % 2, i % 2 + 1)))
            k.act(v(hbi), v(xti), AF.Square, scale=1.0 / 32.0, accum=sc)
            k.rsqrt(sc, sc, 1.0, 1e-6)
            k.stt("dve", v(hbi), v(xti), sc, v(nbc, (slice(None), 0, slice(None))), ALU.mult, ALU.mult)
            pb = k.psb()
            for kc in range(8):
                k.tr(v(pb, (slice(None), slice(kc * 128, (kc + 1) * 128))),
                     v(hbi, (slice(None), slice(kc * 128, (kc + 1) * 128))), v(identb), inc=(kc == 7))
            k.cp("act", v(hT, (slice(None), slice(None), slice(i * 128, (i + 1) * 128))),
                 V(pb, pb.t[:, :].rearrange("p (kc t) -> p kc t", t=128)))
        blocks = [("q", 0, 128, 3), ("k", 128, 128, 3), ("v", 256, 128, 3), ("z", 384, 128, 0),
                  ("r", 512, 128, 1), ("kr", 640, 128, 1), ("vr", 768, 128, 1), ("wa", 896, 128, 1),
                  ("g0", 1024, 128, 1), ("g1", 1152, 65, 1)]
        for nm, c0, m, halo in blocks:
            pf = k.psf()
            for kc in range(8):
                k.mm(v(pf, (slice(0, m), slice(0, NT))), v(winb, (slice(None), kc, slice(c0, c0 + m))),
                     v(hT, (slice(None), kc, slice(0, NT))), start=(kc == 0), stop=(kc == 7))
            if nm == "z":
                k.cp("act", v(PZ, (slice(None), slice(0, NT))), v(pf, (slice(None), slice(0, NT))))
                continue
            P = PB[nm][par]
            H = 3
            dst = V(P, P.t[0:m, 0:ncs * (H + tcs)].rearrange("p (s t) -> p s t", t=H + tcs)[:, :, H:H + tcs])
            src = V(pf, pf.t[0:m, 0:NT].rearrange("p (s t) -> p s t", t=tcs))
            k.cp("act", dst, src)

    def pview(P, ncs, tcs, lo, hi, m=128):
        return V(P, P.t[0:m, 0:ncs * (3 + tcs)].rearrange("p (s t) -> p s t", t=3 + tcs)[:, :, 3 + lo:3 + hi])

    def w3(t, NT, tcs, m=128):
        return V(t, t.t[0:m, 0:NT].rearrange("p (s t) -> p s t", t=tcs))

    def w2(t, NT, m=128, lo=0):
        return v(t, (slice(lo, lo + m) if lo else slice(0, m), slice(0, NT)))

    def doubling(u, nlev, Nb, NTb):
        d = CH[u]
        k.tt("dve", v(d["XT"]), v(ident), v(NTb), ALU.add)
        k.cp("act", v(d["XTb"]), v(d["XT"]))
        P, PT = Nb, NTb
        for m in range(1, nlev):
            last = (m == nlev - 1)
            Pn = d["Pb"] if m % 2 else d["Pb2"]
            PTn = d["PTb"] if m % 2 else d["PTb2"]
            pf = k.psf()
            k.mm(v(pf, (slice(None), slice(0, 128))), v(PT), v(P))
            if not last:
                k.mm(v(pf, (slice(None), slice(128, 256))), v(P), v(PT))
            k.cp("act", v(Pn), v(pf, (slice(None), slice(0, 128))))
            if not last:
                k.cp("dve", v(PTn), v(pf, (slice(None), slice(128, 256))))
            pf2 = k.psf()
            k.mm(v(pf2, (slice(None), slice(0, 128))), v(Pn), v(d["XTb"]))
            k.tt("dve", v(d["XT"]), v(d["XT"]), v(pf2, (slice(None), slice(0, 128))), ALU.add)
            k.cp("act", v(d["XTb"]), v(d["XT"]))
            P, PT = Pn, PTn

    def mixer_block(NT, par, ncs, tcs, nsg, seg, mi, nseq, tokcol0, last_out):
        nch = NT // 128
        nlev = 7 if seg == 128 else 4
        nlev = int(os.environ.get("KNLEV", nlev))
        m = MK[mi]
        f = lambda t, lo=0, hi=None: v(t, (slice(None), slice(lo, NT if hi is None else hi)))
        for si, (nm, cn) in enumerate((("q", "cq"), ("k", "ck"), ("v", "cv"))):
            P = PB[nm][par]
            o3_ = w3(W[cn], NT, tcs)
            eng = "dve"
            k.ts(eng, o3_, pview(P, ncs, tcs, -3, tcs - 3), PC(si * 4 + 0), ALU.mult)
            for tap in (1, 2, 3):
                k.stt(eng, o3_, pview(P, ncs, tcs, tap - 3, tcs + tap - 3), PC(si * 4 + tap), o3_,
                      ALU.mult, ALU.add)
            k.act(f(W[cn]), f(W[cn]), AF.Silu)
        g1 = PB["g1"][par]
        for ri, prt in enumerate((32, 64)):
            pf = k.psf()
            rsrc = V(g1, g1.t[prt:prt + 1, 0:ncs * (3 + tcs)].rearrange("p (s t) -> p s t", t=3 + tcs)[:, :, 3:3 + tcs])
            k.mm(V(pf, pf.t[:, 0:NT].rearrange("p (s t) -> p s t", t=tcs)),
                 v(ones, (slice(prt, prt + 1), slice(0, 128))), rsrc)
            if ri == 0:
                k.act(f(W["betab"]), f(pf), AF.Sigmoid)
            else:
                k.act(f(W["t0"]), f(pf), AF.Exp, bias=PC(13))
                k.act(f(W["t0"]), f(W["t0"]), AF.Ln, bias=1.0)
                k.ts("dve", f(W["lg"]), f(W["t0"]), DC(0), ALU.mult)
        for sgi in range(nsg):
            k.scan(f(W["gb"], sgi * seg, (sgi + 1) * seg), f(ones, 0, seg), f(W["lg"], sgi * seg, (sgi + 1) * seg))
        k.act(f(W["eg"]), f(W["gb"]), AF.Exp)
        k.tt("pool", f(W["bg"]), f(W["betab"]), f(W["eg"]), ALU.mult)
        for nm, cn, on in (("q", "cq", "qn"), ("k", "ck", "kn")):
            k.act(f(WB["sqb"]), f(W[cn]), AF.Square)
            pf = k.psf()
            k.mm(f(pf), v(onesb), f(WB["sqb"]))
            k.rsqrt(f(W["t1"]), f(pf), 1.0, 1e-6)
            if nm == "q":
                k.stt("dve", f(W[on]), f(W[cn]), 128.0 ** -0.5, f(W["t1"]), ALU.mult, ALU.mult)
            else:
                k.tt("dve", f(W[on]), f(W[cn]), f(W["t1"]), ALU.mult)
        k.cp("pool", f(WB["knb"]), f(W["kn"]))
        k.tt("pool", f(WB["kbb"]), f(W["kn"]), f(W["betab"]), ALU.mult)
        k.tt("dve", f(WB["qdT"]), f(W["qn"]), f(W["eg"]), ALU.mult)
        k.cp("pool", f(WB["qnb"]), f(W["qn"]))
        k.tt("pool", f(WB["kbgT"]), f(W["kn"]), f(W["bg"]), ALU.mult)
        k.tt("dve", f(WB["vbT"]), f(W["cv"]), f(W["betab"]), ALU.mult)
        gb3 = w3(W["gb"], NT, seg)
        gendv = V(W["gb"], W["gb"].t[:, 0:NT].rearrange("p (s t) -> p s t", t=seg)[:, :, seg - 1:seg])
        k.tt("dve", w3(W["t2"], NT, seg), V(W["gb"], gendv.ap.to_broadcast([128, nsg, seg])), gb3, ALU.subtract)
        k.act(f(W["t2"]), f(W["t2"]), AF.Exp)
        k.tt("pool", f(WB["kdT"]), f(W["kn"]), f(W["t2"]), ALU.mult)
        k.cp("dve", v(glast, (slice(None), slice(0, nsg))),
             V(W["eg"], W["eg"].t[:, 0:NT].rearrange("p (s t) -> p s t", t=seg)[:, :, seg - 1]))
        k.act(f(W["sz"]), f(PZ), AF.Silu)

        if KSTOP == "M1":
            return
        for nm, on, mrows, mc in (("r", "sr", 128, 15), ("kr", "skr", 128, 16), ("vr", "svr", 128, 17),
                                  ("wa", "swa", 128, 18), ("g0", "sg0", 128, 19), ("g1", "sg1", 32, 20)):
            P = PB[nm][par]
            o3_ = w3(W[on], NT, tcs, mrows)
            cur = pview(P, ncs, tcs, 0, tcs, mrows)
            prv = pview(P, ncs, tcs, -1, tcs - 1, mrows)
            k.tt("pool", o3_, prv, cur, ALU.subtract)
            k.stt("dve", o3_, o3_, v(pp, (slice(0, mrows), slice(mc, mc + 1))), cur, ALU.mult, ALU.add)
        k.act(v(WB["T7"], (slice(0, 64), slice(0, NT))), v(W["swa"], (slice(0, 64), slice(0, NT))), AF.Tanh)
        k.cp("act", v(WB["T7"], (slice(64, 128), slice(0, NT))), v(W["swa"], (slice(64, 128), slice(0, NT))))
        k.act(f(WB["sg0b"]), f(W["sg0"]), AF.Sigmoid)
        k.act(v(WB["sg1b"], (slice(0, 32), slice(0, NT))), v(W["sg1"], (slice(0, 32), slice(0, NT))), AF.Sigmoid)
        pf = k.psf()
        k.mm(f(pf), v(wa2b, (slice(0, 64), slice(None))), v(WB["T7"], (slice(0, 64), slice(0, NT))))
        k.act(f(W["sig"]), f(pf), AF.Sigmoid, bias=PC(21))
        pf = k.psf()
        k.mm(f(pf), v(wa2b, (slice(64, 128), slice(None))), v(WB["T7"], (slice(64, 128), slice(0, NT))))
        k.act(f(W["av"]), f(pf), AF.Sigmoid, bias=PC(22))
        pf = k.psf()
        k.mm(f(pf), v(g2ab), f(WB["sg0b"]), start=True, stop=False)
        k.mm(f(pf), v(g2bb), v(WB["sg1b"], (slice(0, 32), slice(0, NT))), start=False, stop=True)
        k.cp("act", f(W["gg"]), f(pf))
        k.ts("dve", f(W["kk"]), f(W["skr"]), PC(23), ALU.mult)
        k.act(f(WB["sqb"]), f(W["kk"]), AF.Square)
        pf = k.psf()
        k.mm(f(pf), v(oblk), f(WB["sqb"]))
        k.rsqrt(f(W["t1"]), f(pf), 1.0, 1e-6)
        k.tt("dve", f(W["kk"]), f(W["kk"]), f(W["t1"]), ALU.mult)
        k.ts("dve", f(W["t3"]), f(W["av"]), PC(24), ALU.mult, DC(1), ALU.add)
        k.tt("dve", f(W["krp"]), f(W["skr"]), f(W["t3"]), ALU.mult)
        k.tt("pool", f(W["bv"]), f(W["kk"]), f(W["av"]), ALU.mult)
        for sgi in range(nsg):
            k.scan(f(W["cs"], sgi * seg, (sgi + 1) * seg), f(ones, 0, seg), f(W["sig"], sgi * seg, (sgi + 1) * seg))
        k.tt("pool", f(W["cse"]), f(W["cs"]), f(W["sig"]), ALU.subtract)
        k.act(f(W["Ep"]), f(W["cs"]), AF.Exp, scale=C0)
        k.act(f(W["Ee"]), f(W["cse"]), AF.Exp, scale=C0)
        k.act(f(W["Em"]), f(W["cs"]), AF.Exp, scale=-C0)
        cendv = V(W["cs"], W["cs"].t[:, 0:NT].rearrange("p (s t) -> p s t", t=seg)[:, :, seg - 1:seg])
        k.tt("dve", w3(W["Eh"], NT, seg), V(W["cs"], cendv.ap.to_broadcast([128, nsg, seg])), w3(W["cs"], NT, seg),
             ALU.subtract)
        k.act(f(W["Eh"]), f(W["Eh"]), AF.Exp, scale=C0)
        k.cp("dve", v(gend, (slice(None), slice(0, nsg))),
             V(W["Ep"], W["Ep"].t[:, 0:NT].rearrange("p (s t) -> p s t", t=seg)[:, :, seg - 1]))
        BK4 = lambda j: V(BKt, BKt.t[:, 0:nch, j, :])
        AR4 = lambda j: V(ARt, ARt.t[:, 0:nch, j, :])
        c4 = lambda t: V(t, t.t[:, 0:NT].rearrange("p (c t) -> p c t", t=128))
        k.tt("dve", BK4(0), c4(W["bv"]), c4(W["Em"]), ALU.mult)
        k.tt("pool", BK4(1), c4(W["krp"]), c4(W["Em"]), ALU.mult)
        k.stt("dve", AR4(0), c4(W["kk"]), -1.0, c4(W["Ee"]), ALU.mult, ALU.mult)
        k.tt("pool", AR4(1), c4(W["sr"]), c4(W["Ep"]), ALU.mult)
        k.stt("dve", f(WB["atok"]), f(W["kk"]), -1.0, f(W["Ee"]), ALU.mult, ALU.mult)
        k.cp("pool", f(WB["vrb"]), f(W["svr"]))
        k.tt("pool", f(WB["Bh"]), f(W["bv"]), f(W["Eh"]), ALU.mult)
        k.tt("dve", f(WB["Kh"]), f(W["krp"]), f(W["Eh"]), ALU.mult)

        if KSTOP in ("M2", "XM2"):
            return
        for c in range(nch):
            cs_ = slice(c * 128, (c + 1) * 128)
            fc = lambda t: v(t, (slice(None), cs_))
            d = CH[0]
            pf = k.psf()
            k.mm(v(pf, (slice(None), slice(0, 1))), v(W["gb"], (slice(0, 1), cs_)), v(ones, (slice(0, 1), slice(0, 1))))
            k.cp("act", v(gcol, (slice(None), slice(0, 1))), v(pf, (slice(None), slice(0, 1))))
            gbc = v(gcol, (slice(None), slice(0, 1)))
            k.ts("dve", v(d["e1a"]), fc(W["gb"]), gbc, ALU.subtract, 0.0, ALU.min)
            k.ts("dve", v(d["e1b"]), fc(W["gb"]), gbc, ALU.subtract, 0.0, ALU.max)
            k.act(v(d["e1a"]), v(d["e1a"]), AF.Exp)
            k.act(v(d["e1b"]), v(d["e1b"]), AF.Exp, scale=-1.0)
            k.tt("pool", v(d["Dm"]), v(d["e1b"]), v(m["nLs"]), ALU.mult)
            k.tt("pool", v(d["DTm"]), v(d["e1a"]), v(m["nUs"]), ALU.mult)
            k.tt("pool", v(d["DTi"]), v(d["e1a"]), v(m["Ui"]), ALU.mult)
            pf = k.psf()
            k.mm(v(pf, (slice(None), slice(0, 128))), fc(WB["kbb"]), fc(WB["knb"]))
            k.mm(v(pf, (slice(None), slice(128, 256))), fc(WB["knb"]), fc(WB["kbb"]))
            k.tt("dve", v(d["Nb"]), v(pf, (slice(None), slice(0, 128))), v(d["Dm"]), ALU.mult)
            k.tt("dve", v(d["NTb"]), v(pf, (slice(None), slice(128, 256))), v(d["DTm"]), ALU.mult)
            pf3 = k.psf()
            k.mm(v(pf3, (slice(None), slice(0, 128))), fc(WB["knb"]), fc(WB["qnb"]))
            k.tt("dve", v(d["ArbT"]), v(pf3, (slice(None), slice(0, 128))), v(d["DTi"]), ALU.mult)
            doubling(0, nlev, d["Nb"], d["NTb"])
            pb = k.psb()
            k.tr(v(pb, (slice(None), slice(0, 128))), fc(WB["kbgT"]), v(identb), inc=False)
            k.tr(v(pb, (slice(None), slice(128, 256))), fc(WB["vbT"]), v(identb), inc=False)
            k.tr(v(pb, (slice(None), slice(256, 384))), fc(WB["kdT"]), v(identb))
            k.cp("act", v(kbg_t), v(pb, (slice(None), slice(0, 128))))
            k.cp("dve", v(Vb_t), v(pb, (slice(None), slice(128, 256))))
            k.cp("act", v(kd_t), v(pb, (slice(None), slice(256, 384))))
            pf = k.psf()
            k.mm(v(pf, (slice(None), slice(0, 128))), v(kbg_t), v(d["XTb"]))
            k.act(v(AxTg), v(pf, (slice(None), slice(0, 128))), AF.Identity, scale=-1.0)

            if KSTOP == "M3":
                return
            for h in range(2):
                d = CH[1 + h]
                hs = slice(64 * h, 64 * h + 64)
                at = V(ARt, ARt.t[hs, c, 0, :]); rt = V(ARt, ARt.t[hs, c, 1, :])
                bt = V(BKt, BKt.t[hs, c, 0, :]); kt = V(BKt, BKt.t[hs, c, 1, :])
                pf = k.psf()
                if KSTOP == "R0a":
                    return
                k.mm(v(pf, (slice(None), slice(0, 256))), at, V(BKt, BKt.t[hs, c, :, :].rearrange("p a b -> p (a b)")))
                if KSTOP == "R0":
                    return
                k.tt("dve", v(d["Nb"]), v(pf, (slice(None), slice(0, 128))), v(m["Ls"]), ALU.mult)
                k.tt("dve", v(d["Aak"]), v(pf, (slice(None), slice(128, 256))), v(m["Ls"]), ALU.mult)
                if KSTOP == "R1":
                    return
                pf = k.psf()
                k.mm(v(pf, (slice(None), slice(0, 256))), bt, V(ARt, ARt.t[hs, c, :, :].rearrange("p a b -> p (a b)")))
                k.mm(v(pf, (slice(None), slice(256, 384))), kt, rt)
                k.tt("dve", v(d["NTb"]), v(pf, (slice(None), slice(0, 128))), v(m["Us"]), ALU.mult)
                k.tt("dve", v(d["ArbT"]), v(pf, (slice(None), slice(128, 256))), v(m["Ui"]), ALU.mult)
                k.tt("dve", v(d["ArkT"]), v(pf, (slice(None), slice(256, 384))), v(m["Ui"]), ALU.mult)
                if KSTOP == "R2":
                    return
                doubling(1 + h, nlev, d["Nb"], d["NTb"])
                if KSTOP == "R3":
                    return
                pf = k.psf()
                k.mm(v(pf, (slice(None), slice(0, 128))), v(d["Aak"]), v(d["XTb"]))
                k.cp("act", v(d["M2T"]), v(pf, (slice(None), slice(0, 128))))
            if KSTOP == "R5":
                return
            pb = k.psb()
            for i_, nm in enumerate(("atok", "vrb", "Bh", "Kh")):
                k.tr(v(pb, (slice(None), slice(i_ * 128, (i_ + 1) * 128))), fc(WB[nm]), v(identb), inc=(i_ == 3))
            if KSTOP == "R6":
                return
            import os as _os
            _v = _os.environ.get("KV", "")
            for i_, (nm, eng) in enumerate((("atokp", "act"), ("Vp", "act"), ("Bhp", "act"), ("Khp", "act"))):
                if _v == "act":
                    eng = "act"
                if _v == "dve":
                    eng = "dve"
                if _v == "one" and i_ > 0:
                    continue
                dcopy(eng, padz[nm], pb, i_ * 128)
            if KSTOP == "R7":
                return
            pf = k.psf()
            for h in range(2):
                k.mm(v(pf, (slice(None), slice(0, 128))), PADV(padz["atokp"], h), v(CH[1 + h]["XTb"]),
                     start=(h == 0), stop=(h == 1))
            k.cp("act", v(AxTr), v(pf, (slice(None), slice(0, 128))))

            if KSTOP in ("M4", "XM4"):
                return
            if nseq == 1 and KSTOP != "XS2":
                d = CH[0]
                pf = k.psf()
                k.mm(v(pf, (slice(None), slice(0, 128))), v(AxTg), v(Sgb), start=True, stop=False)
                k.mm(v(pf, (slice(None), slice(0, 128))), v(d["XTb"]), v(Vb_t), start=False, stop=True)
                k.cp("act", v(vnew), v(pf, (slice(None), slice(0, 128))))
                pf2 = k.psf()
                k.mm(v(pf2, (slice(None), slice(0, 128))), v(Sgb), fc(WB["qdT"]), start=True, stop=False)
                k.mm(v(pf2, (slice(None), slice(0, 128))), v(vnew), v(d["ArbT"]), start=False, stop=True)
                pf3 = k.psf()
                k.mm(v(pf3, (slice(None), slice(0, 128))), v(kd_t), v(vnew))
                k.stt("dve", v(Sg32), v(Sg32), v(glast, (slice(None), slice(c, c + 1))),
                      v(pf3, (slice(None), slice(0, 128))), ALU.mult, ALU.add)
                k.cp("act", v(Sgb), v(Sg32))
                k.cp("act", fc(W["yg"]), v(pf2, (slice(None), slice(0, 128))))
            if nseq == 1 and KSTOP != "XS1":
                pf = k.psf()
                k.mm(v(pf, (slice(None), slice(0, 128))), v(AxTr), v(Hrb), start=True, stop=False)
                for h in range(2):
                    k.mm(v(pf, (slice(None), slice(0, 128))), v(CH[1 + h]["M2T"]), PADV(padz["Vp"], h),
                         start=False, stop=(h == 1))
                dcopy("act", padz["Up"], pf, 0)
                pf2 = k.psf()
                k.mm(v(pf2, (slice(None), slice(0, 128))), v(Hrb), V(ARt, ARt.t[:, c, 1, :]), start=True, stop=False)
                for h in range(2):
                    k.mm(v(pf2, (slice(None), slice(0, 128))), PADV(padz["Up"], h), v(CH[1 + h]["ArbT"]),
                         start=False, stop=False)
                    k.mm(v(pf2, (slice(None), slice(0, 128))), PADV(padz["Vp"], h), v(CH[1 + h]["ArkT"]),
                         start=False, stop=(h == 1))
                for h in range(2):
                    k.mm(v(pf2, (slice(None), slice(128, 256))), PADV(padz["Bhp"], h), PADV(padz["Up"], h),
                         start=(h == 0), stop=False)
                    k.mm(v(pf2, (slice(None), slice(128, 256))), PADV(padz["Khp"], h), PADV(padz["Vp"], h),
                         start=False, stop=(h == 1))
                k.stt("dve", v(Hr32), v(Hr32), v(gend, (slice(None), slice(c, c + 1))),
                      v(pf2, (slice(None), slice(128, 256))), ALU.mult, ALU.add)
                k.cp("act", v(Hrb), v(Hr32))
                k.cp("act", fc(W["yr"]), v(pf2, (slice(None), slice(0, 128))))
            if nseq != 1:
                d = CH[0]
                for q in range(8):
                    k.tt("pool", v(mskA, (slice(None), q, slice(None))), v(AxTg), v(colm, (slice(None), q, slice(None))), ALU.mult)
                    k.tt("dve", v(mskQ, (slice(None), q, slice(None))), fc(WB["qdT"]), v(colm, (slice(None), q, slice(None))), ALU.mult)
                pf = k.psf()
                for q in range(8):
                    k.mm(v(pf, (slice(None), slice(0, 128))), v(mskA, (slice(None), q, slice(None))),
                         v(S0gb, (slice(None), q, slice(None))), start=(q == 0), stop=False)
                k.mm(v(pf, (slice(None), slice(0, 128))), v(d["XTb"]), v(Vb_t), start=False, stop=True)
                k.cp("act", v(vnew), v(pf, (slice(None), slice(0, 128))))
                pf2 = k.psf()
                for q in range(8):
                    k.mm(v(pf2, (slice(None), slice(0, 128))), v(S0gb, (slice(None), q, slice(None))),
                         v(mskQ, (slice(None), q, slice(None))), start=(q == 0), stop=False)
                k.mm(v(pf2, (slice(None), slice(0, 128))), v(vnew), v(d["ArbT"]), start=False, stop=True)
                k.cp("act", fc(W["yg"]), v(pf2, (slice(None), slice(0, 128))))
                for q in range(8):
                    k.ts("pool", v(mskK, (slice(None), q % 2, slice(0, 128))), v(kd_t), v(rowm, (slice(None), slice(q, q + 1))), ALU.mult)
                    pf3 = k.psf()
                    k.mm(v(pf3, (slice(None), slice(0, 128))), v(mskK, (slice(None), q % 2, slice(0, 128))), v(vnew))
                    k.stt("dve", v(Sfin, (slice(None), q % 2, slice(None))), v(S0g32, (slice(None), q, slice(None))),
                          v(glast, (slice(None), slice(q, q + 1))), v(pf3, (slice(None), slice(0, 128))),
                          ALU.mult, ALU.add)
                    k.dma("sp", V(ogs_s, ogs_s.t[q, :, :]), v(Sfin, (slice(None), q % 2, slice(None))))
                for q in range(8):
                    k.tt("pool", v(mskA, (slice(None), q, slice(None))), v(AxTr), v(colm, (slice(None), q, slice(None))), ALU.mult)
                    k.tt("dve", v(mskQ, (slice(None), q, slice(None))), V(ARt, ARt.t[:, c, 1, :]),
                         v(colm, (slice(None), q, slice(None))), ALU.mult)
                pf = k.psf()
                for q in range(8):
                    k.mm(v(pf, (slice(None), slice(0, 128))), v(mskA, (slice(None), q, slice(None))),
                         v(H0rb, (slice(None), q, slice(None))), start=(q == 0), stop=False)
                for h in range(2):
                    k.mm(v(pf, (slice(None), slice(0, 128))), v(CH[1 + h]["M2T"]), PADV(padz["Vp"], h),
                         start=False, stop=(h == 1))
                dcopy("act", padz["Up"], pf, 0)
                pf2 = k.psf()
                for q in range(8):
                    k.mm(v(pf2, (slice(None), slice(0, 128))), v(H0rb, (slice(None), q, slice(None))),
                         v(mskQ, (slice(None), q, slice(None))), start=(q == 0), stop=False)
                for h in range(2):
                    k.mm(v(pf2, (slice(None), slice(0, 128))), PADV(padz["Up"], h), v(CH[1 + h]["ArbT"]),
                         start=False, stop=False)
                    k.mm(v(pf2, (slice(None), slice(0, 128))), PADV(padz["Vp"], h), v(CH[1 + h]["ArkT"]),
                         start=False, stop=(h == 1))
                k.cp("act", fc(W["yr"]), v(pf2, (slice(None), slice(0, 128))))
                for q in range(8):
                    k.ts("pool", v(mskK, (slice(None), q % 2, slice(None))), v(padz["Bhp"], (slice(None), slice(0, 256))),
                         v(rowm, (slice(None), slice(q, q + 1))), ALU.mult)
                    k.ts("dve", v(mskK2, (slice(None), q % 2, slice(None))), v(padz["Khp"], (slice(None), slice(0, 256))),
                         v(rowm, (slice(None), slice(q, q + 1))), ALU.mult)
                    pf3 = k.psf()
                    for h in range(2):
                        k.mm(v(pf3, (slice(None), slice(0, 128))), v(mskK, (slice(None), q % 2, slice(h * 128, h * 128 + 128))),
                             PADV(padz["Up"], h), start=(h == 0), stop=False)
                        k.mm(v(pf3, (slice(None), slice(0, 128))), v(mskK2, (slice(None), q % 2, slice(h * 128, h * 128 + 128))),
                             PADV(padz["Vp"], h), start=False, stop=(h == 1))
                    k.stt("dve", v(Sfin, (slice(None), q % 2, slice(None))), v(H0r32, (slice(None), q, slice(None))),
                          v(gend, (slice(None), slice(q, q + 1))), v(pf3, (slice(None), slice(0, 128))),
                          ALU.mult, ALU.add)
                    pf4 = k.psf()
                    k.tr(v(pf4, (slice(None), slice(0, 128))), v(Sfin, (slice(None), q % 2, slice(None))), v(ident))
                    k.cp("act", v(HTt), v(pf4, (slice(None), slice(0, 128))))
                    for h in range(2):
                        k.dma("sp", V(ors_s, ors_s.t[q, h, :, :]),
                              v(HTt, (slice(64 * h, 64 * h + 64), slice(64 * h, 64 * h + 64))))

        if KSTOP in ("M5", "XM5", "XS1", "XS2"):
            return
        k.act(f(WB["sqb"]), f(W["yg"]), AF.Square)
        pf = k.psf()
        k.mm(f(pf), v(onesb), f(WB["sqb"]))
        k.rsqrt(f(W["t1"]), f(pf), 1.0 / 128.0, 1e-6)
        k.tt("dve", f(W["t0"]), f(W["yg"]), f(W["t1"]), ALU.mult)
        k.stt("dve", f(WB["ob"]), f(W["t0"]), PC(14), f(W["sz"]), ALU.mult, ALU.mult)
        pc_, pcol_ = tokcol0 // 2048, tokcol0 % 2048
        k.dma("sp", V(agin, agin.t[pc_, 0:128, pcol_:pcol_ + NT]), f(WB["ob"]))
        k.cp("pool", f(WB["sqb"]), f(W["yr"]))
        pf = k.psf()
        k.mm(f(pf), v(oblk), f(WB["sqb"]))
        k.stt("dve", f(W["yc"]), f(pf), -1.0 / 64.0, f(W["yr"]), ALU.mult, ALU.add)
        k.act(f(WB["sqb"]), f(W["yc"]), AF.Square)
        pf = k.psf()
        k.mm(f(pf), v(oblk), f(WB["sqb"]))
        k.rsqrt(f(W["t1"]), f(pf), 1.0 / 64.0, 64e-5)
        k.tt("dve", f(W["yc"]), f(W["yc"]), f(W["t1"]), ALU.mult)
        k.ts("dve", f(W["yc"]), f(W["yc"]), PC(26), ALU.mult, PC(27), ALU.add)
        k.tt("pool", f(W["t0"]), f(W["sr"]), f(W["krp"]), ALU.mult)
        k.ts("pool", f(WB["rk2"]), f(W["t0"]), PC(25), ALU.mult)
        pf = k.psf()
        k.mm(f(pf), v(oblk), f(WB["rk2"]))
        k.tt("dve", f(W["t0"]), f(pf), f(W["svr"]), ALU.mult)
        k.tt("dve", f(W["t0"]), f(W["t0"]), f(W["yc"]), ALU.add)
        k.tt("dve", f(WB["kbb"]), f(W["t0"]), f(W["gg"]), ALU.mult)
        k.dma("sp", V(agin, agin.t[pc_, 128:256, pcol_:pcol_ + NT]), f(WB["kbb"]))

        if last_out is not None:
            og, osh = last_out
            CO = COs[ncs]; CO2 = CO2s[ncs]
            for si, nm in enumerate(("q", "k", "v")):
                k.cp("pool", v(CO, (slice(None), si, slice(0, ncs), slice(None))),
                     pview(PB[nm][par], ncs, tcs, tcs - 3, tcs))
            for bi, nm in enumerate(("r", "kr", "vr", "wa", "g0", "g1")):
                k.cp("pool", v(CO2, (slice(None), bi, slice(0, ncs))),
                     V(PB[nm][par], PB[nm][par].t[:, 0:ncs * (3 + tcs)].rearrange("p (s t) -> p s t", t=3 + tcs)[:, :, 3 + tcs - 1]))
            if KSTOP == "L1":
                return
            n1 = 9 * ncs
            pf = k.psf()
            k.mm(v(pf, (slice(0, n1), slice(0, 128))),
                 V(CO, CO.t[:, :, :, :].rearrange("p a s t -> p (a s t)")), v(ident))
            k.cp("act", v(COt, (slice(0, n1), slice(None))), v(pf, (slice(0, n1), slice(0, 128))))
            if KSTOP == "L2":
                return
            k.dma("sp", v(og), v(COt, (slice(0, n1), slice(None))))
            if KSTOP == "L3":
                return
            n2 = 6 * ncs
            pf = k.psf()
            k.mm(v(pf, (slice(0, n2), slice(0, 128))),
                 V(CO2, CO2.t[:, :, :].rearrange("p a s -> p (a s)")), v(ident))
            k.cp("act", v(COt, (slice(0, n2), slice(None))), v(pf, (slice(0, n2), slice(0, 128))))
            k.dma("sp", v(osh), v(COt, (slice(0, n2), slice(None))))

    import os
    KSTOP = os.environ.get("KSTOP", "")
    KNB = int(os.environ.get("KNB", NBLK_A))
    KSK = int(os.environ.get("KSKIP", -1))
    for blk in range(KNB):
        if blk == KSK:
            continue
        par = 0
        if blk > 0:
            for nm in PB:
                k.cp("pool", v(PB[nm][0], (slice(None), slice(0, 3))), v(PB[nm][0], (slice(None), slice(512, 515))))
        in_proj_block((blk * 512) if not (os.environ.get("KREP", "") and blk == 15) else 0, 512, par, 1, 512, None)
        if KSTOP == "IP":
            S.finish("sp"); stA.close(); k.st.close(); return nc
        mixer_block(512, par, 1, 512, 4, 128, 0, 1, blk * 512,
                    (og_p, osh_p) if (blk == NBLK_A - 1 and KSTOP != "NOLO") else None)
        if blk % 4 == 3 and not KSTOP:
            exchange(blk // 4)
        if KSTOP[:1] in ("M", "R") or (KSTOP[:1] in ("L", "N", "X") and blk == NBLK_A - 1):
            if KDBG:
                S.wait_all_dma("sp")
                k.dma("sp", v(dbg), v(agin))
            S.finish("sp"); stA.close(); k.st.close(); return nc
    k.dma("sp", V(ogs_p, ogs_p.t[0, :, :]), v(Sg32))
    pf = k.psf()
    k.tr(v(pf, (slice(None), slice(0, 128))), v(Hr32), v(ident))
    k.cp("act", v(HTt), v(pf, (slice(None), slice(0, 128))))
    for h in range(2):
        k.dma("sp", V(ors_p, ors_p.t[0, h, :, :]), v(HTt, (slice(64 * h, 64 * h + 64), slice(64 * h, 64 * h + 64))))

    if KSTOP == "A1":
        if KDBG:
            S.wait_all_dma("sp")
            k.dma("sp", v(dbg), v(agin))
        S.finish("sp"); stA.close(); k.st.close(); return nc
    par = 0
    for si, nm in enumerate(("q", "k", "v")):
        P = PB[nm][par]
        k.dma("sp", V(P, P.t[:, 0:8 * 19].rearrange("p (s t) -> p s t", t=19)[:, :, 0:3]), v(sgc, (slice(None), si, slice(None), slice(None))))
    for bi, nm in enumerate(("r", "kr", "vr", "wa", "g0", "g1")):
        P = PB[nm][par]
        k.dma("sp", V(P, P.t[:, 0:8 * 19].rearrange("p (s t) -> p s t", t=19)[:, :, 2]), v(ssh, (slice(None), bi, slice(None))),
              allow_slow_non_contiguous=True)
    k.dma("sp", v(S0g32), v(sgs))
    k.cp("act", v(S0gb), v(S0g32))
    k.memset("pool", v(H0r32), 0.0)
    for h in range(2):
        k.dma("sp", v(H0r32, (slice(64 * h, 64 * h + 64), slice(None), slice(64 * h, 64 * h + 64))),
              v(srs, (slice(64 * h, 64 * h + 64), slice(None), slice(None))))
    k.cp("act", v(H0rb), v(H0r32))
    in_proj_block(SEQ, 128, par, 8, 16, None)
    if KSTOP == "A2":
        S.finish("sp"); stA.close(); k.st.close(); return nc
    mixer_block(128, par, 8, 16, 8, 16, 1, 8, SEQ, (og_s, osh_s))
    if KSTOP == "A3":
        S.finish("sp"); stA.close(); k.st.close(); return nc

    exchange(4)
    for en in ("pool", "sp"):
        S.engs[en]["e"].wait_ge(cc_sem, cc_n[0])
    stA.close()
    if KSTOP == "AG":
        S.finish("sp"); k.st.close(); return nc

    stB = ExitStack()
    Bt = lambda name, shape, dt=F32: k.sb(name, shape, dt, stack=stB)
    woutb = Bt("woutb", [128, 8, D], BF16)
    wupb = Bt("wupb", [128, 8, 2 * FF], BF16)
    wdnb = Bt("wdnb", [128, NFC, D], BF16)
    wo_v = wout.t[:, :].rearrange("(kc p) c -> p kc c", p=128)
    wu_v = wup.t[:, :].rearrange("(kc p) c -> p kc c", p=128)
    wd_v = wdn.t[:, :].rearrange("(fc p) c -> p fc c", p=128)
    for kc in range(8):
        S.dma("pool", woutb.t[:, kc, :], wo_v[:, kc, :], reads=[wout], writes=[woutb])
    for kc in range(8):
        for hf in range(4):
            S.dma("pool", wupb.t[:, kc, hf * 1408:(hf + 1) * 1408], wu_v[:, kc, hf * 1408:(hf + 1) * 1408],
                  reads=[wup], writes=[wupb])
    for fc_ in range(NFC):
        S.dma("pool", wdnb.t[:, fc_, :], wd_v[:, fc_, :], reads=[wdn], writes=[wdnb])
    fcwt = Bt("fcwt", [128, NFC, 3])
    k.dma("sp", v(fcwt), v(fcw))
    SH = Bt("SH", [128, NFC, 2, 2])
    k.dma("sp", v(SH), v(sfc))
    GH = Bt("GH", [128, NFC, 2])
    FO = Bt("FO", [128, NFC, 4])
    FOt = Bt("FOt", [4, 2, 128])
    x1 = [Bt("x1_%d" % i, [128, D]) for i in range(2)]
    h2b = Bt("h2b", [128, D], BF16)
    ssB = Bt("ssB", [128, 2])
    h2T = [Bt("h2T%d" % i, [128, 8, 128], BF16) for i in range(2)]
    oTb = [Bt("oTb%d" % i, [128, 8, 256], BF16) for i in range(2)]
    Ab = Bt("Ab", [128, NFC, 128], BF16)
    Gt = [Bt("Gt%d" % i, [128, 2 * 18 + 100]) for i in range(3)]
    Gc = [Bt("Gc%d" % i, [128, 128]) for i in range(2)]

    regs = []
    for i in range(3):
        r = nc.sync.alloc_register("off%d" % i)
        nc.sync.reg_load(r, offs.t[0:1, i:i + 1])
        regs.append(r)
    toff = nc.sync.snap(regs[0], min_val=0, max_val=3)
    hoff = nc.sync.snap(regs[1], min_val=0, max_val=3)
    soff = nc.sync.snap(regs[2], min_val=0, max_val=96)

    def load_oT(oT, prow, c0, n, dyn_col=None):
        src = agout.t[bass.ds(prow, 1), :, :] if not isinstance(prow, int) else agout.t[prow:prow + 1, :, :]
        if dyn_col is None:
            src = src[:, :, c0:c0 + n]
        else:
            src = src[:, :, c0:2048][:, :, bass.ds(dyn_col, n)]
        S.dma("sp", oT.t[:, :, 0:n], src.rearrange("o (ch p) t -> p (o ch) t", p=128), reads=[agout], writes=[oT])

    def ffn_block(bi, xrow0, mt, ncs, tcs, oT, ocol, mode, yrow0):
        xb_ = x1[bi % 2]; hT2 = h2T[bi % 2]
        k.dma("sp", v(xb_, (slice(0, mt), slice(None))), v(xB, (slice(xrow0, xrow0 + mt), slice(None))))
        for hf in range(2):
            pf = k.psf()
            for ch in range(8):
                k.mm(v(pf, (slice(0, mt), slice(None))), v(oT, (slice(None), ch, slice(ocol, ocol + mt))),
                     v(woutb, (slice(None), ch, slice(hf * 512, (hf + 1) * 512))), start=(ch == 0), stop=(ch == 7))
            k.tt("dve", v(xb_, (slice(0, mt), slice(hf * 512, (hf + 1) * 512))),
                 v(xb_, (slice(0, mt), slice(hf * 512, (hf + 1) * 512))), v(pf, (slice(0, mt), slice(None))), ALU.add)
        sc = v(ssB, (slice(0, mt), slice(bi % 2, bi % 2 + 1)))
        k.act(v(h2b, (slice(0, mt), slice(None))), v(xb_, (slice(0, mt), slice(None))), AF.Square, scale=1.0 / 32.0, accum=sc)
        k.rsqrt(sc, sc, 1.0, 1e-6)
        k.stt("dve", v(h2b, (slice(0, mt), slice(None))), v(xb_, (slice(0, mt), slice(None))), sc,
              v(nbc, (slice(0, mt), 1, slice(None))), ALU.mult, ALU.mult)
        if mt == 128:
            pb = k.psb()
            for kc in range(8):
                k.tr(v(pb, (slice(None), slice(kc * 128, (kc + 1) * 128))), v(h2b, (slice(None), slice(kc * 128, (kc + 1) * 128))),
                     v(identb), inc=(kc == 7))
            k.cp("act", v(hT2), V(pb, pb.t[:, :].rearrange("p (kc t) -> p kc t", t=128)))
        else:
            pq = k.psf()
            for kc in range(8):
                k.mm(v(pq, (slice(None), slice(kc * mt, (kc + 1) * mt))), v(h2b, (slice(0, mt), slice(kc * 128, (kc + 1) * 128))),
                     v(identb, (slice(0, mt), slice(0, mt))), inc=(kc == 7))
            k.cp("act", v(hT2, (slice(None), slice(None), slice(0, mt))),
                 V(pq, pq.t[:, 0:8 * mt].rearrange("p (kc t) -> p kc t", t=mt)))
        W_ = 2 + tcs
        for fc_ in range(NFC):
            pf = k.psf()
            for kc in range(8):
                k.mm(v(pf, (slice(None), slice(0, mt))), v(wupb, (slice(None), kc, slice(fc_ * 128, (fc_ + 1) * 128))),
                     v(hT2, (slice(None), kc, slice(0, mt))), start=(kc == 0), stop=(kc == 7))
            if mode == "halo":
                k.ts("dve", v(GH, (slice(None), fc_, slice(None))), v(pf, (slice(None), slice(0, 2))), v(flg), ALU.mult)
                continue
            for kc in range(8):
                k.mm(v(pf, (slice(None), slice(256, 256 + mt))),
                     v(wupb, (slice(None), kc, slice(FF + fc_ * 128, FF + (fc_ + 1) * 128))),
                     v(hT2, (slice(None), kc, slice(0, mt))), start=(kc == 0), stop=(kc == 7))
            G = Gt[fc_ % 3]
            g3 = V(G, G.t[:, 0:ncs * W_].rearrange("p (s t) -> p s t", t=W_))
            k.cp("act", V(G, g3.ap[:, :, 2:W_]), V(pf, pf.t[:, 0:mt].rearrange("p (s t) -> p s t", t=tcs)))
            if mode == "prompt":
                k.cp("pool", v(G, (slice(None), slice(0, 2))), v(GH, (slice(None), fc_, slice(None))))
            else:
                k.cp("pool", V(G, g3.ap[:, :, 0:2]), v(SH, (slice(None), fc_, slice(None), slice(None))))
            gc = Gc[fc_ % 2]
            gc3 = V(gc, gc.t[:, 0:mt].rearrange("p (s t) -> p s t", t=tcs))
            k.ts("dve", gc3, V(G, g3.ap[:, :, 0:tcs]), v(fcwt, (slice(None), fc_, slice(0, 1))), ALU.mult)
            k.stt("dve", gc3, V(G, g3.ap[:, :, 1:1 + tcs]), v(fcwt, (slice(None), fc_, slice(1, 2))), gc3, ALU.mult, ALU.add)
            k.stt("dve", gc3, V(G, g3.ap[:, :, 2:2 + tcs]), v(fcwt, (slice(None), fc_, slice(2, 3))), gc3, ALU.mult, ALU.add)
            k.act(v(gc, (slice(None), slice(0, mt))), v(gc, (slice(None), slice(0, mt))), AF.Silu)
            k.tt("dve", v(Ab, (slice(None), fc_, slice(0, mt))), v(gc, (slice(None), slice(0, mt))),
                 v(pf, (slice(None), slice(256, 256 + mt))), ALU.mult)
            if mode == "prompt":
                k.cp("pool", v(GH, (slice(None), fc_, slice(None))), v(G, (slice(None), slice(tcs, tcs + 2))))
                k.cp("pool", v(FO, (slice(None), fc_, slice(0, 2))), v(G, (slice(None), slice(tcs, tcs + 2))))
            else:
                k.cp("pool", V(FO, FO.t[:, fc_, :].rearrange("p (s t) -> p s t", t=2)), V(G, g3.ap[:, :, tcs:tcs + 2]))
        if mode == "halo":
            return
        for hf in range(2):
            pf = k.psf()
            for fc_ in range(NFC):
                k.mm(v(pf, (slice(0, mt), slice(None))), v(Ab, (slice(None), fc_, slice(0, mt))),
                     v(wdnb, (slice(None), fc_, slice(hf * 512, (hf + 1) * 512))), start=(fc_ == 0), stop=(fc_ == NFC - 1))
            k.tt("dve", v(xb_, (slice(0, mt), slice(hf * 512, (hf + 1) * 512))),
                 v(xb_, (slice(0, mt), slice(hf * 512, (hf + 1) * 512))), v(pf, (slice(0, mt), slice(None))), ALU.add)
        k.act(v(h2b, (slice(0, mt), slice(None))), v(xb_, (slice(0, mt), slice(None))), AF.Square, scale=1.0 / 32.0, accum=sc)
        k.rsqrt(sc, sc, 1.0, 1e-6)
        k.stt("dve", v(xb_, (slice(0, mt), slice(None))), v(xb_, (slice(0, mt), slice(None))), sc,
              v(nbc, (slice(0, mt), 2, slice(None))), ALU.mult, ALU.mult)
        k.dma("sp", v(yB, (slice(yrow0, yrow0 + mt), slice(None))), v(xb_, (slice(0, mt), slice(None))))

    def ffn_conv_out(dst, n):
        for fc_ in range(NFC):
            pf = k.psf()
            k.mm(v(pf, (slice(0, n), slice(0, 128))), v(FO, (slice(None), fc_, slice(0, n))), v(ident))
            k.cp("act", v(FOt, (slice(0, n), fc_ % 2, slice(None))), v(pf, (slice(0, n), slice(0, 128))))
            k.dma("sp", v(dst, (slice(0, n), slice(fc_ * 128, (fc_ + 1) * 128))), v(FOt, (slice(0, n), fc_ % 2, slice(None))))

    load_oT(oTb[1], hoff, 2046, 2)
    ffn_block(0, 0, 2, 1, 2, oTb[1], 0, "halo", 0)
    for t_ in range(NTB // 128):
        ob_ = oTb[(t_ // 2) % 2]
        if t_ % 2 == 0:
            load_oT(ob_, toff, t_ * 128, 256)
        ffn_block(1 + t_, 2 + t_ * 128, 128, 1, 128, ob_, (t_ % 2) * 128, "prompt", t_ * 128)
    ffn_conv_out(ofc_p, 2)
    load_oT(oTb[0], 4, 0, 32, dyn_col=soff)
    ffn_block(17, 2 + NTB, 32, 2, 16, oTb[0], 0, "sample", NTB)
    ffn_conv_out(ofc_s, 4)

    S.finish("sp")
    stB.close()
    k.st.close()
    return nc


_CACHE = {}


def kernel(**inp):
    f = lambda a: np.ascontiguousarray(np.asarray(a, dtype=np.float32))
    xp = f(inp["x_prompt"]); xs = f(inp["x_sample"])
    w_in = f(inp["w_in"])[0]
    GP = 2056
    in_maps = []
    for c in range(NCORES):
        b, j = c // 4, c % 4
        m = {}
        m["xA"] = np.concatenate([xp[b], xs[8 * b:8 * b + 8].reshape(128, D)], 0)
        hr = xp[b, NTB * j - 2:NTB * j] if j > 0 else xp[b, 0:2]
        m["xB"] = np.concatenate([hr, xp[b, NTB * j:NTB * (j + 1)], xs[2 * c:2 * c + 2].reshape(32, D)], 0)
        cols = []
        for sec in range(4):
            cols.append(np.arange(sec * 512 + j * 128, sec * 512 + (j + 1) * 128))
        for sec in range(3):
            cols.append(GP + np.arange(sec * 512 + j * 128, sec * 512 + (j + 1) * 128))
        cols.append(GP + np.arange(1536, 1536 + 288))
        cols.append(np.array([2048 + j, 2052 + j]))
        cols = np.concatenate(cols)
        m["win"] = np.ascontiguousarray(w_in[:, cols])
        pp = np.empty((128, 40), np.float32)
        cw = f(inp["gdn_conv_w"])[0]
        for sec in range(3):
            for tap in range(4):
                pp[:, sec * 4 + tap] = cw[tap, sec * 512 + j * 128: sec * 512 + (j + 1) * 128]
        pp[:, 12] = f(inp["gdn_a_log"])[0, j]
        pp[:, 13] = f(inp["gdn_dt_bias"])[0, j]
        pp[:, 14] = f(inp["gdn_norm"])[0]
        mu = f(inp["rwkv_mu"])[0]
        for sec in range(3):
            pp[:, 15 + sec] = mu[sec * 512 + j * 128: sec * 512 + (j + 1) * 128]
        pp[:, 18] = mu[1536:1664]
        pp[:, 19] = mu[1664:1792]
        pp[:, 20] = np.tile(mu[1792:1824], 4)
        sl = slice(j * 128, (j + 1) * 128)
        pp[:, 21] = f(inp["rwkv_w0"])[0, sl]
        pp[:, 22] = f(inp["rwkv_a0"])[0, sl]
        pp[:, 23] = f(inp["rwkv_k_k"])[0, sl]
        pp[:, 24] = f(inp["rwkv_k_a"])[0, sl]
        pp[:, 25] = f(inp["rwkv_r_k"])[0].reshape(512)[sl]
        pp[:, 26] = f(inp["rwkv_ln_w"])[0, sl]
        pp[:, 27] = f(inp["rwkv_ln_b"])[0, sl]
        pp[:, 28:] = pp[:, 0:12]
        m["ppar"] = pp
        m["wa2"] = np.ascontiguousarray(np.concatenate([f(inp["rwkv_w2"])[0][:, sl], f(inp["rwkv_a2"])[0][:, sl]], 0))
        g2 = f(inp["rwkv_g2"])[0][:, sl]
        m["g2a"] = np.ascontiguousarray(g2[0:128]); m["g2b"] = np.ascontiguousarray(g2[128:160])
        m["nrm"] = np.stack([f(inp["norm_mix"])[0], f(inp["norm_ffn"])[0], f(inp["norm_final"])], 0)
        wo = f(inp["w_out"])[0]
        rows = np.concatenate([np.concatenate([np.arange(jr * 128, (jr + 1) * 128),
                                               512 + np.arange(jr * 128, (jr + 1) * 128)]) for jr in range(4)])
        m["wout"] = np.ascontiguousarray(wo[rows])
        m["wup"] = f(inp["w_up"])[0]; m["wdn"] = f(inp["w_down"])[0]
        seqs = slice(8 * b, 8 * b + 8)
        gc = f(inp["state_gdn_conv"])[0, seqs]
        m["sgc"] = np.ascontiguousarray(np.stack([gc[:, :, sec * 512 + j * 128: sec * 512 + (j + 1) * 128]
                                                  for sec in range(3)], 0).transpose(3, 0, 1, 2))
        sh = f(inp["state_rwkv_shift"])[0, seqs, 0]
        blks = [sh[:, sec * 512 + j * 128: sec * 512 + (j + 1) * 128] for sec in range(3)]
        blks += [sh[:, 1536:1664], sh[:, 1664:1792], np.tile(sh[:, 1792:1824], (1, 4))]
        m["ssh"] = np.ascontiguousarray(np.stack(blks, 0).transpose(2, 0, 1))
        m["sgs"] = np.ascontiguousarray(f(inp["state_gdn"])[0, seqs, j].transpose(1, 0, 2))
        rs = f(inp["state_rwkv"])[0, seqs, 2 * j:2 * j + 2]
        m["srs"] = np.ascontiguousarray(rs.transpose(1, 3, 0, 2).reshape(128, 8, 64))
        fs = f(inp["state_ffn_conv"])[0, 2 * c:2 * c + 2]
        m["sfc"] = np.ascontiguousarray(fs.reshape(2, 2, NFC, 128).transpose(3, 2, 0, 1))
        m["fcw"] = np.ascontiguousarray(f(inp["ffn_conv_w"])[0].reshape(3, NFC, 128).transpose(2, 1, 0))
        m["offs"] = np.array([[j, max(j - 1, 0), 32 * j, 0]], np.int32)
        m["flag"] = np.full((128, 1), 1.0 if j > 0 else 0.0, np.float32)
        in_maps.append(m)
    if "nc" not in _CACHE:
        _CACHE["nc"] = build()
    res = run_bass_kernel_spmd(_CACHE["nc"], in_maps, core_ids=list(range(NCORES)))
    R = res.results
    y_p = np.empty((2, SEQ, D), np.float32); y_s = np.empty((NSAMP, TS, D), np.float32)
    gcp = np.empty((1, 2, 3, 1536), np.float32); gcs = np.empty((1, NSAMP, 3, 1536), np.float32)
    gsp = np.empty((1, 2, 4, 128, 128), np.float32); gss = np.empty((1, NSAMP, 4, 128, 128), np.float32)
    shp = np.empty((1, 2, 1, 1824), np.float32); shs = np.empty((1, NSAMP, 1, 1824), np.float32)
    rsp = np.empty((1, 2, 8, 64, 64), np.float32); rss = np.empty((1, NSAMP, 8, 64, 64), np.float32)
    fcp = np.empty((1, 2, 2, FF), np.float32); fcs = np.empty((1, NSAMP, 2, FF), np.float32)
    for c in range(NCORES):
        b, j = c // 4, c % 4
        r = R[c]
        y = np.asarray(r["yB"])
        y_p[b, NTB * j:NTB * (j + 1)] = y[0:NTB]
        y_s[2 * c:2 * c + 2] = y[NTB:NTB + 32].reshape(2, TS, D)
        og = np.asarray(r["og_p"]).reshape(3, 3, 128)
        ogs_ = np.asarray(r["og_s"]).reshape(3, 8, 3, 128)
        for sec in range(3):
            cs = slice(sec * 512 + j * 128, sec * 512 + (j + 1) * 128)
            gcp[0, b, :, cs] = og[sec]
            gcs[0, 8 * b:8 * b + 8, :, cs] = ogs_[sec]
        gsp[0, b, j] = np.asarray(r["ogs_p"])[0]
        gss[0, 8 * b:8 * b + 8, j] = np.asarray(r["ogs_s"])
        osh = np.asarray(r["osh_p"])
        oshs = np.asarray(r["osh_s"]).reshape(6, 8, 128)
        for sec in range(3):
            cs = slice(sec * 512 + j * 128, sec * 512 + (j + 1) * 128)
            shp[0, b, 0, cs] = osh[sec]
            shs[0, 8 * b:8 * b + 8, 0, cs] = oshs[sec]
        if j == 0:
            shp[0, b, 0, 1536:1664] = osh[3]; shp[0, b, 0, 1664:1792] = osh[4]; shp[0, b, 0, 1792:1824] = osh[5, 0:32]
            shs[0, 8 * b:8 * b + 8, 0, 1536:1664] = oshs[3]; shs[0, 8 * b:8 * b + 8, 0, 1664:1792] = oshs[4]
            shs[0, 8 * b:8 * b + 8, 0, 1792:1824] = oshs[5][:, 0:32]
        rsp[0, b, 2 * j:2 * j + 2] = np.asarray(r["ors_p"])[0]
        rss[0, 8 * b:8 * b + 8, 2 * j:2 * j + 2] = np.asarray(r["ors_s"])
        if j == 3:
            fcp[0, b] = np.asarray(r["ofc_p"])
        fcs[0, 2 * c:2 * c + 2] = np.asarray(r["ofc_s"]).reshape(2, 2, FF)
    return (y_p, y_s, gcp, gcs, gsp, gss, shp, shs, rsp, rss, fcp, fcs)
```

```python
from contextlib import ExitStack
import numpy as np
import concourse.bass as bass
import concourse.mybir as mybir
from concourse.bass_utils import run_bass_kernel_spmd

F32 = mybir.dt.float32
BF16 = mybir.dt.bfloat16
I32 = mybir.dt.int32
AF = mybir.ActivationFunctionType
ALU = mybir.AluOpType

NCORES = 8
D = 1024
SEQ = 8192
NSAMP = 16
TS = 16
NTA = SEQ + 128
NBLK_A = SEQ // 512
NTB = 2048
FF = 2816
NFC = 22
C0 = -0.6065306597126334


class Buf:
    __slots__ = ("t", "w", "r", "name", "track", "excl")

    def __init__(self, t, name="", track=True):
        self.excl = False
        self.t = t
        self.w = None
        self.r = []
        self.name = name
        self.track = track


class V:
    __slots__ = ("b", "ap")

    def __init__(self, b, ap):
        self.b = b
        self.ap = ap


def v(buf, *idx):
    if not idx:
        return V(buf, buf.t[:])
    if len(idx) == 1:
        return V(buf, buf.t[idx[0]])
    return V(buf, buf.t[idx])


class Sync:
    def __init__(self, nc, n_dma_sems=14, self_sync=True):
        self.nc = nc
        self.stack = ExitStack()
        self.self_sync = self_sync
        self.engs = {}
        for nm, e in (("pe", nc.tensor), ("act", nc.scalar), ("dve", nc.vector),
                      ("pool", nc.gpsimd), ("sp", nc.sync)):
            sem = self.stack.enter_context(nc.semaphore("s_" + nm))
            self.engs[nm] = dict(e=e, sem=sem, cnt=0, waited={}, name=nm)
        self.dma_sems = []
        for i in range(n_dma_sems):
            sem = self.stack.enter_context(nc.semaphore("s_dma%d" % i))
            self.dma_sems.append(dict(sem=sem, cnt=0, name="dma%d" % i))
        self.dma_rr = {"hw": 0, "sw": 0}
        self.sems = {k: e["sem"] for k, e in self.engs.items()}
        for d in self.dma_sems:
            self.sems[d["name"]] = d["sem"]
        self.n_ins = 0
        self.n_wait = 0

    def _wait(self, E, tok):
        key, val = tok
        if E["waited"].get(key, 0) >= val:
            return
        if key == E["name"] and (not self.self_sync or key == "pe"):
            return
        E["e"].wait_ge(self.sems[key], val)
        E["waited"][key] = val
        self.n_wait += 1

    def _deps(self, E, reads, writes):
        reads = [b for b in reads if b.track]
        writes = [b for b in writes if b.track]
        for b in reads:
            if b.w is not None:
                self._wait(E, b.w)
        for b in writes:
            if b.w is not None:
                self._wait(E, b.w)
            for tok in b.r:
                self._wait(E, tok)

    def _mark(self, tok, reads, writes):
        reads = [b for b in reads if b.track]
        writes = [b for b in writes if b.track]
        for b in reads:
            if len(b.r) > 24:
                best = {}
                for k, val in b.r:
                    if best.get(k, 0) < val:
                        best[k] = val
                b.r = list(best.items())
            b.r.append(tok)
        for b in writes:
            b.w = tok
            b.r = []

    def op(self, eng, fn, reads=(), writes=(), inc=True):
        E = self.engs[eng]
        if eng != "pe":
            ex = [b for b in reads if b.excl]
            if ex:
                writes = list(writes) + ex
        self._deps(E, reads, writes)
        ins = fn(E["e"])
        self.n_ins += 1
        if inc:
            E["cnt"] += 1
            ins.then_inc(E["sem"], 1)
            tok = (eng, E["cnt"])
        else:
            tok = (eng, E["cnt"] + 1)
        self._mark(tok, reads, writes)
        return ins

    def dma(self, eng, out_ap, in_ap, reads=(), writes=(), **kw):
        E = self.engs[eng]
        self._deps(E, reads, writes)
        kind = "sw" if eng == "pool" else "hw"
        pool_ = self.dma_sems[0:4] if kind == "sw" else self.dma_sems[4:]
        d = pool_[self.dma_rr[kind] % len(pool_)]
        self.dma_rr[kind] += 1
        if d["cnt"] > 0:
            self._wait(E, (d["name"], d["cnt"]))
        ins = E["e"].dma_start(out=out_ap, in_=in_ap, **kw)
        d["cnt"] += 16
        ins.then_inc(d["sem"], 16)
        tok = (d["name"], d["cnt"])
        self.n_ins += 1
        self._mark(tok, reads, writes)
        return ins

    def wait_all_dma(self, eng):
        E = self.engs[eng]
        for d in self.dma_sems:
            if d["cnt"] > 0:
                self._wait(E, (d["name"], d["cnt"]))

    def finish(self, eng="sp"):
        E = self.engs[eng]
        for d in self.dma_sems:
            if d["cnt"] > 0:
                self._wait(E, (d["name"], d["cnt"]))
        for k, e in self.engs.items():
            if k != eng and e["cnt"] > 0:
                self._wait(E, (k, e["cnt"]))


class K:
    def __init__(self, nc):
        self.nc = nc
        self.S = Sync(nc)
        self.st = self.S.stack
        self.psf_i = 0
        self.psb_i = 0

    def sb(self, name, shape, dtype=F32, stack=None):
        t = (stack or self.st).enter_context(self.nc.sbuf_tensor(name, list(shape), dtype))
        return Buf(t, name)

    def ps(self, name, shape, dtype=F32):
        t = self.st.enter_context(self.nc.psum_tensor(name, list(shape), dtype))
        b = Buf(t, name)
        b.excl = True
        return b

    @staticmethod
    def _sc(x):
        return x.ap if isinstance(x, V) else x

    @staticmethod
    def _rb(*xs):
        return [x.b for x in xs if isinstance(x, V)]

    def tt(self, eng, out, a, b, op):
        self.S.op(eng, lambda e: e.tensor_tensor(out=out.ap, in0=a.ap, in1=b.ap, op=op),
                  reads=self._rb(a, b), writes=[out.b])

    def ts(self, eng, out, a, s1, op0, s2=None, op1=None):
        kw = {}
        if op1 is not None:
            kw["op1"] = op1
        self.S.op(eng, lambda e: e.tensor_scalar(out=out.ap, in0=a.ap, scalar1=self._sc(s1),
                                                 scalar2=self._sc(s2), op0=op0, **kw),
                  reads=self._rb(a, s1, s2), writes=[out.b])

    def stt(self, eng, out, a, s, b, op0, op1):
        self.S.op(eng, lambda e: e.scalar_tensor_tensor(out=out.ap, in0=a.ap, scalar=self._sc(s),
                                                        in1=b.ap, op0=op0, op1=op1),
                  reads=self._rb(a, s, b), writes=[out.b])

    def act(self, out, a, func, bias=None, scale=None, accum=None):
        kw = {}
        if bias is not None:
            kw["bias"] = self._sc(bias)
        if scale is not None:
            kw["scale"] = self._sc(scale)
        if accum is not None:
            kw["accum_out"] = accum.ap
        w = [out.b] + ([accum.b] if accum is not None else [])
        self.S.op("act", lambda e: e.activation(out=out.ap, in_=a.ap, func=func, **kw),
                  reads=self._rb(a, bias, scale), writes=w)

    def rsqrt(self, out, a, scale, eps):
        npart = out.ap.shape[0]
        eb, ec = self.epsv[eps]
        self.act(out, a, AF.Sqrt, bias=V(eb, eb.t[0:npart, ec:ec + 1]), scale=scale)
        self.S.op("dve", lambda e: e.reciprocal(out=out.ap, in_=out.ap), reads=[out.b], writes=[out.b])

    def cp(self, eng, out, a):
        if eng == "act":
            self.S.op("act", lambda e: e.copy(out=out.ap, in_=a.ap), reads=[a.b], writes=[out.b])
        else:
            self.S.op(eng, lambda e: e.tensor_copy(out=out.ap, in_=a.ap), reads=[a.b], writes=[out.b])

    def memset(self, eng, out, val):
        self.S.op(eng, lambda e: e.memset(out.ap, val), writes=[out.b])

    def asel(self, out, a, pattern, cmp, base, cm, fill=0.0):
        self.S.op("pool", lambda e: e.affine_select(out=out.ap, in_=a.ap, pattern=pattern, compare_op=cmp,
                                                    fill=fill, base=base, channel_multiplier=cm),
                  reads=[a.b], writes=[out.b])

    def mm(self, out, lhsT, rhs, start=True, stop=True, inc=None):
        if inc is None:
            inc = stop
        self.S.op("pe", lambda e: e.matmul(out.ap, lhsT.ap, rhs.ap, start=start, stop=stop),
                  reads=[lhsT.b, rhs.b], writes=[out.b], inc=inc)

    def tr(self, out, a, ident, inc=True):
        self.S.op("pe", lambda e: e.transpose(out.ap, a.ap, ident.ap), reads=[a.b, ident.b],
                  writes=[out.b], inc=inc)

    def scan(self, out, ones, a):
        self.S.op("dve", lambda e: e.tensor_tensor_scan(out=out.ap, data0=ones.ap, data1=a.ap, initial=0.0,
                                                       op0=ALU.mult, op1=ALU.add),
                  reads=[ones.b, a.b], writes=[out.b])

    def dma(self, eng, out, a, **kw):
        self.S.dma(eng, out.ap, a.ap, reads=[a.b], writes=[out.b], **kw)

    def psf(self):
        b = self.PSF[self.psf_i % len(self.PSF)]
        self.psf_i += 1
        return b

    def psb(self):
        b = self.PSB[self.psb_i % len(self.PSB)]
        self.psb_i += 1
        return b


def build():
    nc = bass.Bass("TRN2", target_bir_lowering=False)
    k = K(nc)
    S = k.S

    def din(name, shape, dt=F32):
        return Buf(nc.dram_tensor(name, list(shape), dt, kind="ExternalInput"), name, track=False)

    def dout(name, shape, dt=F32):
        return Buf(nc.dram_tensor(name, list(shape), dt, kind="ExternalOutput"), name, track=False)

    xA = din("xA", [NTA, D])
    xB = din("xB", [2 + NTB + 32, D])
    win = din("win", [D, 1186])
    ppar = din("ppar", [128, 40])
    wa2 = din("wa2", [128, 128])
    g2a = din("g2a", [128, 128])
    g2b = din("g2b", [32, 128])
    nrm = din("nrm", [3, D])
    wout = din("wout", [D, D])
    wup = din("wup", [D, 2 * FF])
    wdn = din("wdn", [FF, D])
    sgc = din("sgc", [128, 3, 8, 3])
    ssh = din("ssh", [128, 6, 8])
    sgs = din("sgs", [128, 8, 128])
    srs = din("srs", [128, 8, 64])
    sfc = din("sfc", [128, NFC, 2, 2])
    fcw = din("fcw", [128, NFC, 3])
    offs = din("offs", [1, 4], I32)
    flag = din("flag", [128, 1])

    yB = dout("yB", [NTB + 32, D])
    og_p = dout("og_p", [9, 128])
    og_s = dout("og_s", [72, 128])
    osh_p = dout("osh_p", [6, 128])
    osh_s = dout("osh_s", [48, 128])
    ogs_p = dout("ogs_p", [1, 128, 128])
    ogs_s = dout("ogs_s", [8, 128, 128])
    ors_p = dout("ors_p", [1, 2, 64, 64])
    ors_s = dout("ors_s", [8, 2, 64, 64])
    ofc_p = dout("ofc_p", [2, FF])
    ofc_s = dout("ofc_s", [4, FF])

    import os
    KDBG = os.environ.get("KDBG", "")
    if KDBG:
        dbg = dout("dbg", [5, 256, 2048], BF16)
    agin = Buf(nc.dram_tensor("agin", [5, 256, 2048], BF16), "agin", track=False)
    agout = Buf(nc.dram_tensor("agout", [5, 1024, 2048], BF16), "agout", track=False)
    cc_sem = k.st.enter_context(nc.semaphore("cc"))
    cc_n = [0]

    def exchange(piece):
        S.wait_all_dma("pool")
        nc.gpsimd.collective_compute("AllGather", ALU.bypass, replica_groups=[[0, 1, 2, 3], [4, 5, 6, 7]],
                                     ins=[agin.t.ap()[piece].opt()], outs=[agout.t.ap()[piece].opt()]).then_inc(cc_sem, 1)
        cc_n[0] += 1

    k.PSF = [k.ps("psf%d" % i, [128, 512], F32) for i in range(6)]
    k.PSB = [k.ps("psb%d" % i, [128, 1024], BF16) for i in range(2)]

    ones = k.sb("ones", [128, 512])
    onesb = k.sb("onesb", [128, 128], BF16)
    oblk = k.sb("oblk", [128, 128], BF16)
    ident = k.sb("ident", [128, 128])
    identb = k.sb("identb", [128, 128], BF16)
    tmpc = k.sb("tmpc", [128, 128])
    k.memset("pool", v(ones), 1.0)
    k.asel(v(ident), v(ones, (slice(None), slice(0, 128))), [[-1, 128]], ALU.is_equal, 0, 1)
    k.cp("pool", v(identb), v(ident))
    k.cp("pool", v(onesb), v(ones, (slice(None), slice(0, 128))))
    o128 = v(ones, (slice(None), slice(0, 128)))
    o3 = V(ones, ones.t[:, 0:128].rearrange("p (h x) -> p h x", x=64))
    t3 = V(tmpc, tmpc.t[:, :].rearrange("p (h x) -> p h x", x=64))
    k.asel(t3, o3, [[-64, 2], [0, 64]], ALU.is_ge, 0, 1)
    k.asel(t3, t3, [[64, 2], [0, 64]], ALU.is_ge, 63, -1)
    k.cp("pool", v(oblk), v(tmpc))

    MK = []
    bd = k.sb("bd16", [128, 128])
    b3o = V(ones, ones.t[:, 0:128].rearrange("p (q x) -> p q x", x=16))
    b3 = V(bd, bd.t[:, :].rearrange("p (q x) -> p q x", x=16))
    k.asel(b3, b3o, [[-16, 8], [0, 16]], ALU.is_ge, 0, 1)
    k.asel(b3, b3, [[16, 8], [0, 16]], ALU.is_ge, 15, -1)
    for mi in range(2):
        m = {}
        for nm, pat, cmp, cm in (("Ls", [[-1, 128]], ALU.is_gt, 1), ("Us", [[1, 128]], ALU.is_gt, -1),
                                 ("Ui", [[1, 128]], ALU.is_ge, -1)):
            t = k.sb("m%s%d" % (nm, mi), [128, 128])
            k.asel(v(t), o128, pat, cmp, 0, cm)
            if mi == 1:
                k.tt("pool", v(t), v(t), v(bd), ALU.mult)
            m[nm] = t
        for nm in ("Ls", "Us"):
            t = k.sb("mn%s%d" % (nm, mi), [128, 128])
            k.ts("pool", v(t), v(m[nm]), -1.0, ALU.mult)
            m["n" + nm] = t
        MK.append(m)
    colm = k.sb("colm", [128, 8, 128], BF16)
    on4 = V(ones, ones.t[:, 0:128].rearrange("p (r x) -> p r x", x=16))
    for q in range(8):
        k.asel(V(colm, colm.t[:, q, :].rearrange("p (r x) -> p r x", x=16)), on4, [[1, 8], [0, 16]],
               ALU.is_equal, -q, 0)
    rowm = k.sb("rowm", [128, 8])
    k.asel(v(rowm), v(ones, (slice(None), slice(0, 8))), [[-16, 8]], ALU.is_ge, 0, 1)
    k.asel(v(rowm), v(rowm), [[16, 8]], ALU.is_ge, 15, -1)

    epst = k.sb("epst", [128, 2])
    k.memset("pool", v(epst, (slice(None), slice(0, 1))), 1e-6)
    k.memset("pool", v(epst, (slice(None), slice(1, 2))), 64e-5)
    k.epsv = {1e-6: (epst, 0), 64e-5: (epst, 1)}
    pp = k.sb("pp", [128, 40])
    k.dma("sp", v(pp), v(ppar))
    PC = lambda i: v(pp, (slice(None), slice(i, i + 1)))
    der = k.sb("der", [128, 4])
    DC = lambda i: v(der, (slice(None), slice(i, i + 1)))
    k.act(DC(0), PC(12), AF.Exp)
    k.ts("dve", DC(0), DC(0), -1.0, ALU.mult)
    k.ts("dve", DC(1), PC(24), -1.0, ALU.mult, 1.0, ALU.add)
    flg = k.sb("flg", [128, 1])
    k.dma("sp", v(flg), v(flag))
    nbc = k.sb("nbc", [128, 3, D])
    for i in range(3):
        k.S.dma("sp", nbc.t[:, i, :], nrm.t[i:i + 1, :].partition_broadcast(128), reads=[nrm], writes=[nbc])

    import os
    KSTOP = os.environ.get("KSTOP", "")
    if KSTOP == "C0":
        S.finish("sp"); k.st.close(); return nc
    stA = ExitStack()
    A = lambda name, shape, dt=F32: k.sb(name, shape, dt, stack=stA)
    winb = A("winb", [128, 8, 1280], BF16)
    k.memset("pool", v(winb, (slice(None), slice(None), slice(1152, 1280))), 0.0)
    wv = win.t[:, :].rearrange("(kc p) c -> p kc c", p=128)
    for kc in range(8):
        S.dma("pool", winb.t[:, kc, 0:1185], wv[:, kc, 0:1185], reads=[win], writes=[winb])
    with nc.allow_non_contiguous_dma(reason="single weight column"):
        S.dma("pool", winb.t[:, :, 1216:1217], wv[:, :, 1185:1186], reads=[win], writes=[winb])
    wa2b = A("wa2b", [128, 128], BF16)
    g2ab = A("g2ab", [128, 128], BF16)
    g2bb = A("g2bb", [32, 128], BF16)
    k.dma("pool", v(wa2b), v(wa2))
    k.dma("pool", v(g2ab), v(g2a))
    k.dma("pool", v(g2bb), v(g2b))

    if KSTOP == "C1":
        S.finish("sp"); stA.close(); k.st.close(); return nc
    xt = [A("xt%d" % i, [128, D]) for i in range(2)]
    hb = [A("hb%d" % i, [128, D], BF16) for i in range(2)]
    ss = A("ss", [128, 2])
    hT = A("hT", [128, 8, 512], BF16)
    PB = {}
    for nm in ("q", "k", "v", "r", "kr", "vr", "wa", "g0", "g1"):
        t_ = A("P%s" % nm, [128, 3 + 512])
        k.memset("pool", v(t_), 0.0)
        PB[nm] = [t_, t_]
    PZ = A("PZ", [128, 512])
    W = {}
    for nm in ("cq", "ck", "cv", "t0", "t1", "t2", "t3", "betab", "lg", "gb", "eg", "bg", "kn", "qn",
               "sr", "skr", "svr", "gg", "krp", "Ep", "Ee", "Em", "Eh", "yg", "yr", "sz", "yc"):
        W[nm] = A("W" + nm, [128, 512])
    for a_, b_ in (("swa", "cq"), ("sg0", "ck"), ("sg1", "cv"), ("sig", "lg"), ("av", "bg"), ("kk", "eg"),
                   ("bv", "betab"), ("cs", "kn"), ("cse", "qn")):
        W[a_] = W[b_]
    WB = {}
    for nm in ("sqb", "knb", "kbb", "qdT", "kbgT", "vbT", "kdT", "T7", "sg0b", "sg1b", "atok", "vrb", "Bh", "Kh",
               "rk2", "ob", "qnb"):
        WB[nm] = A("B" + nm, [128, 512], BF16)
    BKt = A("BKt", [128, 4, 2, 128], BF16)
    ARt = A("ARt", [128, 4, 2, 128], BF16)
    gcol = A("gcol", [128, 4])
    glast = A("glast", [128, 8])
    gend = A("gend", [128, 8])

    CH = []
    for u in range(3):
        d = {}
        for nm in ("Nb", "NTb", "XTb", "Pb", "PTb", "Pb2", "PTb2", "Aak", "ArbT", "ArkT", "M2T"):
            d[nm] = A("c%s%d" % (nm, u), [128, 128], BF16)
        for nm in ("XT", "e1a", "e1b", "Dm", "DTm", "DTi"):
            d[nm] = A("c%s%d" % (nm, u), [128, 128])
        CH.append(d)
    kbg_t = A("kbg_t", [128, 128], BF16)
    Vb_t = A("Vb_t", [128, 128], BF16)
    kd_t = A("kd_t", [128, 128], BF16)
    AxTg = A("AxTg", [128, 128], BF16)
    vnew = A("vnew", [128, 128], BF16)
    padz = {}
    for nm in ("atokp", "Vp", "Bhp", "Khp", "Up"):
        t = A("pd" + nm, [128, 384], BF16)
        k.memset("pool", v(t), 0.0)
        padz[nm] = t
    PADV = lambda t, h: v(t, (slice(None), slice(h * 128, h * 128 + 128)))
    DIAG = lambda t: V(t, t.t[:, 0:384].rearrange("p (h x) -> p h x", x=192)[:, :, 0:64])
    AxTr = A("AxTr", [128, 128], BF16)

    def dcopy(eng, t, src, c0):
        for h in range(2):
            k.cp(eng, v(t, (slice(None), slice(h * 192, h * 192 + 64))),
                 v(src, (slice(None), slice(c0 + h * 64, c0 + h * 64 + 64))))

    Sg32 = A("Sg32", [128, 128]); Sgb = A("Sgb", [128, 128], BF16)
    Hr32 = A("Hr32", [128, 128]); Hrb = A("Hrb", [128, 128], BF16)
    k.memset("pool", v(Sg32), 0.0); k.memset("pool", v(Sgb), 0.0)
    k.memset("pool", v(Hr32), 0.0); k.memset("pool", v(Hrb), 0.0)
    S0g32 = A("S0g32", [128, 8, 128]); S0gb = A("S0gb", [128, 8, 128], BF16)
    H0r32 = A("H0r32", [128, 8, 128]); H0rb = A("H0rb", [128, 8, 128], BF16)
    mskA = A("mskA", [128, 8, 128], BF16)
    mskQ = A("mskQ", [128, 8, 128], BF16)
    mskK = A("mskK", [128, 2, 256], BF16)
    mskK2 = A("mskK2", [128, 2, 256], BF16)
    Sfin = A("Sfin", [128, 2, 128])
    COs = {8: A("CO8", [128, 3, 8, 3]), 1: A("CO1", [128, 3, 1, 3])}
    CO2s = {8: A("CO28", [128, 6, 8]), 1: A("CO21", [128, 6, 1])}
    COt = A("COt", [128, 128])
    HTt = A("HTt", [128, 128])

    def in_proj_block(tok0, NT, par, ncs, tcs, halo_src):
        ntile = NT // 128
        for i in range(ntile):
            xti = xt[i % 2]; hbi = hb[i % 2]
            k.dma("sp" if i % 2 == 0 else "act", v(xti), v(xA, (slice(tok0 + i * 128, tok0 + (i + 1) * 128), slice(None))))
            sc = v(ss, (slice(None), slice(i % 2, i % 2 + 1)))
            k.act(v(hbi), v(xti), AF.Square, scale=1.0 / 32.0, accum=sc)
            k.rsqrt(sc, sc, 1.0, 1e-6)
            k.stt("dve", v(hbi), v(xti), sc, v(nbc, (slice(None), 0, slice(None))), ALU.mult, ALU.mult)
            pb = k.psb()
            for kc in range(8):
                k.tr(v(pb, (slice(None), slice(kc * 128, (kc + 1) * 128))),
                     v(hbi, (slice(None), slice(kc * 128, (kc + 1) * 128))), v(identb), inc=(kc == 7))
            k.cp("act", v(hT, (slice(None), slice(None), slice(i * 128, (i + 1) * 128))),
                 V(pb, pb.t[:, :].rearrange("p (kc t) -> p kc t", t=128)))
        blocks = [("q", 0, 128, 3), ("k", 128, 128, 3), ("v", 256, 128, 3), ("z", 384, 128, 0),
                  ("r", 512, 128, 1), ("kr", 640, 128, 1), ("vr", 768, 128, 1), ("wa", 896, 128, 1),
                  ("g0", 1024, 128, 1), ("g1", 1152, 65, 1)]
        for nm, c0, m, halo in blocks:
            pf = k.psf()
            for kc in range(8):
                k.mm(v(pf, (slice(0, m), slice(0, NT))), v(winb, (slice(None), kc, slice(c0, c0 + m))),
                     v(hT, (slice(None), kc, slice(0, NT))), start=(kc == 0), stop=(kc == 7))
            if nm == "z":
                k.cp("act", v(PZ, (slice(None), slice(0, NT))), v(pf, (slice(None), slice(0, NT))))
                continue
            P = PB[nm][par]
            H = 3
            dst = V(P, P.t[0:m, 0:ncs * (H + tcs)].rearrange("p (s t) -> p s t", t=H + tcs)[:, :, H:H + tcs])
            src = V(pf, pf.t[0:m, 0:NT].rearrange("p (s t) -> p s t", t=tcs))
            k.cp("act", dst, src)

    def pview(P, ncs, tcs, lo, hi, m=128):
        return V(P, P.t[0:m, 0:ncs * (3 + tcs)].rearrange("p (s t) -> p s t", t=3 + tcs)[:, :, 3 + lo:3 + hi])

    def w3(t, NT, tcs, m=128):
        return V(t, t.t[0:m, 0:NT].rearrange("p (s t) -> p s t", t=tcs))

    def w2(t, NT, m=128, lo=0):
        return v(t, (slice(lo, lo + m) if lo else slice(0, m), slice(0, NT)))

    def doubling_multi(units, nlev):
        cur = {}
        for u in units:
            d = CH[u]
            k.tt("dve", v(d["XT"]), v(ident), v(d["NTb"]), ALU.add)
            k.cp("act", v(d["XTb"]), v(d["XT"]))
            cur[u] = (d["Nb"], d["NTb"])
        for m in range(1, nlev):
            last = (m == nlev - 1)
            pfs = {}
            for u in units:
                P, PT = cur[u]
                pf = k.psf()
                pfs[u] = pf
                k.mm(v(pf, (slice(None), slice(0, 128))), v(PT), v(P))
                if not last:
                    k.mm(v(pf, (slice(None), slice(128, 256))), v(P), v(PT))
            nxt = {}
            for i_, u in enumerate(units):
                d = CH[u]
                Pn = d["Pb"] if m % 2 else d["Pb2"]
                PTn = d["PTb"] if m % 2 else d["PTb2"]
                eng = "act" if i_ % 2 == 0 else "dve"
                k.cp(eng, v(Pn), v(pfs[u], (slice(None), slice(0, 128))))
                if not last:
                    k.cp(eng, v(PTn), v(pfs[u], (slice(None), slice(128, 256))))
                nxt[u] = (Pn, PTn)
            pf2s = {}
            for u in units:
                pf2 = k.psf()
                pf2s[u] = pf2
                k.mm(v(pf2, (slice(None), slice(0, 128))), v(nxt[u][0]), v(CH[u]["XTb"]))
            for u in units:
                d = CH[u]
                k.tt("dve", v(d["XT"]), v(d["XT"]), v(pf2s[u], (slice(None), slice(0, 128))), ALU.add)
                k.cp("act", v(d["XTb"]), v(d["XT"]))
            cur = nxt

    def mixer_block(NT, par, ncs, tcs, nsg, seg, mi, nseq, tokcol0, last_out):
        nch = NT // 128
        nlev = 7 if seg == 128 else 4
        nlev = int(os.environ.get("KNLEV", nlev))
        m = MK[mi]
        f = lambda t, lo=0, hi=None: v(t, (slice(None), slice(lo, NT if hi is None else hi)))
        for si, (nm, cn) in enumerate((("q", "cq"), ("k", "ck"), ("v", "cv"))):
            P = PB[nm][par]
            o3_ = w3(W[cn], NT, tcs)
            eng = "dve"
            k.ts(eng, o3_, pview(P, ncs, tcs, -3, tcs - 3), PC(si * 4 + 0), ALU.mult)
            for tap in (1, 2, 3):
                k.stt(eng, o3_, pview(P, ncs, tcs, tap - 3, tcs + tap - 3), PC(si * 4 + tap), o3_,
                      ALU.mult, ALU.add)
            k.act(f(W[cn]), f(W[cn]), AF.Silu)
        g1 = PB["g1"][par]
        for ri, prt in enumerate((32, 64)):
            pf = k.psf()
            rsrc = V(g1, g1.t[prt:prt + 1, 0:ncs * (3 + tcs)].rearrange("p (s t) -> p s t", t=3 + tcs)[:, :, 3:3 + tcs])
            k.mm(V(pf, pf.t[:, 0:NT].rearrange("p (s t) -> p s t", t=tcs)),
                 v(ones, (slice(prt, prt + 1), slice(0, 128))), rsrc)
            if ri == 0:
                k.act(f(W["betab"]), f(pf), AF.Sigmoid)
            else:
                k.act(f(W["t0"]), f(pf), AF.Exp, bias=PC(13))
                k.act(f(W["t0"]), f(W["t0"]), AF.Ln, bias=1.0)
                k.ts("dve", f(W["lg"]), f(W["t0"]), DC(0), ALU.mult)
        for sgi in range(nsg):
            k.scan(f(W["gb"], sgi * seg, (sgi + 1) * seg), f(ones, 0, seg), f(W["lg"], sgi * seg, (sgi + 1) * seg))
        k.act(f(W["eg"]), f(W["gb"]), AF.Exp)
        k.tt("pool", f(W["bg"]), f(W["betab"]), f(W["eg"]), ALU.mult)
        for nm, cn, on in (("q", "cq", "qn"), ("k", "ck", "kn")):
            k.act(f(WB["sqb"]), f(W[cn]), AF.Square)
            pf = k.psf()
            k.mm(f(pf), v(onesb), f(WB["sqb"]))
            k.rsqrt(f(W["t1"]), f(pf), 1.0, 1e-6)
            if nm == "q":
                k.stt("dve", f(W[on]), f(W[cn]), 128.0 ** -0.5, f(W["t1"]), ALU.mult, ALU.mult)
            else:
                k.tt("dve", f(W[on]), f(W[cn]), f(W["t1"]), ALU.mult)
        k.cp("pool", f(WB["knb"]), f(W["kn"]))
        k.tt("pool", f(WB["kbb"]), f(W["kn"]), f(W["betab"]), ALU.mult)
        k.tt("dve", f(WB["qdT"]), f(W["qn"]), f(W["eg"]), ALU.mult)
        k.cp("pool", f(WB["qnb"]), f(W["qn"]))
        k.tt("pool", f(WB["kbgT"]), f(W["kn"]), f(W["bg"]), ALU.mult)
        k.tt("dve", f(WB["vbT"]), f(W["cv"]), f(W["betab"]), ALU.mult)
        gb3 = w3(W["gb"], NT, seg)
        gendv = V(W["gb"], W["gb"].t[:, 0:NT].rearrange("p (s t) -> p s t", t=seg)[:, :, seg - 1:seg])
        k.tt("dve", w3(W["t2"], NT, seg), V(W["gb"], gendv.ap.to_broadcast([128, nsg, seg])), gb3, ALU.subtract)
        k.act(f(W["t2"]), f(W["t2"]), AF.Exp)
        k.tt("pool", f(WB["kdT"]), f(W["kn"]), f(W["t2"]), ALU.mult)
        k.cp("dve", v(glast, (slice(None), slice(0, nsg))),
             V(W["eg"], W["eg"].t[:, 0:NT].rearrange("p (s t) -> p s t", t=seg)[:, :, seg - 1]))
        k.act(f(W["sz"]), f(PZ), AF.Silu)

        if KSTOP == "M1":
            return
        for nm, on, mrows, mc in (("r", "sr", 128, 15), ("kr", "skr", 128, 16), ("vr", "svr", 128, 17),
                                  ("wa", "swa", 128, 18), ("g0", "sg0", 128, 19), ("g1", "sg1", 32, 20)):
            P = PB[nm][par]
            o3_ = w3(W[on], NT, tcs, mrows)
            cur = pview(P, ncs, tcs, 0, tcs, mrows)
            prv = pview(P, ncs, tcs, -1, tcs - 1, mrows)
            k.tt("pool", o3_, prv, cur, ALU.subtract)
            k.stt("dve", o3_, o3_, v(pp, (slice(0, mrows), slice(mc, mc + 1))), cur, ALU.mult, ALU.add)
        k.act(v(WB["T7"], (slice(0, 64), slice(0, NT))), v(W["swa"], (slice(0, 64), slice(0, NT))), AF.Tanh)
        k.cp("act", v(WB["T7"], (slice(64, 128), slice(0, NT))), v(W["swa"], (slice(64, 128), slice(0, NT))))
        k.act(f(WB["sg0b"]), f(W["sg0"]), AF.Sigmoid)
        k.act(v(WB["sg1b"], (slice(0, 32), slice(0, NT))), v(W["sg1"], (slice(0, 32), slice(0, NT))), AF.Sigmoid)
        pf = k.psf()
        k.mm(f(pf), v(wa2b, (slice(0, 64), slice(None))), v(WB["T7"], (slice(0, 64), slice(0, NT))))
        k.act(f(W["sig"]), f(pf), AF.Sigmoid, bias=PC(21))
        pf = k.psf()
        k.mm(f(pf), v(wa2b, (slice(64, 128), slice(None))), v(WB["T7"], (slice(64, 128), slice(0, NT))))
        k.act(f(W["av"]), f(pf), AF.Sigmoid, bias=PC(22))
        pf = k.psf()
        k.mm(f(pf), v(g2ab), f(WB["sg0b"]), start=True, stop=False)
        k.mm(f(pf), v(g2bb), v(WB["sg1b"], (slice(0, 32), slice(0, NT))), start=False, stop=True)
        k.cp("act", f(W["gg"]), f(pf))
        k.ts("dve", f(W["kk"]), f(W["skr"]), PC(23), ALU.mult)
        k.act(f(WB["sqb"]), f(W["kk"]), AF.Square)
        pf = k.psf()
        k.mm(f(pf), v(oblk), f(WB["sqb"]))
        k.rsqrt(f(W["t1"]), f(pf), 1.0, 1e-6)
        k.tt("dve", f(W["kk"]), f(W["kk"]), f(W["t1"]), ALU.mult)
        k.ts("dve", f(W["t3"]), f(W["av"]), PC(24), ALU.mult, DC(1), ALU.add)
        k.tt("dve", f(W["krp"]), f(W["skr"]), f(W["t3"]), ALU.mult)
        k.tt("pool", f(W["bv"]), f(W["kk"]), f(W["av"]), ALU.mult)
        for sgi in range(nsg):
            k.scan(f(W["cs"], sgi * seg, (sgi + 1) * seg), f(ones, 0, seg), f(W["sig"], sgi * seg, (sgi + 1) * seg))
        k.tt("pool", f(W["cse"]), f(W["cs"]), f(W["sig"]), ALU.subtract)
        k.act(f(W["Ep"]), f(W["cs"]), AF.Exp, scale=C0)
        k.act(f(W["Ee"]), f(W["cse"]), AF.Exp, scale=C0)
        k.act(f(W["Em"]), f(W["cs"]), AF.Exp, scale=-C0)
        cendv = V(W["cs"], W["cs"].t[:, 0:NT].rearrange("p (s t) -> p s t", t=seg)[:, :, seg - 1:seg])
        k.tt("dve", w3(W["Eh"], NT, seg), V(W["cs"], cendv.ap.to_broadcast([128, nsg, seg])), w3(W["cs"], NT, seg),
             ALU.subtract)
        k.act(f(W["Eh"]), f(W["Eh"]), AF.Exp, scale=C0)
        k.cp("dve", v(gend, (slice(None), slice(0, nsg))),
             V(W["Ep"], W["Ep"].t[:, 0:NT].rearrange("p (s t) -> p s t", t=seg)[:, :, seg - 1]))
        BK4 = lambda j: V(BKt, BKt.t[:, 0:nch, j, :])
        AR4 = lambda j: V(ARt, ARt.t[:, 0:nch, j, :])
        c4 = lambda t: V(t, t.t[:, 0:NT].rearrange("p (c t) -> p c t", t=128))
        k.tt("dve", BK4(0), c4(W["bv"]), c4(W["Em"]), ALU.mult)
        k.tt("pool", BK4(1), c4(W["krp"]), c4(W["Em"]), ALU.mult)
        k.stt("dve", AR4(0), c4(W["kk"]), -1.0, c4(W["Ee"]), ALU.mult, ALU.mult)
        k.tt("pool", AR4(1), c4(W["sr"]), c4(W["Ep"]), ALU.mult)
        k.stt("dve", f(WB["atok"]), f(W["kk"]), -1.0, f(W["Ee"]), ALU.mult, ALU.mult)
        k.cp("pool", f(WB["vrb"]), f(W["svr"]))
        k.tt("pool", f(WB["Bh"]), f(W["bv"]), f(W["Eh"]), ALU.mult)
        k.tt("dve", f(WB["Kh"]), f(W["krp"]), f(W["Eh"]), ALU.mult)

        if KSTOP in ("M2", "XM2"):
            return
        for c in range(nch):
            cs_ = slice(c * 128, (c + 1) * 128)
            fc = lambda t: v(t, (slice(None), cs_))
            d = CH[0]
            pf = k.psf()
            k.mm(v(pf, (slice(None), slice(0, 1))), v(W["gb"], (slice(0, 1), cs_)), v(ones, (slice(0, 1), slice(0, 1))))
            k.cp("act", v(gcol, (slice(None), slice(0, 1))), v(pf, (slice(None), slice(0, 1))))
            gbc = v(gcol, (slice(None), slice(0, 1)))
            k.ts("dve", v(d["e1a"]), fc(W["gb"]), gbc, ALU.subtract, 0.0, ALU.min)
            k.ts("dve", v(d["e1b"]), fc(W["gb"]), gbc, ALU.subtract, 0.0, ALU.max)
            k.act(v(d["e1a"]), v(d["e1a"]), AF.Exp)
            k.act(v(d["e1b"]), v(d["e1b"]), AF.Exp, scale=-1.0)
            k.tt("pool", v(d["Dm"]), v(d["e1b"]), v(m["nLs"]), ALU.mult)
            k.tt("pool", v(d["DTm"]), v(d["e1a"]), v(m["nUs"]), ALU.mult)
            k.tt("pool", v(d["DTi"]), v(d["e1a"]), v(m["Ui"]), ALU.mult)
            pf = k.psf()
            k.mm(v(pf, (slice(None), slice(0, 128))), fc(WB["kbb"]), fc(WB["knb"]))
            k.mm(v(pf, (slice(None), slice(128, 256))), fc(WB["knb"]), fc(WB["kbb"]))
            k.tt("dve", v(d["Nb"]), v(pf, (slice(None), slice(0, 128))), v(d["Dm"]), ALU.mult)
            k.tt("dve", v(d["NTb"]), v(pf, (slice(None), slice(128, 256))), v(d["DTm"]), ALU.mult)
            pf3 = k.psf()
            k.mm(v(pf3, (slice(None), slice(0, 128))), fc(WB["knb"]), fc(WB["qnb"]))
            k.tt("dve", v(d["ArbT"]), v(pf3, (slice(None), slice(0, 128))), v(d["DTi"]), ALU.mult)
            pb = k.psb()
            k.tr(v(pb, (slice(None), slice(0, 128))), fc(WB["kbgT"]), v(identb), inc=False)
            k.tr(v(pb, (slice(None), slice(128, 256))), fc(WB["vbT"]), v(identb), inc=False)
            k.tr(v(pb, (slice(None), slice(256, 384))), fc(WB["kdT"]), v(identb))
            k.cp("act", v(kbg_t), v(pb, (slice(None), slice(0, 128))))
            k.cp("act", v(Vb_t), v(pb, (slice(None), slice(128, 256))))
            k.cp("act", v(kd_t), v(pb, (slice(None), slice(256, 384))))

            if KSTOP == "M3":
                return
            for h in range(2):
                d = CH[1 + h]
                hs = slice(64 * h, 64 * h + 64)
                at = V(ARt, ARt.t[hs, c, 0, :]); rt = V(ARt, ARt.t[hs, c, 1, :])
                bt = V(BKt, BKt.t[hs, c, 0, :]); kt = V(BKt, BKt.t[hs, c, 1, :])
                pf = k.psf()
                if KSTOP == "R0a":
                    return
                k.mm(v(pf, (slice(None), slice(0, 256))), at, V(BKt, BKt.t[hs, c, :, :].rearrange("p a b -> p (a b)")))
                if KSTOP == "R0":
                    return
                k.tt("dve", v(d["Nb"]), v(pf, (slice(None), slice(0, 128))), v(m["Ls"]), ALU.mult)
                k.tt("dve", v(d["Aak"]), v(pf, (slice(None), slice(128, 256))), v(m["Ls"]), ALU.mult)
                if KSTOP == "R1":
                    return
                pf = k.psf()
                k.mm(v(pf, (slice(None), slice(0, 256))), bt, V(ARt, ARt.t[hs, c, :, :].rearrange("p a b -> p (a b)")))
                k.mm(v(pf, (slice(None), slice(256, 384))), kt, rt)
                k.tt("dve", v(d["NTb"]), v(pf, (slice(None), slice(0, 128))), v(m["Us"]), ALU.mult)
                k.tt("dve", v(d["ArbT"]), v(pf, (slice(None), slice(128, 256))), v(m["Ui"]), ALU.mult)
                k.tt("dve", v(d["ArkT"]), v(pf, (slice(None), slice(256, 384))), v(m["Ui"]), ALU.mult)
                if KSTOP == "R2":
                    return
            doubling_multi([0, 1, 2], nlev)
            d = CH[0]
            pf = k.psf()
            k.mm(v(pf, (slice(None), slice(0, 128))), v(kbg_t), v(d["XTb"]))
            k.act(v(AxTg), v(pf, (slice(None), slice(0, 128))), AF.Identity, scale=-1.0)
            for h in range(2):
                d = CH[1 + h]
                pf = k.psf()
                k.mm(v(pf, (slice(None), slice(0, 128))), v(d["Aak"]), v(d["XTb"]))
                k.cp("act", v(d["M2T"]), v(pf, (slice(None), slice(0, 128))))
            pb = k.psb()
            for i_, nm in enumerate(("atok", "vrb", "Bh", "Kh")):
                k.tr(v(pb, (slice(None), slice(i_ * 128, (i_ + 1) * 128))), fc(WB[nm]), v(identb), inc=(i_ == 3))
            if KSTOP == "R6":
                return
            import os as _os
            _v = _os.environ.get("KV", "")
            for i_, (nm, eng) in enumerate((("atokp", "act"), ("Vp", "act"), ("Bhp", "act"), ("Khp", "act"))):
                if _v == "act":
                    eng = "act"
                if _v == "dve":
                    eng = "dve"
                if _v == "one" and i_ > 0:
                    continue
                dcopy(eng, padz[nm], pb, i_ * 128)
            if KSTOP == "R7":
                return
            pf = k.psf()
            for h in range(2):
                k.mm(v(pf, (slice(None), slice(0, 128))), PADV(padz["atokp"], h), v(CH[1 + h]["XTb"]),
                     start=(h == 0), stop=(h == 1))
            k.cp("act", v(AxTr), v(pf, (slice(None), slice(0, 128))))

            if KSTOP in ("M4", "XM4"):
                return
            if nseq == 1 and KSTOP != "XS2":
                d = CH[0]
                pf = k.psf()
                k.mm(v(pf, (slice(None), slice(0, 128))), v(AxTg), v(Sgb), start=True, stop=False)
                k.mm(v(pf, (slice(None), slice(0, 128))), v(d["XTb"]), v(Vb_t), start=False, stop=True)
                k.cp("act", v(vnew), v(pf, (slice(None), slice(0, 128))))
                pf2 = k.psf()
                k.mm(v(pf2, (slice(None), slice(0, 128))), v(Sgb), fc(WB["qdT"]), start=True, stop=False)
                k.mm(v(pf2, (slice(None), slice(0, 128))), v(vnew), v(d["ArbT"]), start=False, stop=True)
                pf3 = k.psf()
                k.mm(v(pf3, (slice(None), slice(0, 128))), v(kd_t), v(vnew))
                k.stt("dve", v(Sg32), v(Sg32), v(glast, (slice(None), slice(c, c + 1))),
                      v(pf3, (slice(None), slice(0, 128))), ALU.mult, ALU.add)
                k.cp("act", v(Sgb), v(Sg32))
                k.cp("act", fc(W["yg"]), v(pf2, (slice(None), slice(0, 128))))
            if nseq == 1 and KSTOP != "XS1":
                pf = k.psf()
                k.mm(v(pf, (slice(None), slice(0, 128))), v(AxTr), v(Hrb), start=True, stop=False)
                for h in range(2):
                    k.mm(v(pf, (slice(None), slice(0, 128))), v(CH[1 + h]["M2T"]), PADV(padz["Vp"], h),
                         start=False, stop=(h == 1))
                dcopy("act", padz["Up"], pf, 0)
                pf2 = k.psf()
                k.mm(v(pf2, (slice(None), slice(0, 128))), v(Hrb), V(ARt, ARt.t[:, c, 1, :]), start=True, stop=False)
                for h in range(2):
                    k.mm(v(pf2, (slice(None), slice(0, 128))), PADV(padz["Up"], h), v(CH[1 + h]["ArbT"]),
                         start=False, stop=False)
                    k.mm(v(pf2, (slice(None), slice(0, 128))), PADV(padz["Vp"], h), v(CH[1 + h]["ArkT"]),
                         start=False, stop=(h == 1))
                for h in range(2):
                    k.mm(v(pf2, (slice(None), slice(128, 256))), PADV(padz["Bhp"], h), PADV(padz["Up"], h),
                         start=(h == 0), stop=False)
                    k.mm(v(pf2, (slice(None), slice(128, 256))), PADV(padz["Khp"], h), PADV(padz["Vp"], h),
                         start=False, stop=(h == 1))
                k.stt("dve", v(Hr32), v(Hr32), v(gend, (slice(None), slice(c, c + 1))),
                      v(pf2, (slice(None), slice(128, 256))), ALU.mult, ALU.add)
                k.cp("act", v(Hrb), v(Hr32))
                k.cp("act", fc(W["yr"]), v(pf2, (slice(None), slice(0, 128))))
            if nseq != 1:
                d = CH[0]
                for q in range(8):
                    k.tt("pool", v(mskA, (slice(None), q, slice(None))), v(AxTg), v(colm, (slice(None), q, slice(None))), ALU.mult)
                    k.tt("dve", v(mskQ, (slice(None), q, slice(None))), fc(WB["qdT"]), v(colm, (slice(None), q, slice(None))), ALU.mult)
                pf = k.psf()
                for q in range(8):
                    k.mm(v(pf, (slice(None), slice(0, 128))), v(mskA, (slice(None), q, slice(None))),
                         v(S0gb, (slice(None), q, slice(None))), start=(q == 0), stop=False)
                k.mm(v(pf, (slice(None), slice(0, 128))), v(d["XTb"]), v(Vb_t), start=False, stop=True)
                k.cp("act", v(vnew), v(pf, (slice(None), slice(0, 128))))
                pf2 = k.psf()
                for q in range(8):
                    k.mm(v(pf2, (slice(None), slice(0, 128))), v(S0gb, (slice(None), q, slice(None))),
                         v(mskQ, (slice(None), q, slice(None))), start=(q == 0), stop=False)
                k.mm(v(pf2, (slice(None), slice(0, 128))), v(vnew), v(d["ArbT"]), start=False, stop=True)
                k.cp("act", fc(W["yg"]), v(pf2, (slice(None), slice(0, 128))))
                for q in range(8):
                    k.ts("pool", v(mskK, (slice(None), q % 2, slice(0, 128))), v(kd_t), v(rowm, (slice(None), slice(q, q + 1))), ALU.mult)
                    pf3 = k.psf()
                    k.mm(v(pf3, (slice(None), slice(0, 128))), v(mskK, (slice(None), q % 2, slice(0, 128))), v(vnew))
                    k.stt("dve", v(Sfin, (slice(None), q % 2, slice(None))), v(S0g32, (slice(None), q, slice(None))),
                          v(glast, (slice(None), slice(q, q + 1))), v(pf3, (slice(None), slice(0, 128))),
                          ALU.mult, ALU.add)
                    k.dma("sp", V(ogs_s, ogs_s.t[q, :, :]), v(Sfin, (slice(None), q % 2, slice(None))))
                for q in range(8):
                    k.tt("pool", v(mskA, (slice(None), q, slice(None))), v(AxTr), v(colm, (slice(None), q, slice(None))), ALU.mult)
                    k.tt("dve", v(mskQ, (slice(None), q, slice(None))), V(ARt, ARt.t[:, c, 1, :]),
                         v(colm, (slice(None), q, slice(None))), ALU.mult)
                pf = k.psf()
                for q in range(8):
                    k.mm(v(pf, (slice(None), slice(0, 128))), v(mskA, (slice(None), q, slice(None))),
                         v(H0rb, (slice(None), q, slice(None))), start=(q == 0), stop=False)
                for h in range(2):
                    k.mm(v(pf, (slice(None), slice(0, 128))), v(CH[1 + h]["M2T"]), PADV(padz["Vp"], h),
                         start=False, stop=(h == 1))
                dcopy("act", padz["Up"], pf, 0)
                pf2 = k.psf()
                for q in range(8):
                    k.mm(v(pf2, (slice(None), slice(0, 128))), v(H0rb, (slice(None), q, slice(None))),
                         v(mskQ, (slice(None), q, slice(None))), start=(q == 0), stop=False)
                for h in range(2):
                    k.mm(v(pf2, (slice(None), slice(0, 128))), PADV(padz["Up"], h), v(CH[1 + h]["ArbT"]),
                         start=False, stop=False)
                    k.mm(v(pf2, (slice(None), slice(0, 128))), PADV(padz["Vp"], h), v(CH[1 + h]["ArkT"]),
                         start=False, stop=(h == 1))
                k.cp("act", fc(W["yr"]), v(pf2, (slice(None), slice(0, 128))))
                for q in range(8):
                    k.ts("pool", v(mskK, (slice(None), q % 2, slice(None))), v(padz["Bhp"], (slice(None), slice(0, 256))),
                         v(rowm, (slice(None), slice(q, q + 1))), ALU.mult)
                    k.ts("dve", v(mskK2, (slice(None), q % 2, slice(None))), v(padz["Khp"], (slice(None), slice(0, 256))),
                         v(rowm, (slice(None), slice(q, q + 1))), ALU.mult)
                    pf3 = k.psf()
                    for h in range(2):
                        k.mm(v(pf3, (slice(None), slice(0, 128))), v(mskK, (slice(None), q % 2, slice(h * 128, h * 128 + 128))),
                             PADV(padz["Up"], h), start=(h == 0), stop=False)
                        k.mm(v(pf3, (slice(None), slice(0, 128))), v(mskK2, (slice(None), q % 2, slice(h * 128, h * 128 + 128))),
                             PADV(padz["Vp"], h), start=False, stop=(h == 1))
                    k.stt("dve", v(Sfin, (slice(None), q % 2, slice(None))), v(H0r32, (slice(None), q, slice(None))),
                          v(gend, (slice(None), slice(q, q + 1))), v(pf3, (slice(None), slice(0, 128))),
                          ALU.mult, ALU.add)
                    pf4 = k.psf()
                    k.tr(v(pf4, (slice(None), slice(0, 128))), v(Sfin, (slice(None), q % 2, slice(None))), v(ident))
                    k.cp("act", v(HTt), v(pf4, (slice(None), slice(0, 128))))
                    for h in range(2):
                        k.dma("sp", V(ors_s, ors_s.t[q, h, :, :]),
                              v(HTt, (slice(64 * h, 64 * h + 64), slice(64 * h, 64 * h + 64))))

        if KSTOP in ("M5", "XM5", "XS1", "XS2"):
            return
        k.act(f(WB["sqb"]), f(W["yg"]), AF.Square)
        pf = k.psf()
        k.mm(f(pf), v(onesb), f(WB["sqb"]))
        k.rsqrt(f(W["t1"]), f(pf), 1.0 / 128.0, 1e-6)
        k.tt("dve", f(W["t0"]), f(W["yg"]), f(W["t1"]), ALU.mult)
        k.stt("dve", f(WB["ob"]), f(W["t0"]), PC(14), f(W["sz"]), ALU.mult, ALU.mult)
        pc_, pcol_ = tokcol0 // 2048, tokcol0 % 2048
        k.dma("sp", V(agin, agin.t[pc_, 0:128, pcol_:pcol_ + NT]), f(WB["ob"]))
        k.cp("pool", f(WB["sqb"]), f(W["yr"]))
        pf = k.psf()
        k.mm(f(pf), v(oblk), f(WB["sqb"]))
        k.stt("dve", f(W["yc"]), f(pf), -1.0 / 64.0, f(W["yr"]), ALU.mult, ALU.add)
        k.act(f(WB["sqb"]), f(W["yc"]), AF.Square)
        pf = k.psf()
        k.mm(f(pf), v(oblk), f(WB["sqb"]))
        k.rsqrt(f(W["t1"]), f(pf), 1.0 / 64.0, 64e-5)
        k.tt("dve", f(W["yc"]), f(W["yc"]), f(W["t1"]), ALU.mult)
        k.ts("dve", f(W["yc"]), f(W["yc"]), PC(26), ALU.mult, PC(27), ALU.add)
        k.tt("pool", f(W["t0"]), f(W["sr"]), f(W["krp"]), ALU.mult)
        k.ts("pool", f(WB["rk2"]), f(W["t0"]), PC(25), ALU.mult)
        pf = k.psf()
        k.mm(f(pf), v(oblk), f(WB["rk2"]))
        k.tt("dve", f(W["t0"]), f(pf), f(W["svr"]), ALU.mult)
        k.tt("dve", f(W["t0"]), f(W["t0"]), f(W["yc"]), ALU.add)
        k.tt("dve", f(WB["kbb"]), f(W["t0"]), f(W["gg"]), ALU.mult)
        k.dma("sp", V(agin, agin.t[pc_, 128:256, pcol_:pcol_ + NT]), f(WB["kbb"]))

        if last_out is not None:
            og, osh = last_out
            CO = COs[ncs]; CO2 = CO2s[ncs]
            for si, nm in enumerate(("q", "k", "v")):
                k.cp("pool", v(CO, (slice(None), si, slice(0, ncs), slice(None))),
                     pview(PB[nm][par], ncs, tcs, tcs - 3, tcs))
            for bi, nm in enumerate(("r", "kr", "vr", "wa", "g0", "g1")):
                k.cp("pool", v(CO2, (slice(None), bi, slice(0, ncs))),
                     V(PB[nm][par], PB[nm][par].t[:, 0:ncs * (3 + tcs)].rearrange("p (s t) -> p s t", t=3 + tcs)[:, :, 3 + tcs - 1]))
            if KSTOP == "L1":
                return
            n1 = 9 * ncs
            pf = k.psf()
            k.mm(v(pf, (slice(0, n1), slice(0, 128))),
                 V(CO, CO.t[:, :, :, :].rearrange("p a s t -> p (a s t)")), v(ident))
            k.cp("act", v(COt, (slice(0, n1), slice(None))), v(pf, (slice(0, n1), slice(0, 128))))
            if KSTOP == "L2":
                return
            k.dma("sp", v(og), v(COt, (slice(0, n1), slice(None))))
            if KSTOP == "L3":
                return
            n2 = 6 * ncs
            pf = k.psf()
            k.mm(v(pf, (slice(0, n2), slice(0, 128))),
                 V(CO2, CO2.t[:, :, :].rearrange("p a s -> p (a s)")), v(ident))
            k.cp("act", v(COt, (slice(0, n2), slice(None))), v(pf, (slice(0, n2), slice(0, 128))))
            k.dma("sp", v(osh), v(COt, (slice(0, n2), slice(None))))

    import os
    KSTOP = os.environ.get("KSTOP", "")
    KNB = int(os.environ.get("KNB", NBLK_A))
    KSK = int(os.environ.get("KSKIP", -1))
    for blk in range(KNB):
        if blk == KSK:
            continue
        par = 0
        if blk > 0:
            for nm in PB:
                k.cp("pool", v(PB[nm][0], (slice(None), slice(0, 3))), v(PB[nm][0], (slice(None), slice(512, 515))))
        in_proj_block((blk * 512) if not (os.environ.get("KREP", "") and blk == 15) else 0, 512, par, 1, 512, None)
        if KSTOP == "IP":
            S.finish("sp"); stA.close(); k.st.close(); return nc
        mixer_block(512, par, 1, 512, 4, 128, 0, 1, blk * 512,
                    (og_p, osh_p) if (blk == NBLK_A - 1 and KSTOP != "NOLO") else None)
        if blk % 4 == 3 and not KSTOP:
            exchange(blk // 4)
        if KSTOP[:1] in ("M", "R") or (KSTOP[:1] in ("L", "N", "X") and blk == NBLK_A - 1):
            if KDBG:
                S.wait_all_dma("sp")
                k.dma("sp", v(dbg), v(agin))
            S.finish("sp"); stA.close(); k.st.close(); return nc
    k.dma("sp", V(ogs_p, ogs_p.t[0, :, :]), v(Sg32))
    pf = k.psf()
    k.tr(v(pf, (slice(None), slice(0, 128))), v(Hr32), v(ident))
    k.cp("act", v(HTt), v(pf, (slice(None), slice(0, 128))))
    for h in range(2):
        k.dma("sp", V(ors_p, ors_p.t[0, h, :, :]), v(HTt, (slice(64 * h, 64 * h + 64), slice(64 * h, 64 * h + 64))))

    if KSTOP == "A1":
        if KDBG:
            S.wait_all_dma("sp")
            k.dma("sp", v(dbg), v(agin))
        S.finish("sp"); stA.close(); k.st.close(); return nc
    par = 0
    for si, nm in enumerate(("q", "k", "v")):
        P = PB[nm][par]
        k.dma("sp", V(P, P.t[:, 0:8 * 19].rearrange("p (s t) -> p s t", t=19)[:, :, 0:3]), v(sgc, (slice(None), si, slice(None), slice(None))))
    for bi, nm in enumerate(("r", "kr", "vr", "wa", "g0", "g1")):
        P = PB[nm][par]
        k.dma("sp", V(P, P.t[:, 0:8 * 19].rearrange("p (s t) -> p s t", t=19)[:, :, 2]), v(ssh, (slice(None), bi, slice(None))),
              allow_slow_non_contiguous=True)
    k.dma("sp", v(S0g32), v(sgs))
    k.cp("act", v(S0gb), v(S0g32))
    k.memset("pool", v(H0r32), 0.0)
    for h in range(2):
        k.dma("sp", v(H0r32, (slice(64 * h, 64 * h + 64), slice(None), slice(64 * h, 64 * h + 64))),
              v(srs, (slice(64 * h, 64 * h + 64), slice(None), slice(None))))
    k.cp("act", v(H0rb), v(H0r32))
    in_proj_block(SEQ, 128, par, 8, 16, None)
    if KSTOP == "A2":
        S.finish("sp"); stA.close(); k.st.close(); return nc
    mixer_block(128, par, 8, 16, 8, 16, 1, 8, SEQ, (og_s, osh_s))
    if KSTOP == "A3":
        S.finish("sp"); stA.close(); k.st.close(); return nc

    exchange(4)
    for en in ("pool", "sp"):
        S.engs[en]["e"].wait_ge(cc_sem, cc_n[0])
    stA.close()
    if KSTOP == "AG":
        S.finish("sp"); k.st.close(); return nc

    stB = ExitStack()
    Bt = lambda name, shape, dt=F32: k.sb(name, shape, dt, stack=stB)
    woutb = Bt("woutb", [128, 8, D], BF16)
    wupb = Bt("wupb", [128, 8, 2 * FF], BF16)
    wdnb = Bt("wdnb", [128, NFC, D], BF16)
    wo_v = wout.t[:, :].rearrange("(kc p) c -> p kc c", p=128)
    wu_v = wup.t[:, :].rearrange("(kc p) c -> p kc c", p=128)
    wd_v = wdn.t[:, :].rearrange("(fc p) c -> p fc c", p=128)
    for kc in range(8):
        S.dma("pool", woutb.t[:, kc, :], wo_v[:, kc, :], reads=[wout], writes=[woutb])
    for kc in range(8):
        for hf in range(4):
            S.dma("pool", wupb.t[:, kc, hf * 1408:(hf + 1) * 1408], wu_v[:, kc, hf * 1408:(hf + 1) * 1408],
                  reads=[wup], writes=[wupb])
    for fc_ in range(NFC):
        S.dma("pool", wdnb.t[:, fc_, :], wd_v[:, fc_, :], reads=[wdn], writes=[wdnb])
    fcwt = Bt("fcwt", [128, NFC, 3])
    k.dma("sp", v(fcwt), v(fcw))
    SH = Bt("SH", [128, NFC, 2, 2])
    k.dma("sp", v(SH), v(sfc))
    GH = Bt("GH", [128, NFC, 2])
    FO = Bt("FO", [128, NFC, 4])
    FOt = Bt("FOt", [4, 2, 128])
    x1 = [Bt("x1_%d" % i, [128, D]) for i in range(2)]
    h2b = Bt("h2b", [128, D], BF16)
    ssB = Bt("ssB", [128, 2])
    h2T = [Bt("h2T%d" % i, [128, 8, 128], BF16) for i in range(2)]
    oTb = [Bt("oTb%d" % i, [128, 8, 256], BF16) for i in range(2)]
    Ab = Bt("Ab", [128, NFC, 128], BF16)
    Gt = [Bt("Gt%d" % i, [128, 2 * 18 + 100]) for i in range(3)]
    Gc = [Bt("Gc%d" % i, [128, 128]) for i in range(2)]

    regs = []
    for i in range(3):
        r = nc.sync.alloc_register("off%d" % i)
        nc.sync.reg_load(r, offs.t[0:1, i:i + 1])
        regs.append(r)
    toff = nc.sync.snap(regs[0], min_val=0, max_val=3)
    hoff = nc.sync.snap(regs[1], min_val=0, max_val=3)
    soff = nc.sync.snap(regs[2], min_val=0, max_val=96)

    def load_oT(oT, prow, c0, n, dyn_col=None):
        src = agout.t[bass.ds(prow, 1), :, :] if not isinstance(prow, int) else agout.t[prow:prow + 1, :, :]
        if dyn_col is None:
            src = src[:, :, c0:c0 + n]
        else:
            src = src[:, :, c0:2048][:, :, bass.ds(dyn_col, n)]
        S.dma("sp", oT.t[:, :, 0:n], src.rearrange("o (ch p) t -> p (o ch) t", p=128), reads=[agout], writes=[oT])

    def ffn_block(bi, xrow0, mt, ncs, tcs, oT, ocol, mode, yrow0):
        xb_ = x1[bi % 2]; hT2 = h2T[bi % 2]
        k.dma("sp", v(xb_, (slice(0, mt), slice(None))), v(xB, (slice(xrow0, xrow0 + mt), slice(None))))
        for hf in range(2):
            pf = k.psf()
            for ch in range(8):
                k.mm(v(pf, (slice(0, mt), slice(None))), v(oT, (slice(None), ch, slice(ocol, ocol + mt))),
                     v(woutb, (slice(None), ch, slice(hf * 512, (hf + 1) * 512))), start=(ch == 0), stop=(ch == 7))
            k.tt("dve", v(xb_, (slice(0, mt), slice(hf * 512, (hf + 1) * 512))),
                 v(xb_, (slice(0, mt), slice(hf * 512, (hf + 1) * 512))), v(pf, (slice(0, mt), slice(None))), ALU.add)
        sc = v(ssB, (slice(0, mt), slice(bi % 2, bi % 2 + 1)))
        k.act(v(h2b, (slice(0, mt), slice(None))), v(xb_, (slice(0, mt), slice(None))), AF.Square, scale=1.0 / 32.0, accum=sc)
        k.rsqrt(sc, sc, 1.0, 1e-6)
        k.stt("dve", v(h2b, (slice(0, mt), slice(None))), v(xb_, (slice(0, mt), slice(None))), sc,
              v(nbc, (slice(0, mt), 1, slice(None))), ALU.mult, ALU.mult)
        if mt == 128:
            pb = k.psb()
            for kc in range(8):
                k.tr(v(pb, (slice(None), slice(kc * 128, (kc + 1) * 128))), v(h2b, (slice(None), slice(kc * 128, (kc + 1) * 128))),
                     v(identb), inc=(kc == 7))
            k.cp("act", v(hT2), V(pb, pb.t[:, :].rearrange("p (kc t) -> p kc t", t=128)))
        else:
            pq = k.psf()
            for kc in range(8):
                k.mm(v(pq, (slice(None), slice(kc * mt, (kc + 1) * mt))), v(h2b, (slice(0, mt), slice(kc * 128, (kc + 1) * 128))),
                     v(identb, (slice(0, mt), slice(0, mt))), inc=(kc == 7))
            k.cp("act", v(hT2, (slice(None), slice(None), slice(0, mt))),
                 V(pq, pq.t[:, 0:8 * mt].rearrange("p (kc t) -> p kc t", t=mt)))
        W_ = 2 + tcs
        for fc_ in range(NFC):
            pf = k.psf()
            for kc in range(8):
                k.mm(v(pf, (slice(None), slice(0, mt))), v(wupb, (slice(None), kc, slice(fc_ * 128, (fc_ + 1) * 128))),
                     v(hT2, (slice(None), kc, slice(0, mt))), start=(kc == 0), stop=(kc == 7))
            if mode == "halo":
                k.ts("dve", v(GH, (slice(None), fc_, slice(None))), v(pf, (slice(None), slice(0, 2))), v(flg), ALU.mult)
                continue
            for kc in range(8):
                k.mm(v(pf, (slice(None), slice(256, 256 + mt))),
                     v(wupb, (slice(None), kc, slice(FF + fc_ * 128, FF + (fc_ + 1) * 128))),
                     v(hT2, (slice(None), kc, slice(0, mt))), start=(kc == 0), stop=(kc == 7))
            G = Gt[fc_ % 3]
            g3 = V(G, G.t[:, 0:ncs * W_].rearrange("p (s t) -> p s t", t=W_))
            k.cp("act", V(G, g3.ap[:, :, 2:W_]), V(pf, pf.t[:, 0:mt].rearrange("p (s t) -> p s t", t=tcs)))
            if mode == "prompt":
                k.cp("pool", v(G, (slice(None), slice(0, 2))), v(GH, (slice(None), fc_, slice(None))))
            else:
                k.cp("pool", V(G, g3.ap[:, :, 0:2]), v(SH, (slice(None), fc_, slice(None), slice(None))))
            gc = Gc[fc_ % 2]
            gc3 = V(gc, gc.t[:, 0:mt].rearrange("p (s t) -> p s t", t=tcs))
            k.ts("dve", gc3, V(G, g3.ap[:, :, 0:tcs]), v(fcwt, (slice(None), fc_, slice(0, 1))), ALU.mult)
            k.stt("dve", gc3, V(G, g3.ap[:, :, 1:1 + tcs]), v(fcwt, (slice(None), fc_, slice(1, 2))), gc3, ALU.mult, ALU.add)
            k.stt("dve", gc3, V(G, g3.ap[:, :, 2:2 + tcs]), v(fcwt, (slice(None), fc_, slice(2, 3))), gc3, ALU.mult, ALU.add)
            k.act(v(gc, (slice(None), slice(0, mt))), v(gc, (slice(None), slice(0, mt))), AF.Silu)
            k.tt("dve", v(Ab, (slice(None), fc_, slice(0, mt))), v(gc, (slice(None), slice(0, mt))),
                 v(pf, (slice(None), slice(256, 256 + mt))), ALU.mult)
            if mode == "prompt":
                k.cp("pool", v(GH, (slice(None), fc_, slice(None))), v(G, (slice(None), slice(tcs, tcs + 2))))
                k.cp("pool", v(FO, (slice(None), fc_, slice(0, 2))), v(G, (slice(None), slice(tcs, tcs + 2))))
            else:
                k.cp("pool", V(FO, FO.t[:, fc_, :].rearrange("p (s t) -> p s t", t=2)), V(G, g3.ap[:, :, tcs:tcs + 2]))
        if mode == "halo":
            return
        for hf in range(2):
            pf = k.psf()
            for fc_ in range(NFC):
                k.mm(v(pf, (slice(0, mt), slice(None))), v(Ab, (slice(None), fc_, slice(0, mt))),
                     v(wdnb, (slice(None), fc_, slice(hf * 512, (hf + 1) * 512))), start=(fc_ == 0), stop=(fc_ == NFC - 1))
            k.tt("dve", v(xb_, (slice(0, mt), slice(hf * 512, (hf + 1) * 512))),
                 v(xb_, (slice(0, mt), slice(hf * 512, (hf + 1) * 512))), v(pf, (slice(0, mt), slice(None))), ALU.add)
        k.act(v(h2b, (slice(0, mt), slice(None))), v(xb_, (slice(0, mt), slice(None))), AF.Square, scale=1.0 / 32.0, accum=sc)
        k.rsqrt(sc, sc, 1.0, 1e-6)
        k.stt("dve", v(xb_, (slice(0, mt), slice(None))), v(xb_, (slice(0, mt), slice(None))), sc,
              v(nbc, (slice(0, mt), 2, slice(None))), ALU.mult, ALU.mult)
        k.dma("sp", v(yB, (slice(yrow0, yrow0 + mt), slice(None))), v(xb_, (slice(0, mt), slice(None))))

    def ffn_conv_out(dst, n):
        for fc_ in range(NFC):
            pf = k.psf()
            k.mm(v(pf, (slice(0, n), slice(0, 128))), v(FO, (slice(None), fc_, slice(0, n))), v(ident))
            k.cp("act", v(FOt, (slice(0, n), fc_ % 2, slice(None))), v(pf, (slice(0, n), slice(0, 128))))
            k.dma("sp", v(dst, (slice(0, n), slice(fc_ * 128, (fc_ + 1) * 128))), v(FOt, (slice(0, n), fc_ % 2, slice(None))))

    load_oT(oTb[1], hoff, 2046, 2)
    ffn_block(0, 0, 2, 1, 2, oTb[1], 0, "halo", 0)
    for t_ in range(NTB // 128):
        ob_ = oTb[(t_ // 2) % 2]
        if t_ % 2 == 0:
            load_oT(ob_, toff, t_ * 128, 256)
        ffn_block(1 + t_, 2 + t_ * 128, 128, 1, 128, ob_, (t_ % 2) * 128, "prompt", t_ * 128)
    ffn_conv_out(ofc_p, 2)
    load_oT(oTb[0], 4, 0, 32, dyn_col=soff)
    ffn_block(17, 2 + NTB, 32, 2, 16, oTb[0], 0, "sample", NTB)
    ffn_conv_out(ofc_s, 4)

    S.finish("sp")
    stB.close()
    k.st.close()
    return nc


_CACHE = {}


def kernel(**inp):
    f = lambda a: np.ascontiguousarray(np.asarray(a, dtype=np.float32))
    xp = f(inp["x_prompt"]); xs = f(inp["x_sample"])
    w_in = f(inp["w_in"])[0]
    GP = 2056
    in_maps = []
    for c in range(NCORES):
        b, j = c // 4, c % 4
        m = {}
        m["xA"] = np.concatenate([xp[b], xs[8 * b:8 * b + 8].reshape(128, D)], 0)
        hr = xp[b, NTB * j - 2:NTB * j] if j > 0 else xp[b, 0:2]
        m["xB"] = np.concatenate([hr, xp[b, NTB * j:NTB * (j + 1)], xs[2 * c:2 * c + 2].reshape(32, D)], 0)
        cols = []
        for sec in range(4):
            cols.append(np.arange(sec * 512 + j * 128, sec * 512 + (j + 1) * 128))
        for sec in range(3):
            cols.append(GP + np.arange(sec * 512 + j * 128, sec * 512 + (j + 1) * 128))
        cols.append(GP + np.arange(1536, 1536 + 288))
        cols.append(np.array([2048 + j, 2052 + j]))
        cols = np.concatenate(cols)
        m["win"] = np.ascontiguousarray(w_in[:, cols])
        pp = np.empty((128, 40), np.float32)
        cw = f(inp["gdn_conv_w"])[0]
        for sec in range(3):
            for tap in range(4):
                pp[:, sec * 4 + tap] = cw[tap, sec * 512 + j * 128: sec * 512 + (j + 1) * 128]
        pp[:, 12] = f(inp["gdn_a_log"])[0, j]
        pp[:, 13] = f(inp["gdn_dt_bias"])[0, j]
        pp[:, 14] = f(inp["gdn_norm"])[0]
        mu = f(inp["rwkv_mu"])[0]
        for sec in range(3):
            pp[:, 15 + sec] = mu[sec * 512 + j * 128: sec * 512 + (j + 1) * 128]
        pp[:, 18] = mu[1536:1664]
        pp[:, 19] = mu[1664:1792]
        pp[:, 20] = np.tile(mu[1792:1824], 4)
        sl = slice(j * 128, (j + 1) * 128)
        pp[:, 21] = f(inp["rwkv_w0"])[0, sl]
        pp[:, 22] = f(inp["rwkv_a0"])[0, sl]
        pp[:, 23] = f(inp["rwkv_k_k"])[0, sl]
        pp[:, 24] = f(inp["rwkv_k_a"])[0, sl]
        pp[:, 25] = f(inp["rwkv_r_k"])[0].reshape(512)[sl]
        pp[:, 26] = f(inp["rwkv_ln_w"])[0, sl]
        pp[:, 27] = f(inp["rwkv_ln_b"])[0, sl]
        pp[:, 28:] = pp[:, 0:12]
        m["ppar"] = pp
        m["wa2"] = np.ascontiguousarray(np.concatenate([f(inp["rwkv_w2"])[0][:, sl], f(inp["rwkv_a2"])[0][:, sl]], 0))
        g2 = f(inp["rwkv_g2"])[0][:, sl]
        m["g2a"] = np.ascontiguousarray(g2[0:128]); m["g2b"] = np.ascontiguousarray(g2[128:160])
        m["nrm"] = np.stack([f(inp["norm_mix"])[0], f(inp["norm_ffn"])[0], f(inp["norm_final"])], 0)
        wo = f(inp["w_out"])[0]
        rows = np.concatenate([np.concatenate([np.arange(jr * 128, (jr + 1) * 128),
                                               512 + np.arange(jr * 128, (jr + 1) * 128)]) for jr in range(4)])
        m["wout"] = np.ascontiguousarray(wo[rows])
        m["wup"] = f(inp["w_up"])[0]; m["wdn"] = f(inp["w_down"])[0]
        seqs = slice(8 * b, 8 * b + 8)
        gc = f(inp["state_gdn_conv"])[0, seqs]
        m["sgc"] = np.ascontiguousarray(np.stack([gc[:, :, sec * 512 + j * 128: sec * 512 + (j + 1) * 128]
                                                  for sec in range(3)], 0).transpose(3, 0, 1, 2))
        sh = f(inp["state_rwkv_shift"])[0, seqs, 0]
        blks = [sh[:, sec * 512 + j * 128: sec * 512 + (j + 1) * 128] for sec in range(3)]
        blks += [sh[:, 1536:1664], sh[:, 1664:1792], np.tile(sh[:, 1792:1824], (1, 4))]
        m["ssh"] = np.ascontiguousarray(np.stack(blks, 0).transpose(2, 0, 1))
        m["sgs"] = np.ascontiguousarray(f(inp["state_gdn"])[0, seqs, j].transpose(1, 0, 2))
        rs = f(inp["state_rwkv"])[0, seqs, 2 * j:2 * j + 2]
        m["srs"] = np.ascontiguousarray(rs.transpose(1, 3, 0, 2).reshape(128, 8, 64))
        fs = f(inp["state_ffn_conv"])[0, 2 * c:2 * c + 2]
        m["sfc"] = np.ascontiguousarray(fs.reshape(2, 2, NFC, 128).transpose(3, 2, 0, 1))
        m["fcw"] = np.ascontiguousarray(f(inp["ffn_conv_w"])[0].reshape(3, NFC, 128).transpose(2, 1, 0))
        m["offs"] = np.array([[j, max(j - 1, 0), 32 * j, 0]], np.int32)
        m["flag"] = np.full((128, 1), 1.0 if j > 0 else 0.0, np.float32)
        in_maps.append(m)
    if "nc" not in _CACHE:
        _CACHE["nc"] = build()
    res = run_bass_kernel_spmd(_CACHE["nc"], in_maps, core_ids=list(range(NCORES)))
    R = res.results
    y_p = np.empty((2, SEQ, D), np.float32); y_s = np.empty((NSAMP, TS, D), np.float32)
    gcp = np.empty((1, 2, 3, 1536), np.float32); gcs = np.empty((1, NSAMP, 3, 1536), np.float32)
    gsp = np.empty((1, 2, 4, 128, 128), np.float32); gss = np.empty((1, NSAMP, 4, 128, 128), np.float32)
    shp = np.empty((1, 2, 1, 1824), np.float32); shs = np.empty((1, NSAMP, 1, 1824), np.float32)
    rsp = np.empty((1, 2, 8, 64, 64), np.float32); rss = np.empty((1, NSAMP, 8, 64, 64), np.float32)
    fcp = np.empty((1, 2, 2, FF), np.float32); fcs = np.empty((1, NSAMP, 2, FF), np.float32)
    for c in range(NCORES):
        b, j = c // 4, c % 4
        r = R[c]
        y = np.asarray(r["yB"])
        y_p[b, NTB * j:NTB * (j + 1)] = y[0:NTB]
        y_s[2 * c:2 * c + 2] = y[NTB:NTB + 32].reshape(2, TS, D)
        og = np.asarray(r["og_p"]).reshape(3, 3, 128)
        ogs_ = np.asarray(r["og_s"]).reshape(3, 8, 3, 128)
        for sec in range(3):
            cs = slice(sec * 512 + j * 128, sec * 512 + (j + 1) * 128)
            gcp[0, b, :, cs] = og[sec]
            gcs[0, 8 * b:8 * b + 8, :, cs] = ogs_[sec]
        gsp[0, b, j] = np.asarray(r["ogs_p"])[0]
        gss[0, 8 * b:8 * b + 8, j] = np.asarray(r["ogs_s"])
        osh = np.asarray(r["osh_p"])
        oshs = np.asarray(r["osh_s"]).reshape(6, 8, 128)
        for sec in range(3):
            cs = slice(sec * 512 + j * 128, sec * 512 + (j + 1) * 128)
            shp[0, b, 0, cs] = osh[sec]
            shs[0, 8 * b:8 * b + 8, 0, cs] = oshs[sec]
        if j == 0:
            shp[0, b, 0, 1536:1664] = osh[3]; shp[0, b, 0, 1664:1792] = osh[4]; shp[0, b, 0, 1792:1824] = osh[5, 0:32]
            shs[0, 8 * b:8 * b + 8, 0, 1536:1664] = oshs[3]; shs[0, 8 * b:8 * b + 8, 0, 1664:1792] = oshs[4]
            shs[0, 8 * b:8 * b + 8, 0, 1792:1824] = oshs[5][:, 0:32]
        rsp[0, b, 2 * j:2 * j + 2] = np.asarray(r["ors_p"])[0]
        rss[0, 8 * b:8 * b + 8, 2 * j:2 * j + 2] = np.asarray(r["ors_s"])
        if j == 3:
            fcp[0, b] = np.asarray(r["ofc_p"])
        fcs[0, 2 * c:2 * c + 2] = np.asarray(r["ofc_s"]).reshape(2, 2, FF)
    return (y_p, y_s, gcp, gcs, gsp, gss, shp, shs, rsp, rss, fcp, fcs)
```

```python
from contextlib import ExitStack
import numpy as np
import concourse.bass as bass
import concourse.mybir as mybir
from concourse.bass_utils import run_bass_kernel_spmd

F32 = mybir.dt.float32
BF16 = mybir.dt.bfloat16
I32 = mybir.dt.int32
AF = mybir.ActivationFunctionType
ALU = mybir.AluOpType

NCORES = 8
D = 1024
SEQ = 8192
NSAMP = 16
TS = 16
NTA = SEQ + 128
NBLK_A = SEQ // 512
NTB = 2048
FF = 2816
NFC = 22
C0 = -0.6065306597126334


class Buf:
    __slots__ = ("t", "w", "r", "name", "track", "excl", "small")

    def __init__(self, t, name="", track=True):
        self.excl = False
        self.small = True
        self.t = t
        self.w = None
        self.r = []
        self.name = name
        self.track = track


class V:
    __slots__ = ("b", "ap")

    def __init__(self, b, ap):
        self.b = b
        self.ap = ap


def v(buf, *idx):
    if not idx:
        return V(buf, buf.t[:])
    if len(idx) == 1:
        return V(buf, buf.t[idx[0]])
    return V(buf, buf.t[idx])


class Sync:
    def __init__(self, nc, n_dma_sems=14, self_sync=True):
        self.nc = nc
        self.stack = ExitStack()
        self.self_sync = self_sync
        self.engs = {}
        for nm, e in (("pe", nc.tensor), ("act", nc.scalar), ("dve", nc.vector),
                      ("pool", nc.gpsimd), ("sp", nc.sync)):
            sem = self.stack.enter_context(nc.semaphore("s_" + nm))
            self.engs[nm] = dict(e=e, sem=sem, cnt=0, waited={}, name=nm)
        self.dma_sems = []
        for i in range(n_dma_sems):
            sem = self.stack.enter_context(nc.semaphore("s_dma%d" % i))
            self.dma_sems.append(dict(sem=sem, cnt=0, name="dma%d" % i))
        self.dma_rr = {"hw": 0, "sw": 0}
        self.sems = {k: e["sem"] for k, e in self.engs.items()}
        for d in self.dma_sems:
            self.sems[d["name"]] = d["sem"]
        self.n_ins = 0
        self.n_wait = 0

    def _wait(self, E, tok, small=True):
        key, val = tok
        if E["waited"].get(key, 0) >= val:
            return
        if key == E["name"] and (not self.self_sync or key == "pe" or not small):
            return
        E["e"].wait_ge(self.sems[key], val)
        E["waited"][key] = val
        self.n_wait += 1

    def _deps(self, E, reads, writes):
        reads = [b for b in reads if b.track]
        writes = [b for b in writes if b.track]
        for b in reads:
            if b.w is not None:
                self._wait(E, b.w, b.small)
        for b in writes:
            if b.w is not None:
                self._wait(E, b.w, b.small)
            for tok in b.r:
                self._wait(E, tok, b.small)

    def _mark(self, tok, reads, writes):
        reads = [b for b in reads if b.track]
        writes = [b for b in writes if b.track]
        for b in reads:
            if len(b.r) > 24:
                best = {}
                for k, val in b.r:
                    if best.get(k, 0) < val:
                        best[k] = val
                b.r = list(best.items())
            b.r.append(tok)
        for b in writes:
            b.w = tok
            b.r = []

    def op(self, eng, fn, reads=(), writes=(), inc=True):
        E = self.engs[eng]
        if eng != "pe":
            ex = [b for b in reads if b.excl]
            if ex:
                writes = list(writes) + ex
        self._deps(E, reads, writes)
        ins = fn(E["e"])
        self.n_ins += 1
        if inc:
            E["cnt"] += 1
            ins.then_inc(E["sem"], 1)
            tok = (eng, E["cnt"])
        else:
            tok = (eng, E["cnt"] + 1)
        self._mark(tok, reads, writes)
        return ins

    def dma(self, eng, out_ap, in_ap, reads=(), writes=(), **kw):
        E = self.engs[eng]
        self._deps(E, reads, writes)
        kind = "sw" if eng == "pool" else "hw"
        pool_ = self.dma_sems[0:4] if kind == "sw" else self.dma_sems[4:]
        d = pool_[self.dma_rr[kind] % len(pool_)]
        self.dma_rr[kind] += 1
        if d["cnt"] > 0:
            self._wait(E, (d["name"], d["cnt"]))
        ins = E["e"].dma_start(out=out_ap, in_=in_ap, **kw)
        d["cnt"] += 16
        ins.then_inc(d["sem"], 16)
        tok = (d["name"], d["cnt"])
        self.n_ins += 1
        self._mark(tok, reads, writes)
        return ins

    def wait_all_dma(self, eng):
        E = self.engs[eng]
        for d in self.dma_sems:
            if d["cnt"] > 0:
                self._wait(E, (d["name"], d["cnt"]))

    def finish(self, eng="sp"):
        E = self.engs[eng]
        for d in self.dma_sems:
            if d["cnt"] > 0:
                self._wait(E, (d["name"], d["cnt"]))
        for k, e in self.engs.items():
            if k != eng and e["cnt"] > 0:
                self._wait(E, (k, e["cnt"]))


class K:
    def __init__(self, nc):
        self.nc = nc
        self.S = Sync(nc)
        self.st = self.S.stack
        self.psf_i = 0
        self.psb_i = 0

    def sb(self, name, shape, dtype=F32, stack=None):
        t = (stack or self.st).enter_context(self.nc.sbuf_tensor(name, list(shape), dtype))
        b = Buf(t, name)
        b.small = int(np.prod(shape[1:])) < 128
        return b

    def ps(self, name, shape, dtype=F32):
        t = self.st.enter_context(self.nc.psum_tensor(name, list(shape), dtype))
        b = Buf(t, name)
        b.excl = True
        return b

    @staticmethod
    def _sc(x):
        return x.ap if isinstance(x, V) else x

    @staticmethod
    def _rb(*xs):
        return [x.b for x in xs if isinstance(x, V)]

    def tt(self, eng, out, a, b, op):
        self.S.op(eng, lambda e: e.tensor_tensor(out=out.ap, in0=a.ap, in1=b.ap, op=op),
                  reads=self._rb(a, b), writes=[out.b])

    def ts(self, eng, out, a, s1, op0, s2=None, op1=None):
        kw = {}
        if op1 is not None:
            kw["op1"] = op1
        self.S.op(eng, lambda e: e.tensor_scalar(out=out.ap, in0=a.ap, scalar1=self._sc(s1),
                                                 scalar2=self._sc(s2), op0=op0, **kw),
                  reads=self._rb(a, s1, s2), writes=[out.b])

    def stt(self, eng, out, a, s, b, op0, op1):
        self.S.op(eng, lambda e: e.scalar_tensor_tensor(out=out.ap, in0=a.ap, scalar=self._sc(s),
                                                        in1=b.ap, op0=op0, op1=op1),
                  reads=self._rb(a, s, b), writes=[out.b])

    def act(self, out, a, func, bias=None, scale=None, accum=None):
        kw = {}
        if bias is not None:
            kw["bias"] = self._sc(bias)
        if scale is not None:
            kw["scale"] = self._sc(scale)
        if accum is not None:
            kw["accum_out"] = accum.ap
        w = [out.b] + ([accum.b] if accum is not None else [])
        self.S.op("act", lambda e: e.activation(out=out.ap, in_=a.ap, func=func, **kw),
                  reads=self._rb(a, bias, scale), writes=w)

    def rsqrt(self, out, a, scale, eps):
        npart = out.ap.shape[0]
        eb, ec = self.epsv[eps]
        self.act(out, a, AF.Sqrt, bias=V(eb, eb.t[0:npart, ec:ec + 1]), scale=scale)
        self.S.op("dve", lambda e: e.reciprocal(out=out.ap, in_=out.ap), reads=[out.b], writes=[out.b])

    def cp(self, eng, out, a):
        if eng == "act":
            self.S.op("act", lambda e: e.copy(out=out.ap, in_=a.ap), reads=[a.b], writes=[out.b])
        else:
            self.S.op(eng, lambda e: e.tensor_copy(out=out.ap, in_=a.ap), reads=[a.b], writes=[out.b])

    def memset(self, eng, out, val):
        self.S.op(eng, lambda e: e.memset(out.ap, val), writes=[out.b])

    def asel(self, out, a, pattern, cmp, base, cm, fill=0.0):
        self.S.op("pool", lambda e: e.affine_select(out=out.ap, in_=a.ap, pattern=pattern, compare_op=cmp,
                                                    fill=fill, base=base, channel_multiplier=cm),
                  reads=[a.b], writes=[out.b])

    def mm(self, out, lhsT, rhs, start=True, stop=True, inc=None):
        if inc is None:
            inc = stop
        self.S.op("pe", lambda e: e.matmul(out.ap, lhsT.ap, rhs.ap, start=start, stop=stop),
                  reads=[lhsT.b, rhs.b], writes=[out.b], inc=inc)

    def tr(self, out, a, ident, inc=True):
        self.S.op("pe", lambda e: e.transpose(out.ap, a.ap, ident.ap), reads=[a.b, ident.b],
                  writes=[out.b], inc=inc)

    def scan(self, out, ones, a):
        self.S.op("dve", lambda e: e.tensor_tensor_scan(out=out.ap, data0=ones.ap, data1=a.ap, initial=0.0,
                                                       op0=ALU.mult, op1=ALU.add),
                  reads=[ones.b, a.b], writes=[out.b])

    def dma(self, eng, out, a, **kw):
        self.S.dma(eng, out.ap, a.ap, reads=[a.b], writes=[out.b], **kw)

    def psf(self):
        b = self.PSF[self.psf_i % len(self.PSF)]
        self.psf_i += 1
        return b

    def psb(self):
        b = self.PSB[self.psb_i % len(self.PSB)]
        self.psb_i += 1
        return b


def build():
    nc = bass.Bass("TRN2", target_bir_lowering=False)
    k = K(nc)
    S = k.S

    def din(name, shape, dt=F32):
        return Buf(nc.dram_tensor(name, list(shape), dt, kind="ExternalInput"), name, track=False)

    def dout(name, shape, dt=F32):
        return Buf(nc.dram_tensor(name, list(shape), dt, kind="ExternalOutput"), name, track=False)

    xA = din("xA", [NTA, D])
    xB = din("xB", [2 + NTB + 32, D])
    win = din("win", [D, 1186])
    ppar = din("ppar", [128, 40])
    wa2 = din("wa2", [128, 128])
    g2a = din("g2a", [128, 128])
    g2b = din("g2b", [32, 128])
    nrm = din("nrm", [3, D])
    wout = din("wout", [D, D])
    wup = din("wup", [D, 2 * FF])
    wdn = din("wdn", [FF, D])
    sgc = din("sgc", [128, 3, 8, 3])
    ssh = din("ssh", [128, 6, 8])
    sgs = din("sgs", [128, 8, 128])
    srs = din("srs", [128, 8, 64])
    sfc = din("sfc", [128, NFC, 2, 2])
    fcw = din("fcw", [128, NFC, 3])
    offs = din("offs", [1, 4], I32)
    flag = din("flag", [128, 1])

    yB = dout("yB", [NTB + 32, D])
    og_p = dout("og_p", [9, 128])
    og_s = dout("og_s", [72, 128])
    osh_p = dout("osh_p", [6, 128])
    osh_s = dout("osh_s", [48, 128])
    ogs_p = dout("ogs_p", [1, 128, 128])
    ogs_s = dout("ogs_s", [8, 128, 128])
    ors_p = dout("ors_p", [1, 2, 64, 64])
    ors_s = dout("ors_s", [8, 2, 64, 64])
    ofc_p = dout("ofc_p", [2, FF])
    ofc_s = dout("ofc_s", [4, FF])

    import os
    KDBG = os.environ.get("KDBG", "")
    if KDBG:
        dbg = dout("dbg", [5, 256, 2048], BF16)
    agin = Buf(nc.dram_tensor("agin", [5, 256, 2048], BF16), "agin", track=False)
    agout = Buf(nc.dram_tensor("agout", [5, 1024, 2048], BF16), "agout", track=False)
    cc_sem = k.st.enter_context(nc.semaphore("cc"))
    cc_n = [0]

    def exchange(piece):
        S.wait_all_dma("pool")
        nc.gpsimd.collective_compute("AllGather", ALU.bypass, replica_groups=[[0, 1, 2, 3], [4, 5, 6, 7]],
                                     ins=[agin.t.ap()[piece].opt()], outs=[agout.t.ap()[piece].opt()]).then_inc(cc_sem, 1)
        cc_n[0] += 1

    k.PSF = [k.ps("psf%d" % i, [128, 512], F32) for i in range(6)]
    k.PSB = [k.ps("psb%d" % i, [128, 1024], BF16) for i in range(2)]

    ones = k.sb("ones", [128, 512])
    onesb = k.sb("onesb", [128, 128], BF16)
    oblk = k.sb("oblk", [128, 128], BF16)
    ident = k.sb("ident", [128, 128])
    identb = k.sb("identb", [128, 128], BF16)
    tmpc = k.sb("tmpc", [128, 128])
    k.memset("pool", v(ones), 1.0)
    k.asel(v(ident), v(ones, (slice(None), slice(0, 128))), [[-1, 128]], ALU.is_equal, 0, 1)
    k.cp("pool", v(identb), v(ident))
    k.cp("pool", v(onesb), v(ones, (slice(None), slice(0, 128))))
    o128 = v(ones, (slice(None), slice(0, 128)))
    o3 = V(ones, ones.t[:, 0:128].rearrange("p (h x) -> p h x", x=64))
    t3 = V(tmpc, tmpc.t[:, :].rearrange("p (h x) -> p h x", x=64))
    k.asel(t3, o3, [[-64, 2], [0, 64]], ALU.is_ge, 0, 1)
    k.asel(t3, t3, [[64, 2], [0, 64]], ALU.is_ge, 63, -1)
    k.cp("pool", v(oblk), v(tmpc))

    MK = []
    bd = k.sb("bd16", [128, 128])
    b3o = V(ones, ones.t[:, 0:128].rearrange("p (q x) -> p q x", x=16))
    b3 = V(bd, bd.t[:, :].rearrange("p (q x) -> p q x", x=16))
    k.asel(b3, b3o, [[-16, 8], [0, 16]], ALU.is_ge, 0, 1)
    k.asel(b3, b3, [[16, 8], [0, 16]], ALU.is_ge, 15, -1)
    for mi in range(2):
        m = {}
        for nm, pat, cmp, cm in (("Ls", [[-1, 128]], ALU.is_gt, 1), ("Us", [[1, 128]], ALU.is_gt, -1),
                                 ("Ui", [[1, 128]], ALU.is_ge, -1)):
            t = k.sb("m%s%d" % (nm, mi), [128, 128])
            k.asel(v(t), o128, pat, cmp, 0, cm)
            if mi == 1:
                k.tt("pool", v(t), v(t), v(bd), ALU.mult)
            m[nm] = t
        for nm in ("Ls", "Us"):
            t = k.sb("mn%s%d" % (nm, mi), [128, 128])
            k.ts("pool", v(t), v(m[nm]), -1.0, ALU.mult)
            m["n" + nm] = t
        MK.append(m)
    colm = k.sb("colm", [128, 8, 128], BF16)
    on4 = V(ones, ones.t[:, 0:128].rearrange("p (r x) -> p r x", x=16))
    for q in range(8):
        k.asel(V(colm, colm.t[:, q, :].rearrange("p (r x) -> p r x", x=16)), on4, [[1, 8], [0, 16]],
               ALU.is_equal, -q, 0)
    rowm = k.sb("rowm", [128, 8])
    k.asel(v(rowm), v(ones, (slice(None), slice(0, 8))), [[-16, 8]], ALU.is_ge, 0, 1)
    k.asel(v(rowm), v(rowm), [[16, 8]], ALU.is_ge, 15, -1)

    epst = k.sb("epst", [128, 2])
    k.memset("pool", v(epst, (slice(None), slice(0, 1))), 1e-6)
    k.memset("pool", v(epst, (slice(None), slice(1, 2))), 64e-5)
    k.epsv = {1e-6: (epst, 0), 64e-5: (epst, 1)}
    pp = k.sb("pp", [128, 40])
    k.dma("sp", v(pp), v(ppar))
    PC = lambda i: v(pp, (slice(None), slice(i, i + 1)))
    der = k.sb("der", [128, 4])
    DC = lambda i: v(der, (slice(None), slice(i, i + 1)))
    k.act(DC(0), PC(12), AF.Exp)
    k.ts("dve", DC(0), DC(0), -1.0, ALU.mult)
    k.ts("dve", DC(1), PC(24), -1.0, ALU.mult, 1.0, ALU.add)
    flg = k.sb("flg", [128, 1])
    k.dma("sp", v(flg), v(flag))
    nbc = k.sb("nbc", [128, 3, D])
    for i in range(3):
        k.S.dma("sp", nbc.t[:, i, :], nrm.t[i:i + 1, :].partition_broadcast(128), reads=[nrm], writes=[nbc])

    import os
    KSTOP = os.environ.get("KSTOP", "")
    if KSTOP == "C0":
        S.finish("sp"); k.st.close(); return nc
    stA = ExitStack()
    A = lambda name, shape, dt=F32: k.sb(name, shape, dt, stack=stA)
    winb = A("winb", [128, 8, 1280], BF16)
    k.memset("pool", v(winb, (slice(None), slice(None), slice(1152, 1280))), 0.0)
    wv = win.t[:, :].rearrange("(kc p) c -> p kc c", p=128)
    for kc in range(8):
        S.dma("pool", winb.t[:, kc, 0:1185], wv[:, kc, 0:1185], reads=[win], writes=[winb])
    with nc.allow_non_contiguous_dma(reason="single weight column"):
        S.dma("pool", winb.t[:, :, 1216:1217], wv[:, :, 1185:1186], reads=[win], writes=[winb])
    wa2b = A("wa2b", [128, 128], BF16)
    g2ab = A("g2ab", [128, 128], BF16)
    g2bb = A("g2bb", [32, 128], BF16)
    k.dma("pool", v(wa2b), v(wa2))
    k.dma("pool", v(g2ab), v(g2a))
    k.dma("pool", v(g2bb), v(g2b))

    if KSTOP == "C1":
        S.finish("sp"); stA.close(); k.st.close(); return nc
    xt = [A("xt%d" % i, [128, D]) for i in range(2)]
    hb = [A("hb%d" % i, [128, D], BF16) for i in range(2)]
    ss = A("ss", [128, 2])
    hT = A("hT", [128, 8, 512], BF16)
    PB = {}
    for nm in ("q", "k", "v", "r", "kr", "vr", "wa", "g0", "g1"):
        t_ = A("P%s" % nm, [128, 3 + 512])
        k.memset("pool", v(t_), 0.0)
        PB[nm] = [t_, t_]
    PZ = A("PZ", [128, 512])
    W = {}
    for nm in ("cq", "ck", "cv", "t0", "t1", "t2", "t3", "betab", "lg", "gb", "eg", "bg", "kn", "qn",
               "sr", "skr", "svr", "gg", "krp", "Ep", "Ee", "Em", "Eh", "yg", "yr", "sz", "yc"):
        W[nm] = A("W" + nm, [128, 512])
    for a_, b_ in (("swa", "cq"), ("sg0", "ck"), ("sg1", "cv"), ("sig", "lg"), ("av", "bg"), ("kk", "eg"),
                   ("bv", "betab"), ("cs", "kn"), ("cse", "qn")):
        W[a_] = W[b_]
    WB = {}
    for nm in ("sqb", "knb", "kbb", "qdT", "kbgT", "vbT", "kdT", "T7", "sg0b", "sg1b", "atok", "vrb", "Bh", "Kh",
               "rk2", "ob", "qnb"):
        WB[nm] = A("B" + nm, [128, 512], BF16)
    BKt = A("BKt", [128, 4, 2, 128], BF16)
    ARt = A("ARt", [128, 4, 2, 128], BF16)
    gcol = A("gcol", [128, 4])
    glast = A("glast", [128, 8])
    gend = A("gend", [128, 8])

    CH = []
    for u in range(3):
        d = {}
        for nm in ("Nb", "NTb", "XTb", "Pb", "PTb", "Pb2", "PTb2", "Aak", "ArbT", "ArkT", "M2T"):
            d[nm] = A("c%s%d" % (nm, u), [128, 128], BF16)
        for nm in ("XT", "e1a", "e1b", "Dm", "DTm", "DTi"):
            d[nm] = A("c%s%d" % (nm, u), [128, 128])
        CH.append(d)
    kbg_t = A("kbg_t", [128, 128], BF16)
    Vb_t = A("Vb_t", [128, 128], BF16)
    kd_t = A("kd_t", [128, 128], BF16)
    AxTg = A("AxTg", [128, 128], BF16)
    vnew = A("vnew", [128, 128], BF16)
    padz = {}
    for nm in ("atokp", "Vp", "Bhp", "Khp", "Up"):
        t = A("pd" + nm, [128, 384], BF16)
        k.memset("pool", v(t), 0.0)
        padz[nm] = t
    PADV = lambda t, h: v(t, (slice(None), slice(h * 128, h * 128 + 128)))
    DIAG = lambda t: V(t, t.t[:, 0:384].rearrange("p (h x) -> p h x", x=192)[:, :, 0:64])
    AxTr = A("AxTr", [128, 128], BF16)

    def dcopy(eng, t, src, c0):
        for h in range(2):
            k.cp(eng, v(t, (slice(None), slice(h * 192, h * 192 + 64))),
                 v(src, (slice(None), slice(c0 + h * 64, c0 + h * 64 + 64))))

    Sg32 = A("Sg32", [128, 128]); Sgb = A("Sgb", [128, 128], BF16)
    Hr32 = A("Hr32", [128, 128]); Hrb = A("Hrb", [128, 128], BF16)
    k.memset("pool", v(Sg32), 0.0); k.memset("pool", v(Sgb), 0.0)
    k.memset("pool", v(Hr32), 0.0); k.memset("pool", v(Hrb), 0.0)
    S0g32 = A("S0g32", [128, 8, 128]); S0gb = A("S0gb", [128, 8, 128], BF16)
    H0r32 = A("H0r32", [128, 8, 128]); H0rb = A("H0rb", [128, 8, 128], BF16)
    mskA = A("mskA", [128, 8, 128], BF16)
    mskQ = A("mskQ", [128, 8, 128], BF16)
    mskK = A("mskK", [128, 2, 256], BF16)
    mskK2 = A("mskK2", [128, 2, 256], BF16)
    Sfin = A("Sfin", [128, 2, 128])
    COs = {8: A("CO8", [128, 3, 8, 3]), 1: A("CO1", [128, 3, 1, 3])}
    CO2s = {8: A("CO28", [128, 6, 8]), 1: A("CO21", [128, 6, 1])}
    COt = A("COt", [128, 128])
    HTt = A("HTt", [128, 128])

    def in_proj_block(tok0, NT, par, ncs, tcs, halo_src):
        ntile = NT // 128
        for i in range(ntile):
            xti = xt[i % 2]; hbi = hb[i % 2]
            k.dma("sp" if i % 2 == 0 else "act", v(xti), v(xA, (slice(tok0 + i * 128, tok0 + (i + 1) * 128), slice(None))))
            sc = v(ss, (slice(None), slice(i % 2, i % 2 + 1)))
            k.act(v(hbi), v(xti), AF.Square, scale=1.0 / 32.0, accum=sc)
            k.rsqrt(sc, sc, 1.0, 1e-6)
            k.stt("dve", v(hbi), v(xti), sc, v(nbc, (slice(None), 0, slice(None))), ALU.mult, ALU.mult)
            pb = k.psb()
            for kc in range(8):
                k.tr(v(pb, (slice(None), slice(kc * 128, (kc + 1) * 128))),
                     v(hbi, (slice(None), slice(kc * 128, (kc + 1) * 128))), v(identb), inc=(kc == 7))
            k.cp("act", v(hT, (slice(None), slice(None), slice(i * 128, (i + 1) * 128))),
                 V(pb, pb.t[:, :].rearrange("p (kc t) -> p kc t", t=128)))
        blocks = [("q", 0, 128, 3), ("k", 128, 128, 3), ("v", 256, 128, 3), ("z", 384, 128, 0),
                  ("r", 512, 128, 1), ("kr", 640, 128, 1), ("vr", 768, 128, 1), ("wa", 896, 128, 1),
                  ("g0", 1024, 128, 1), ("g1", 1152, 65, 1)]
        for nm, c0, m, halo in blocks:
            pf = k.psf()
            for kc in range(8):
                k.mm(v(pf, (slice(0, m), slice(0, NT))), v(winb, (slice(None), kc, slice(c0, c0 + m))),
                     v(hT, (slice(None), kc, slice(0, NT))), start=(kc == 0), stop=(kc == 7))
            if nm == "z":
                k.cp("act", v(PZ, (slice(None), slice(0, NT))), v(pf, (slice(None), slice(0, NT))))
                continue
            P = PB[nm][par]
            H = 3
            dst = V(P, P.t[0:m, 0:ncs * (H + tcs)].rearrange("p (s t) -> p s t", t=H + tcs)[:, :, H:H + tcs])
            src = V(pf, pf.t[0:m, 0:NT].rearrange("p (s t) -> p s t", t=tcs))
            k.cp("act", dst, src)

    def pview(P, ncs, tcs, lo, hi, m=128):
        return V(P, P.t[0:m, 0:ncs * (3 + tcs)].rearrange("p (s t) -> p s t", t=3 + tcs)[:, :, 3 + lo:3 + hi])

    def w3(t, NT, tcs, m=128):
        return V(t, t.t[0:m, 0:NT].rearrange("p (s t) -> p s t", t=tcs))

    def w2(t, NT, m=128, lo=0):
        return v(t, (slice(lo, lo + m) if lo else slice(0, m), slice(0, NT)))

    def doubling_multi(units, nlev):
        cur = {}
        for u in units:
            d = CH[u]
            k.tt("dve", v(d["XT"]), v(ident), v(d["NTb"]), ALU.add)
            k.cp("act", v(d["XTb"]), v(d["XT"]))
            cur[u] = (d["Nb"], d["NTb"])
        for m in range(1, nlev):
            last = (m == nlev - 1)
            pfs = {}
            for u in units:
                P, PT = cur[u]
                pf = k.psf()
                pfs[u] = pf
                k.mm(v(pf, (slice(None), slice(0, 128))), v(PT), v(P))
                if not last:
                    k.mm(v(pf, (slice(None), slice(128, 256))), v(P), v(PT))
            nxt = {}
            for i_, u in enumerate(units):
                d = CH[u]
                Pn = d["Pb"] if m % 2 else d["Pb2"]
                PTn = d["PTb"] if m % 2 else d["PTb2"]
                eng = "act" if i_ % 2 == 0 else "dve"
                k.cp(eng, v(Pn), v(pfs[u], (slice(None), slice(0, 128))))
                if not last:
                    k.cp(eng, v(PTn), v(pfs[u], (slice(None), slice(128, 256))))
                nxt[u] = (Pn, PTn)
            pf2s = {}
            for u in units:
                pf2 = k.psf()
                pf2s[u] = pf2
                k.mm(v(pf2, (slice(None), slice(0, 128))), v(nxt[u][0]), v(CH[u]["XTb"]))
            for u in units:
                d = CH[u]
                k.tt("dve", v(d["XT"]), v(d["XT"]), v(pf2s[u], (slice(None), slice(0, 128))), ALU.add)
                k.cp("act", v(d["XTb"]), v(d["XT"]))
            cur = nxt

    def mixer_block(NT, par, ncs, tcs, nsg, seg, mi, nseq, tokcol0, last_out):
        nch = NT // 128
        nlev = 7 if seg == 128 else 4
        nlev = int(os.environ.get("KNLEV", nlev))
        m = MK[mi]
        f = lambda t, lo=0, hi=None: v(t, (slice(None), slice(lo, NT if hi is None else hi)))
        for si, (nm, cn) in enumerate((("q", "cq"), ("k", "ck"), ("v", "cv"))):
            P = PB[nm][par]
            o3_ = w3(W[cn], NT, tcs)
            eng = "dve"
            k.ts(eng, o3_, pview(P, ncs, tcs, -3, tcs - 3), PC(si * 4 + 0), ALU.mult)
            for tap in (1, 2, 3):
                k.stt(eng, o3_, pview(P, ncs, tcs, tap - 3, tcs + tap - 3), PC(si * 4 + tap), o3_,
                      ALU.mult, ALU.add)
            k.act(f(W[cn]), f(W[cn]), AF.Silu)
        g1 = PB["g1"][par]
        for ri, prt in enumerate((32, 64)):
            pf = k.psf()
            rsrc = V(g1, g1.t[prt:prt + 1, 0:ncs * (3 + tcs)].rearrange("p (s t) -> p s t", t=3 + tcs)[:, :, 3:3 + tcs])
            k.mm(V(pf, pf.t[:, 0:NT].rearrange("p (s t) -> p s t", t=tcs)),
                 v(ones, (slice(prt, prt + 1), slice(0, 128))), rsrc)
            if ri == 0:
                k.act(f(W["betab"]), f(pf), AF.Sigmoid)
            else:
                k.act(f(W["t0"]), f(pf), AF.Exp, bias=PC(13))
                k.act(f(W["t0"]), f(W["t0"]), AF.Ln, bias=1.0)
                k.ts("dve", f(W["lg"]), f(W["t0"]), DC(0), ALU.mult)
        for sgi in range(nsg):
            k.scan(f(W["gb"], sgi * seg, (sgi + 1) * seg), f(ones, 0, seg), f(W["lg"], sgi * seg, (sgi + 1) * seg))
        k.act(f(W["eg"]), f(W["gb"]), AF.Exp)
        k.tt("pool", f(W["bg"]), f(W["betab"]), f(W["eg"]), ALU.mult)
        for nm, cn, on in (("q", "cq", "qn"), ("k", "ck", "kn")):
            k.act(f(WB["sqb"]), f(W[cn]), AF.Square)
            pf = k.psf()
            k.mm(f(pf), v(onesb), f(WB["sqb"]))
            k.rsqrt(f(W["t1"]), f(pf), 1.0, 1e-6)
            if nm == "q":
                k.stt("dve", f(W[on]), f(W[cn]), 128.0 ** -0.5, f(W["t1"]), ALU.mult, ALU.mult)
            else:
                k.tt("dve", f(W[on]), f(W[cn]), f(W["t1"]), ALU.mult)
        k.cp("pool", f(WB["knb"]), f(W["kn"]))
        k.tt("pool", f(WB["kbb"]), f(W["kn"]), f(W["betab"]), ALU.mult)
        k.tt("dve", f(WB["qdT"]), f(W["qn"]), f(W["eg"]), ALU.mult)
        k.cp("pool", f(WB["qnb"]), f(W["qn"]))
        k.tt("pool", f(WB["kbgT"]), f(W["kn"]), f(W["bg"]), ALU.mult)
        k.tt("dve", f(WB["vbT"]), f(W["cv"]), f(W["betab"]), ALU.mult)
        gb3 = w3(W["gb"], NT, seg)
        gendv = V(W["gb"], W["gb"].t[:, 0:NT].rearrange("p (s t) -> p s t", t=seg)[:, :, seg - 1:seg])
        k.tt("dve", w3(W["t2"], NT, seg), V(W["gb"], gendv.ap.to_broadcast([128, nsg, seg])), gb3, ALU.subtract)
        k.act(f(W["t2"]), f(W["t2"]), AF.Exp)
        k.tt("pool", f(WB["kdT"]), f(W["kn"]), f(W["t2"]), ALU.mult)
        k.cp("dve", v(glast, (slice(None), slice(0, nsg))),
             V(W["eg"], W["eg"].t[:, 0:NT].rearrange("p (s t) -> p s t", t=seg)[:, :, seg - 1]))
        k.act(f(W["sz"]), f(PZ), AF.Silu)

        if KSTOP == "M1":
            return
        for nm, on, mrows, mc in (("r", "sr", 128, 15), ("kr", "skr", 128, 16), ("vr", "svr", 128, 17),
                                  ("wa", "swa", 128, 18), ("g0", "sg0", 128, 19), ("g1", "sg1", 32, 20)):
            P = PB[nm][par]
            o3_ = w3(W[on], NT, tcs, mrows)
            cur = pview(P, ncs, tcs, 0, tcs, mrows)
            prv = pview(P, ncs, tcs, -1, tcs - 1, mrows)
            k.tt("pool", o3_, prv, cur, ALU.subtract)
            k.stt("dve", o3_, o3_, v(pp, (slice(0, mrows), slice(mc, mc + 1))), cur, ALU.mult, ALU.add)
        k.act(v(WB["T7"], (slice(0, 64), slice(0, NT))), v(W["swa"], (slice(0, 64), slice(0, NT))), AF.Tanh)
        k.cp("act", v(WB["T7"], (slice(64, 128), slice(0, NT))), v(W["swa"], (slice(64, 128), slice(0, NT))))
        k.act(f(WB["sg0b"]), f(W["sg0"]), AF.Sigmoid)
        k.act(v(WB["sg1b"], (slice(0, 32), slice(0, NT))), v(W["sg1"], (slice(0, 32), slice(0, NT))), AF.Sigmoid)
        pf = k.psf()
        k.mm(f(pf), v(wa2b, (slice(0, 64), slice(None))), v(WB["T7"], (slice(0, 64), slice(0, NT))))
        k.act(f(W["sig"]), f(pf), AF.Sigmoid, bias=PC(21))
        pf = k.psf()
        k.mm(f(pf), v(wa2b, (slice(64, 128), slice(None))), v(WB["T7"], (slice(64, 128), slice(0, NT))))
        k.act(f(W["av"]), f(pf), AF.Sigmoid, bias=PC(22))
        pf = k.psf()
        k.mm(f(pf), v(g2ab), f(WB["sg0b"]), start=True, stop=False)
        k.mm(f(pf), v(g2bb), v(WB["sg1b"], (slice(0, 32), slice(0, NT))), start=False, stop=True)
        k.cp("act", f(W["gg"]), f(pf))
        k.ts("dve", f(W["kk"]), f(W["skr"]), PC(23), ALU.mult)
        k.act(f(WB["sqb"]), f(W["kk"]), AF.Square)
        pf = k.psf()
        k.mm(f(pf), v(oblk), f(WB["sqb"]))
        k.rsqrt(f(W["t1"]), f(pf), 1.0, 1e-6)
        k.tt("dve", f(W["kk"]), f(W["kk"]), f(W["t1"]), ALU.mult)
        k.ts("dve", f(W["t3"]), f(W["av"]), PC(24), ALU.mult, DC(1), ALU.add)
        k.tt("dve", f(W["krp"]), f(W["skr"]), f(W["t3"]), ALU.mult)
        k.tt("pool", f(W["bv"]), f(W["kk"]), f(W["av"]), ALU.mult)
        for sgi in range(nsg):
            k.scan(f(W["cs"], sgi * seg, (sgi + 1) * seg), f(ones, 0, seg), f(W["sig"], sgi * seg, (sgi + 1) * seg))
        k.tt("pool", f(W["cse"]), f(W["cs"]), f(W["sig"]), ALU.subtract)
        k.act(f(W["Ep"]), f(W["cs"]), AF.Exp, scale=C0)
        k.act(f(W["Ee"]), f(W["cse"]), AF.Exp, scale=C0)
        k.act(f(W["Em"]), f(W["cs"]), AF.Exp, scale=-C0)
        cendv = V(W["cs"], W["cs"].t[:, 0:NT].rearrange("p (s t) -> p s t", t=seg)[:, :, seg - 1:seg])
        k.tt("dve", w3(W["Eh"], NT, seg), V(W["cs"], cendv.ap.to_broadcast([128, nsg, seg])), w3(W["cs"], NT, seg),
             ALU.subtract)
        k.act(f(W["Eh"]), f(W["Eh"]), AF.Exp, scale=C0)
        k.cp("dve", v(gend, (slice(None), slice(0, nsg))),
             V(W["Ep"], W["Ep"].t[:, 0:NT].rearrange("p (s t) -> p s t", t=seg)[:, :, seg - 1]))
        BK4 = lambda j: V(BKt, BKt.t[:, 0:nch, j, :])
        AR4 = lambda j: V(ARt, ARt.t[:, 0:nch, j, :])
        c4 = lambda t: V(t, t.t[:, 0:NT].rearrange("p (c t) -> p c t", t=128))
        k.tt("dve", BK4(0), c4(W["bv"]), c4(W["Em"]), ALU.mult)
        k.tt("pool", BK4(1), c4(W["krp"]), c4(W["Em"]), ALU.mult)
        k.stt("dve", AR4(0), c4(W["kk"]), -1.0, c4(W["Ee"]), ALU.mult, ALU.mult)
        k.tt("pool", AR4(1), c4(W["sr"]), c4(W["Ep"]), ALU.mult)
        k.stt("dve", f(WB["atok"]), f(W["kk"]), -1.0, f(W["Ee"]), ALU.mult, ALU.mult)
        k.cp("pool", f(WB["vrb"]), f(W["svr"]))
        k.tt("pool", f(WB["Bh"]), f(W["bv"]), f(W["Eh"]), ALU.mult)
        k.tt("dve", f(WB["Kh"]), f(W["krp"]), f(W["Eh"]), ALU.mult)

        if KSTOP in ("M2", "XM2"):
            return
        for c in range(nch):
            cs_ = slice(c * 128, (c + 1) * 128)
            fc = lambda t: v(t, (slice(None), cs_))
            d = CH[0]
            pf = k.psf()
            k.mm(v(pf, (slice(None), slice(0, 1))), v(W["gb"], (slice(0, 1), cs_)), v(ones, (slice(0, 1), slice(0, 1))))
            k.cp("act", v(gcol, (slice(None), slice(0, 1))), v(pf, (slice(None), slice(0, 1))))
            gbc = v(gcol, (slice(None), slice(0, 1)))
            k.ts("dve", v(d["e1a"]), fc(W["gb"]), gbc, ALU.subtract, 0.0, ALU.min)
            k.ts("dve", v(d["e1b"]), fc(W["gb"]), gbc, ALU.subtract, 0.0, ALU.max)
            k.act(v(d["e1a"]), v(d["e1a"]), AF.Exp)
            k.act(v(d["e1b"]), v(d["e1b"]), AF.Exp, scale=-1.0)
            k.tt("pool", v(d["Dm"]), v(d["e1b"]), v(m["nLs"]), ALU.mult)
            k.tt("pool", v(d["DTm"]), v(d["e1a"]), v(m["nUs"]), ALU.mult)
            k.tt("pool", v(d["DTi"]), v(d["e1a"]), v(m["Ui"]), ALU.mult)
            pf = k.psf()
            k.mm(v(pf, (slice(None), slice(0, 128))), fc(WB["kbb"]), fc(WB["knb"]))
            k.mm(v(pf, (slice(None), slice(128, 256))), fc(WB["knb"]), fc(WB["kbb"]))
            k.tt("dve", v(d["Nb"]), v(pf, (slice(None), slice(0, 128))), v(d["Dm"]), ALU.mult)
            k.tt("dve", v(d["NTb"]), v(pf, (slice(None), slice(128, 256))), v(d["DTm"]), ALU.mult)
            pf3 = k.psf()
            k.mm(v(pf3, (slice(None), slice(0, 128))), fc(WB["knb"]), fc(WB["qnb"]))
            k.tt("dve", v(d["ArbT"]), v(pf3, (slice(None), slice(0, 128))), v(d["DTi"]), ALU.mult)
            pb = k.psb()
            k.tr(v(pb, (slice(None), slice(0, 128))), fc(WB["kbgT"]), v(identb), inc=False)
            k.tr(v(pb, (slice(None), slice(128, 256))), fc(WB["vbT"]), v(identb), inc=False)
            k.tr(v(pb, (slice(None), slice(256, 384))), fc(WB["kdT"]), v(identb))
            k.cp("act", v(kbg_t), v(pb, (slice(None), slice(0, 128))))
            k.cp("act", v(Vb_t), v(pb, (slice(None), slice(128, 256))))
            k.cp("act", v(kd_t), v(pb, (slice(None), slice(256, 384))))

            if KSTOP == "M3":
                return
            for h in range(2):
                d = CH[1 + h]
                hs = slice(64 * h, 64 * h + 64)
                at = V(ARt, ARt.t[hs, c, 0, :]); rt = V(ARt, ARt.t[hs, c, 1, :])
                bt = V(BKt, BKt.t[hs, c, 0, :]); kt = V(BKt, BKt.t[hs, c, 1, :])
                pf = k.psf()
                if KSTOP == "R0a":
                    return
                k.mm(v(pf, (slice(None), slice(0, 256))), at, V(BKt, BKt.t[hs, c, :, :].rearrange("p a b -> p (a b)")))
                if KSTOP == "R0":
                    return
                k.tt("dve", v(d["Nb"]), v(pf, (slice(None), slice(0, 128))), v(m["Ls"]), ALU.mult)
                k.tt("dve", v(d["Aak"]), v(pf, (slice(None), slice(128, 256))), v(m["Ls"]), ALU.mult)
                if KSTOP == "R1":
                    return
                pf = k.psf()
                k.mm(v(pf, (slice(None), slice(0, 256))), bt, V(ARt, ARt.t[hs, c, :, :].rearrange("p a b -> p (a b)")))
                k.mm(v(pf, (slice(None), slice(256, 384))), kt, rt)
                k.tt("dve", v(d["NTb"]), v(pf, (slice(None), slice(0, 128))), v(m["Us"]), ALU.mult)
                k.tt("dve", v(d["ArbT"]), v(pf, (slice(None), slice(128, 256))), v(m["Ui"]), ALU.mult)
                k.tt("dve", v(d["ArkT"]), v(pf, (slice(None), slice(256, 384))), v(m["Ui"]), ALU.mult)
                if KSTOP == "R2":
                    return
            doubling_multi([0, 1, 2], nlev)
            d = CH[0]
            pf = k.psf()
            k.mm(v(pf, (slice(None), slice(0, 128))), v(kbg_t), v(d["XTb"]))
            k.act(v(AxTg), v(pf, (slice(None), slice(0, 128))), AF.Identity, scale=-1.0)
            for h in range(2):
                d = CH[1 + h]
                pf = k.psf()
                k.mm(v(pf, (slice(None), slice(0, 128))), v(d["Aak"]), v(d["XTb"]))
                k.cp("act", v(d["M2T"]), v(pf, (slice(None), slice(0, 128))))
            pb = k.psb()
            for i_, nm in enumerate(("atok", "vrb", "Bh", "Kh")):
                k.tr(v(pb, (slice(None), slice(i_ * 128, (i_ + 1) * 128))), fc(WB[nm]), v(identb), inc=(i_ == 3))
            if KSTOP == "R6":
                return
            import os as _os
            _v = _os.environ.get("KV", "")
            for i_, (nm, eng) in enumerate((("atokp", "act"), ("Vp", "act"), ("Bhp", "act"), ("Khp", "act"))):
                if _v == "act":
                    eng = "act"
                if _v == "dve":
                    eng = "dve"
                if _v == "one" and i_ > 0:
                    continue
                dcopy(eng, padz[nm], pb, i_ * 128)
            if KSTOP == "R7":
                return
            pf = k.psf()
            for h in range(2):
                k.mm(v(pf, (slice(None), slice(0, 128))), PADV(padz["atokp"], h), v(CH[1 + h]["XTb"]),
                     start=(h == 0), stop=(h == 1))
            k.cp("act", v(AxTr), v(pf, (slice(None), slice(0, 128))))

            if KSTOP in ("M4", "XM4"):
                return
            if nseq == 1 and KSTOP != "XS2":
                d = CH[0]
                pf = k.psf()
                k.mm(v(pf, (slice(None), slice(0, 128))), v(AxTg), v(Sgb), start=True, stop=False)
                k.mm(v(pf, (slice(None), slice(0, 128))), v(d["XTb"]), v(Vb_t), start=False, stop=True)
                k.cp("act", v(vnew), v(pf, (slice(None), slice(0, 128))))
                pf2 = k.psf()
                k.mm(v(pf2, (slice(None), slice(0, 128))), v(Sgb), fc(WB["qdT"]), start=True, stop=False)
                k.mm(v(pf2, (slice(None), slice(0, 128))), v(vnew), v(d["ArbT"]), start=False, stop=True)
                pf3 = k.psf()
                k.mm(v(pf3, (slice(None), slice(0, 128))), v(kd_t), v(vnew))
                k.stt("dve", v(Sg32), v(Sg32), v(glast, (slice(None), slice(c, c + 1))),
                      v(pf3, (slice(None), slice(0, 128))), ALU.mult, ALU.add)
                k.cp("act", v(Sgb), v(Sg32))
                k.cp("act", fc(W["yg"]), v(pf2, (slice(None), slice(0, 128))))
            if nseq == 1 and KSTOP != "XS1":
                pf = k.psf()
                k.mm(v(pf, (slice(None), slice(0, 128))), v(AxTr), v(Hrb), start=True, stop=False)
                for h in range(2):
                    k.mm(v(pf, (slice(None), slice(0, 128))), v(CH[1 + h]["M2T"]), PADV(padz["Vp"], h),
                         start=False, stop=(h == 1))
                dcopy("act", padz["Up"], pf, 0)
                pf2 = k.psf()
                k.mm(v(pf2, (slice(None), slice(0, 128))), v(Hrb), V(ARt, ARt.t[:, c, 1, :]), start=True, stop=False)
                for h in range(2):
                    k.mm(v(pf2, (slice(None), slice(0, 128))), PADV(padz["Up"], h), v(CH[1 + h]["ArbT"]),
                         start=False, stop=False)
                    k.mm(v(pf2, (slice(None), slice(0, 128))), PADV(padz["Vp"], h), v(CH[1 + h]["ArkT"]),
                         start=False, stop=(h == 1))
                for h in range(2):
                    k.mm(v(pf2, (slice(None), slice(128, 256))), PADV(padz["Bhp"], h), PADV(padz["Up"], h),
                         start=(h == 0), stop=False)
                    k.mm(v(pf2, (slice(None), slice(128, 256))), PADV(padz["Khp"], h), PADV(padz["Vp"], h),
                         start=False, stop=(h == 1))
                k.stt("dve", v(Hr32), v(Hr32), v(gend, (slice(None), slice(c, c + 1))),
                      v(pf2, (slice(None), slice(128, 256))), ALU.mult, ALU.add)
                k.cp("act", v(Hrb), v(Hr32))
                k.cp("act", fc(W["yr"]), v(pf2, (slice(None), slice(0, 128))))
            if nseq != 1:
                d = CH[0]
                for q in range(8):
                    k.tt("pool", v(mskA, (slice(None), q, slice(None))), v(AxTg), v(colm, (slice(None), q, slice(None))), ALU.mult)
                    k.tt("dve", v(mskQ, (slice(None), q, slice(None))), fc(WB["qdT"]), v(colm, (slice(None), q, slice(None))), ALU.mult)
                pf = k.psf()
                for q in range(8):
                    k.mm(v(pf, (slice(None), slice(0, 128))), v(mskA, (slice(None), q, slice(None))),
                         v(S0gb, (slice(None), q, slice(None))), start=(q == 0), stop=False)
                k.mm(v(pf, (slice(None), slice(0, 128))), v(d["XTb"]), v(Vb_t), start=False, stop=True)
                k.cp("act", v(vnew), v(pf, (slice(None), slice(0, 128))))
                pf2 = k.psf()
                for q in range(8):
                    k.mm(v(pf2, (slice(None), slice(0, 128))), v(S0gb, (slice(None), q, slice(None))),
                         v(mskQ, (slice(None), q, slice(None))), start=(q == 0), stop=False)
                k.mm(v(pf2, (slice(None), slice(0, 128))), v(vnew), v(d["ArbT"]), start=False, stop=True)
                k.cp("act", fc(W["yg"]), v(pf2, (slice(None), slice(0, 128))))
                for q in range(8):
                    k.ts("pool", v(mskK, (slice(None), q % 2, slice(0, 128))), v(kd_t), v(rowm, (slice(None), slice(q, q + 1))), ALU.mult)
                    pf3 = k.psf()
                    k.mm(v(pf3, (slice(None), slice(0, 128))), v(mskK, (slice(None), q % 2, slice(0, 128))), v(vnew))
                    k.stt("dve", v(Sfin, (slice(None), q % 2, slice(None))), v(S0g32, (slice(None), q, slice(None))),
                          v(glast, (slice(None), slice(q, q + 1))), v(pf3, (slice(None), slice(0, 128))),
                          ALU.mult, ALU.add)
                    k.dma("sp", V(ogs_s, ogs_s.t[q, :, :]), v(Sfin, (slice(None), q % 2, slice(None))))
                for q in range(8):
                    k.tt("pool", v(mskA, (slice(None), q, slice(None))), v(AxTr), v(colm, (slice(None), q, slice(None))), ALU.mult)
                    k.tt("dve", v(mskQ, (slice(None), q, slice(None))), V(ARt, ARt.t[:, c, 1, :]),
                         v(colm, (slice(None), q, slice(None))), ALU.mult)
                pf = k.psf()
                for q in range(8):
                    k.mm(v(pf, (slice(None), slice(0, 128))), v(mskA, (slice(None), q, slice(None))),
                         v(H0rb, (slice(None), q, slice(None))), start=(q == 0), stop=False)
                for h in range(2):
                    k.mm(v(pf, (slice(None), slice(0, 128))), v(CH[1 + h]["M2T"]), PADV(padz["Vp"], h),
                         start=False, stop=(h == 1))
                dcopy("act", padz["Up"], pf, 0)
                pf2 = k.psf()
                for q in range(8):
                    k.mm(v(pf2, (slice(None), slice(0, 128))), v(H0rb, (slice(None), q, slice(None))),
                         v(mskQ, (slice(None), q, slice(None))), start=(q == 0), stop=False)
                for h in range(2):
                    k.mm(v(pf2, (slice(None), slice(0, 128))), PADV(padz["Up"], h), v(CH[1 + h]["ArbT"]),
                         start=False, stop=False)
                    k.mm(v(pf2, (slice(None), slice(0, 128))), PADV(padz["Vp"], h), v(CH[1 + h]["ArkT"]),
                         start=False, stop=(h == 1))
                k.cp("act", fc(W["yr"]), v(pf2, (slice(None), slice(0, 128))))
                for q in range(8):
                    k.ts("pool", v(mskK, (slice(None), q % 2, slice(None))), v(padz["Bhp"], (slice(None), slice(0, 256))),
                         v(rowm, (slice(None), slice(q, q + 1))), ALU.mult)
                    k.ts("dve", v(mskK2, (slice(None), q % 2, slice(None))), v(padz["Khp"], (slice(None), slice(0, 256))),
                         v(rowm, (slice(None), slice(q, q + 1))), ALU.mult)
                    pf3 = k.psf()
                    for h in range(2):
                        k.mm(v(pf3, (slice(None), slice(0, 128))), v(mskK, (slice(None), q % 2, slice(h * 128, h * 128 + 128))),
                             PADV(padz["Up"], h), start=(h == 0), stop=False)
                        k.mm(v(pf3, (slice(None), slice(0, 128))), v(mskK2, (slice(None), q % 2, slice(h * 128, h * 128 + 128))),
                             PADV(padz["Vp"], h), start=False, stop=(h == 1))
                    k.stt("dve", v(Sfin, (slice(None), q % 2, slice(None))), v(H0r32, (slice(None), q, slice(None))),
                          v(gend, (slice(None), slice(q, q + 1))), v(pf3, (slice(None), slice(0, 128))),
                          ALU.mult, ALU.add)
                    pf4 = k.psf()
                    k.tr(v(pf4, (slice(None), slice(0, 128))), v(Sfin, (slice(None), q % 2, slice(None))), v(ident))
                    k.cp("act", v(HTt), v(pf4, (slice(None), slice(0, 128))))
                    for h in range(2):
                        k.dma("sp", V(ors_s, ors_s.t[q, h, :, :]),
                              v(HTt, (slice(64 * h, 64 * h + 64), slice(64 * h, 64 * h + 64))))

        if KSTOP in ("M5", "XM5", "XS1", "XS2"):
            return
        k.act(f(WB["sqb"]), f(W["yg"]), AF.Square)
        pf = k.psf()
        k.mm(f(pf), v(onesb), f(WB["sqb"]))
        k.rsqrt(f(W["t1"]), f(pf), 1.0 / 128.0, 1e-6)
        k.tt("dve", f(W["t0"]), f(W["yg"]), f(W["t1"]), ALU.mult)
        k.stt("dve", f(WB["ob"]), f(W["t0"]), PC(14), f(W["sz"]), ALU.mult, ALU.mult)
        pc_, pcol_ = tokcol0 // 2048, tokcol0 % 2048
        k.dma("sp", V(agin, agin.t[pc_, 0:128, pcol_:pcol_ + NT]), f(WB["ob"]))
        k.cp("pool", f(WB["sqb"]), f(W["yr"]))
        pf = k.psf()
        k.mm(f(pf), v(oblk), f(WB["sqb"]))
        k.stt("dve", f(W["yc"]), f(pf), -1.0 / 64.0, f(W["yr"]), ALU.mult, ALU.add)
        k.act(f(WB["sqb"]), f(W["yc"]), AF.Square)
        pf = k.psf()
        k.mm(f(pf), v(oblk), f(WB["sqb"]))
        k.rsqrt(f(W["t1"]), f(pf), 1.0 / 64.0, 64e-5)
        k.tt("dve", f(W["yc"]), f(W["yc"]), f(W["t1"]), ALU.mult)
        k.ts("dve", f(W["yc"]), f(W["yc"]), PC(26), ALU.mult, PC(27), ALU.add)
        k.tt("pool", f(W["t0"]), f(W["sr"]), f(W["krp"]), ALU.mult)
        k.ts("pool", f(WB["rk2"]), f(W["t0"]), PC(25), ALU.mult)
        pf = k.psf()
        k.mm(f(pf), v(oblk), f(WB["rk2"]))
        k.tt("dve", f(W["t0"]), f(pf), f(W["svr"]), ALU.mult)
        k.tt("dve", f(W["t0"]), f(W["t0"]), f(W["yc"]), ALU.add)
        k.tt("dve", f(WB["kbb"]), f(W["t0"]), f(W["gg"]), ALU.mult)
        k.dma("sp", V(agin, agin.t[pc_, 128:256, pcol_:pcol_ + NT]), f(WB["kbb"]))

        if last_out is not None:
            og, osh = last_out
            CO = COs[ncs]; CO2 = CO2s[ncs]
            for si, nm in enumerate(("q", "k", "v")):
                k.cp("pool", v(CO, (slice(None), si, slice(0, ncs), slice(None))),
                     pview(PB[nm][par], ncs, tcs, tcs - 3, tcs))
            for bi, nm in enumerate(("r", "kr", "vr", "wa", "g0", "g1")):
                k.cp("pool", v(CO2, (slice(None), bi, slice(0, ncs))),
                     V(PB[nm][par], PB[nm][par].t[:, 0:ncs * (3 + tcs)].rearrange("p (s t) -> p s t", t=3 + tcs)[:, :, 3 + tcs - 1]))
            if KSTOP == "L1":
                return
            n1 = 9 * ncs
            pf = k.psf()
            k.mm(v(pf, (slice(0, n1), slice(0, 128))),
                 V(CO, CO.t[:, :, :, :].rearrange("p a s t -> p (a s t)")), v(ident))
            k.cp("act", v(COt, (slice(0, n1), slice(None))), v(pf, (slice(0, n1), slice(0, 128))))
            if KSTOP == "L2":
                return
            k.dma("sp", v(og), v(COt, (slice(0, n1), slice(None))))
            if KSTOP == "L3":
                return
            n2 = 6 * ncs
            pf = k.psf()
            k.mm(v(pf, (slice(0, n2), slice(0, 128))),
                 V(CO2, CO2.t[:, :, :].rearrange("p a s -> p (a s)")), v(ident))
            k.cp("act", v(COt, (slice(0, n2), slice(None))), v(pf, (slice(0, n2), slice(0, 128))))
            k.dma("sp", v(osh), v(COt, (slice(0, n2), slice(None))))

    import os
    KSTOP = os.environ.get("KSTOP", "")
    KNB = int(os.environ.get("KNB", NBLK_A))
    KSK = int(os.environ.get("KSKIP", -1))
    for blk in range(KNB):
        if blk == KSK:
            continue
        par = 0
        if blk > 0:
            for nm in PB:
                k.cp("pool", v(PB[nm][0], (slice(None), slice(0, 3))), v(PB[nm][0], (slice(None), slice(512, 515))))
        in_proj_block((blk * 512) if not (os.environ.get("KREP", "") and blk == 15) else 0, 512, par, 1, 512, None)
        if KSTOP == "IP":
            S.finish("sp"); stA.close(); k.st.close(); return nc
        mixer_block(512, par, 1, 512, 4, 128, 0, 1, blk * 512,
                    (og_p, osh_p) if (blk == NBLK_A - 1 and KSTOP != "NOLO") else None)
        if blk % 4 == 3 and not KSTOP:
            exchange(blk // 4)
        if KSTOP[:1] in ("M", "R") or (KSTOP[:1] in ("L", "N", "X") and blk == NBLK_A - 1):
            if KDBG:
                S.wait_all_dma("sp")
                k.dma("sp", v(dbg), v(agin))
            S.finish("sp"); stA.close(); k.st.close(); return nc
    k.dma("sp", V(ogs_p, ogs_p.t[0, :, :]), v(Sg32))
    pf = k.psf()
    k.tr(v(pf, (slice(None), slice(0, 128))), v(Hr32), v(ident))
    k.cp("act", v(HTt), v(pf, (slice(None), slice(0, 128))))
    for h in range(2):
        k.dma("sp", V(ors_p, ors_p.t[0, h, :, :]), v(HTt, (slice(64 * h, 64 * h + 64), slice(64 * h, 64 * h + 64))))

    if KSTOP == "A1":
        if KDBG:
            S.wait_all_dma("sp")
            k.dma("sp", v(dbg), v(agin))
        S.finish("sp"); stA.close(); k.st.close(); return nc
    par = 0
    for si, nm in enumerate(("q", "k", "v")):
        P = PB[nm][par]
        k.dma("sp", V(P, P.t[:, 0:8 * 19].rearrange("p (s t) -> p s t", t=19)[:, :, 0:3]), v(sgc, (slice(None), si, slice(None), slice(None))))
    for bi, nm in enumerate(("r", "kr", "vr", "wa", "g0", "g1")):
        P = PB[nm][par]
        k.dma("sp", V(P, P.t[:, 0:8 * 19].rearrange("p (s t) -> p s t", t=19)[:, :, 2]), v(ssh, (slice(None), bi, slice(None))),
              allow_slow_non_contiguous=True)
    k.dma("sp", v(S0g32), v(sgs))
    k.cp("act", v(S0gb), v(S0g32))
    k.memset("pool", v(H0r32), 0.0)
    for h in range(2):
        k.dma("sp", v(H0r32, (slice(64 * h, 64 * h + 64), slice(None), slice(64 * h, 64 * h + 64))),
              v(srs, (slice(64 * h, 64 * h + 64), slice(None), slice(None))))
    k.cp("act", v(H0rb), v(H0r32))
    in_proj_block(SEQ, 128, par, 8, 16, None)
    if KSTOP == "A2":
        S.finish("sp"); stA.close(); k.st.close(); return nc
    mixer_block(128, par, 8, 16, 8, 16, 1, 8, SEQ, (og_s, osh_s))
    if KSTOP == "A3":
        S.finish("sp"); stA.close(); k.st.close(); return nc

    exchange(4)
    for en in ("pool", "sp"):
        S.engs[en]["e"].wait_ge(cc_sem, cc_n[0])
    stA.close()
    if KSTOP == "AG":
        S.finish("sp"); k.st.close(); return nc

    stB = ExitStack()
    Bt = lambda name, shape, dt=F32: k.sb(name, shape, dt, stack=stB)
    woutb = Bt("woutb", [128, 8, D], BF16)
    wupb = Bt("wupb", [128, 8, 2 * FF], BF16)
    wdnb = Bt("wdnb", [128, NFC, D], BF16)
    wo_v = wout.t[:, :].rearrange("(kc p) c -> p kc c", p=128)
    wu_v = wup.t[:, :].rearrange("(kc p) c -> p kc c", p=128)
    wd_v = wdn.t[:, :].rearrange("(fc p) c -> p fc c", p=128)
    for kc in range(8):
        S.dma("pool", woutb.t[:, kc, :], wo_v[:, kc, :], reads=[wout], writes=[woutb])
    for kc in range(8):
        for hf in range(4):
            S.dma("pool", wupb.t[:, kc, hf * 1408:(hf + 1) * 1408], wu_v[:, kc, hf * 1408:(hf + 1) * 1408],
                  reads=[wup], writes=[wupb])
    for fc_ in range(NFC):
        S.dma("pool", wdnb.t[:, fc_, :], wd_v[:, fc_, :], reads=[wdn], writes=[wdnb])
    fcwt = Bt("fcwt", [128, NFC, 3])
    k.dma("sp", v(fcwt), v(fcw))
    SH = Bt("SH", [128, NFC, 2, 2])
    k.dma("sp", v(SH), v(sfc))
    GH = Bt("GH", [128, NFC, 2])
    FO = Bt("FO", [128, NFC, 4])
    FOt = Bt("FOt", [4, 2, 128])
    x1 = [Bt("x1_%d" % i, [128, D]) for i in range(2)]
    h2b = Bt("h2b", [128, D], BF16)
    ssB = Bt("ssB", [128, 2])
    h2T = [Bt("h2T%d" % i, [128, 8, 128], BF16) for i in range(2)]
    oTb = [Bt("oTb%d" % i, [128, 8, 256], BF16) for i in range(2)]
    Ab = Bt("Ab", [128, NFC, 128], BF16)
    Gt = [Bt("Gt%d" % i, [128, 2 * 18 + 100]) for i in range(3)]
    Gc = [Bt("Gc%d" % i, [128, 128]) for i in range(2)]

    regs = []
    for i in range(3):
        r = nc.sync.alloc_register("off%d" % i)
        nc.sync.reg_load(r, offs.t[0:1, i:i + 1])
        regs.append(r)
    toff = nc.sync.snap(regs[0], min_val=0, max_val=3)
    hoff = nc.sync.snap(regs[1], min_val=0, max_val=3)
    soff = nc.sync.snap(regs[2], min_val=0, max_val=96)

    def load_oT(oT, prow, c0, n, dyn_col=None):
        src = agout.t[bass.ds(prow, 1), :, :] if not isinstance(prow, int) else agout.t[prow:prow + 1, :, :]
        if dyn_col is None:
            src = src[:, :, c0:c0 + n]
        else:
            src = src[:, :, c0:2048][:, :, bass.ds(dyn_col, n)]
        S.dma("sp", oT.t[:, :, 0:n], src.rearrange("o (ch p) t -> p (o ch) t", p=128), reads=[agout], writes=[oT])

    def ffn_block(bi, xrow0, mt, ncs, tcs, oT, ocol, mode, yrow0):
        xb_ = x1[bi % 2]; hT2 = h2T[bi % 2]
        k.dma("sp", v(xb_, (slice(0, mt), slice(None))), v(xB, (slice(xrow0, xrow0 + mt), slice(None))))
        for hf in range(2):
            pf = k.psf()
            for ch in range(8):
                k.mm(v(pf, (slice(0, mt), slice(None))), v(oT, (slice(None), ch, slice(ocol, ocol + mt))),
                     v(woutb, (slice(None), ch, slice(hf * 512, (hf + 1) * 512))), start=(ch == 0), stop=(ch == 7))
            k.tt("dve", v(xb_, (slice(0, mt), slice(hf * 512, (hf + 1) * 512))),
                 v(xb_, (slice(0, mt), slice(hf * 512, (hf + 1) * 512))), v(pf, (slice(0, mt), slice(None))), ALU.add)
        sc = v(ssB, (slice(0, mt), slice(bi % 2, bi % 2 + 1)))
        k.act(v(h2b, (slice(0, mt), slice(None))), v(xb_, (slice(0, mt), slice(None))), AF.Square, scale=1.0 / 32.0, accum=sc)
        k.rsqrt(sc, sc, 1.0, 1e-6)
        k.stt("dve", v(h2b, (slice(0, mt), slice(None))), v(xb_, (slice(0, mt), slice(None))), sc,
              v(nbc, (slice(0, mt), 1, slice(None))), ALU.mult, ALU.mult)
        if mt == 128:
            pb = k.psb()
            for kc in range(8):
                k.tr(v(pb, (slice(None), slice(kc * 128, (kc + 1) * 128))), v(h2b, (slice(None), slice(kc * 128, (kc + 1) * 128))),
                     v(identb), inc=(kc == 7))
            k.cp("act", v(hT2), V(pb, pb.t[:, :].rearrange("p (kc t) -> p kc t", t=128)))
        else:
            pq = k.psf()
            for kc in range(8):
                k.mm(v(pq, (slice(None), slice(kc * mt, (kc + 1) * mt))), v(h2b, (slice(0, mt), slice(kc * 128, (kc + 1) * 128))),
                     v(identb, (slice(0, mt), slice(0, mt))), inc=(kc == 7))
            k.cp("act", v(hT2, (slice(None), slice(None), slice(0, mt))),
                 V(pq, pq.t[:, 0:8 * mt].rearrange("p (kc t) -> p kc t", t=mt)))
        W_ = 2 + tcs
        for fc_ in range(NFC):
            pf = k.psf()
            for kc in range(8):
                k.mm(v(pf, (slice(None), slice(0, mt))), v(wupb, (slice(None), kc, slice(fc_ * 128, (fc_ + 1) * 128))),
                     v(hT2, (slice(None), kc, slice(0, mt))), start=(kc == 0), stop=(kc == 7))
            if mode == "halo":
                k.ts("dve", v(GH, (slice(None), fc_, slice(None))), v(pf, (slice(None), slice(0, 2))), v(flg), ALU.mult)
                continue
            for kc in range(8):
                k.mm(v(pf, (slice(None), slice(256, 256 + mt))),
                     v(wupb, (slice(None), kc, slice(FF + fc_ * 128, FF + (fc_ + 1) * 128))),
                     v(hT2, (slice(None), kc, slice(0, mt))), start=(kc == 0), stop=(kc == 7))
            G = Gt[fc_ % 3]
            g3 = V(G, G.t[:, 0:ncs * W_].rearrange("p (s t) -> p s t", t=W_))
            k.cp("act", V(G, g3.ap[:, :, 2:W_]), V(pf, pf.t[:, 0:mt].rearrange("p (s t) -> p s t", t=tcs)))
            if mode == "prompt":
                k.cp("pool", v(G, (slice(None), slice(0, 2))), v(GH, (slice(None), fc_, slice(None))))
            else:
                k.cp("pool", V(G, g3.ap[:, :, 0:2]), v(SH, (slice(None), fc_, slice(None), slice(None))))
            gc = Gc[fc_ % 2]
            gc3 = V(gc, gc.t[:, 0:mt].rearrange("p (s t) -> p s t", t=tcs))
            k.ts("dve", gc3, V(G, g3.ap[:, :, 0:tcs]), v(fcwt, (slice(None), fc_, slice(0, 1))), ALU.mult)
            k.stt("dve", gc3, V(G, g3.ap[:, :, 1:1 + tcs]), v(fcwt, (slice(None), fc_, slice(1, 2))), gc3, ALU.mult, ALU.add)
            k.stt("dve", gc3, V(G, g3.ap[:, :, 2:2 + tcs]), v(fcwt, (slice(None), fc_, slice(2, 3))), gc3, ALU.mult, ALU.add)
            k.act(v(gc, (slice(None), slice(0, mt))), v(gc, (slice(None), slice(0, mt))), AF.Silu)
            k.tt("dve", v(Ab, (slice(None), fc_, slice(0, mt))), v(gc, (slice(None), slice(0, mt))),
                 v(pf, (slice(None), slice(256, 256 + mt))), ALU.mult)
            if mode == "prompt":
                k.cp("pool", v(GH, (slice(None), fc_, slice(None))), v(G, (slice(None), slice(tcs, tcs + 2))))
                k.cp("pool", v(FO, (slice(None), fc_, slice(0, 2))), v(G, (slice(None), slice(tcs, tcs + 2))))
            else:
                k.cp("pool", V(FO, FO.t[:, fc_, :].rearrange("p (s t) -> p s t", t=2)), V(G, g3.ap[:, :, tcs:tcs + 2]))
        if mode == "halo":
            return
        for hf in range(2):
            pf = k.psf()
            for fc_ in range(NFC):
                k.mm(v(pf, (slice(0, mt), slice(None))), v(Ab, (slice(None), fc_, slice(0, mt))),
                     v(wdnb, (slice(None), fc_, slice(hf * 512, (hf + 1) * 512))), start=(fc_ == 0), stop=(fc_ == NFC - 1))
            k.tt("dve", v(xb_, (slice(0, mt), slice(hf * 512, (hf + 1) * 512))),
                 v(xb_, (slice(0, mt), slice(hf * 512, (hf + 1) * 512))), v(pf, (slice(0, mt), slice(None))), ALU.add)
        k.act(v(h2b, (slice(0, mt), slice(None))), v(xb_, (slice(0, mt), slice(None))), AF.Square, scale=1.0 / 32.0, accum=sc)
        k.rsqrt(sc, sc, 1.0, 1e-6)
        k.stt("dve", v(xb_, (slice(0, mt), slice(None))), v(xb_, (slice(0, mt), slice(None))), sc,
              v(nbc, (slice(0, mt), 2, slice(None))), ALU.mult, ALU.mult)
        k.dma("sp", v(yB, (slice(yrow0, yrow0 + mt), slice(None))), v(xb_, (slice(0, mt), slice(None))))

    def ffn_conv_out(dst, n):
        for fc_ in range(NFC):
            pf = k.psf()
            k.mm(v(pf, (slice(0, n), slice(0, 128))), v(FO, (slice(None), fc_, slice(0, n))), v(ident))
            k.cp("act", v(FOt, (slice(0, n), fc_ % 2, slice(None))), v(pf, (slice(0, n), slice(0, 128))))
            k.dma("sp", v(dst, (slice(0, n), slice(fc_ * 128, (fc_ + 1) * 128))), v(FOt, (slice(0, n), fc_ % 2, slice(None))))

    load_oT(oTb[1], hoff, 2046, 2)
    ffn_block(0, 0, 2, 1, 2, oTb[1], 0, "halo", 0)
    for t_ in range(NTB // 128):
        ob_ = oTb[(t_ // 2) % 2]
        if t_ % 2 == 0:
            load_oT(ob_, toff, t_ * 128, 256)
        ffn_block(1 + t_, 2 + t_ * 128, 128, 1, 128, ob_, (t_ % 2) * 128, "prompt", t_ * 128)
    ffn_conv_out(ofc_p, 2)
    load_oT(oTb[0], 4, 0, 32, dyn_col=soff)
    ffn_block(17, 2 + NTB, 32, 2, 16, oTb[0], 0, "sample", NTB)
    ffn_conv_out(ofc_s, 4)

    S.finish("sp")
    stB.close()
    k.st.close()
    return nc


_CACHE = {}


def kernel(**inp):
    f = lambda a: np.ascontiguousarray(np.asarray(a, dtype=np.float32))
    xp = f(inp["x_prompt"]); xs = f(inp["x_sample"])
    w_in = f(inp["w_in"])[0]
    GP = 2056
    in_maps = []
    for c in range(NCORES):
        b, j = c // 4, c % 4
        m = {}
        m["xA"] = np.concatenate([xp[b], xs[8 * b:8 * b + 8].reshape(128, D)], 0)
        hr = xp[b, NTB * j - 2:NTB * j] if j > 0 else xp[b, 0:2]
        m["xB"] = np.concatenate([hr, xp[b, NTB * j:NTB * (j + 1)], xs[2 * c:2 * c + 2].reshape(32, D)], 0)
        cols = []
        for sec in range(4):
            cols.append(np.arange(sec * 512 + j * 128, sec * 512 + (j + 1) * 128))
        for sec in range(3):
            cols.append(GP + np.arange(sec * 512 + j * 128, sec * 512 + (j + 1) * 128))
        cols.append(GP + np.arange(1536, 1536 + 288))
        cols.append(np.array([2048 + j, 2052 + j]))
        cols = np.concatenate(cols)
        m["win"] = np.ascontiguousarray(w_in[:, cols])
        pp = np.empty((128, 40), np.float32)
        cw = f(inp["gdn_conv_w"])[0]
        for sec in range(3):
            for tap in range(4):
                pp[:, sec * 4 + tap] = cw[tap, sec * 512 + j * 128: sec * 512 + (j + 1) * 128]
        pp[:, 12] = f(inp["gdn_a_log"])[0, j]
        pp[:, 13] = f(inp["gdn_dt_bias"])[0, j]
        pp[:, 14] = f(inp["gdn_norm"])[0]
        mu = f(inp["rwkv_mu"])[0]
        for sec in range(3):
            pp[:, 15 + sec] = mu[sec * 512 + j * 128: sec * 512 + (j + 1) * 128]
        pp[:, 18] = mu[1536:1664]
        pp[:, 19] = mu[1664:1792]
        pp[:, 20] = np.tile(mu[1792:1824], 4)
        sl = slice(j * 128, (j + 1) * 128)
        pp[:, 21] = f(inp["rwkv_w0"])[0, sl]
        pp[:, 22] = f(inp["rwkv_a0"])[0, sl]
        pp[:, 23] = f(inp["rwkv_k_k"])[0, sl]
        pp[:, 24] = f(inp["rwkv_k_a"])[0, sl]
        pp[:, 25] = f(inp["rwkv_r_k"])[0].reshape(512)[sl]
        pp[:, 26] = f(inp["rwkv_ln_w"])[0, sl]
        pp[:, 27] = f(inp["rwkv_ln_b"])[0, sl]
        pp[:, 28:] = pp[:, 0:12]
        m["ppar"] = pp
        m["wa2"] = np.ascontiguousarray(np.concatenate([f(inp["rwkv_w2"])[0][:, sl], f(inp["rwkv_a2"])[0][:, sl]], 0))
        g2 = f(inp["rwkv_g2"])[0][:, sl]
        m["g2a"] = np.ascontiguousarray(g2[0:128]); m["g2b"] = np.ascontiguousarray(g2[128:160])
        m["nrm"] = np.stack([f(inp["norm_mix"])[0], f(inp["norm_ffn"])[0], f(inp["norm_final"])], 0)
        wo = f(inp["w_out"])[0]
        rows = np.concatenate([np.concatenate([np.arange(jr * 128, (jr + 1) * 128),
                                               512 + np.arange(jr * 128, (jr + 1) * 128)]) for jr in range(4)])
        m["wout"] = np.ascontiguousarray(wo[rows])
        m["wup"] = f(inp["w_up"])[0]; m["wdn"] = f(inp["w_down"])[0]
        seqs = slice(8 * b, 8 * b + 8)
        gc = f(inp["state_gdn_conv"])[0, seqs]
        m["sgc"] = np.ascontiguousarray(np.stack([gc[:, :, sec * 512 + j * 128: sec * 512 + (j + 1) * 128]
                                                  for sec in range(3)], 0).transpose(3, 0, 1, 2))
        sh = f(inp["state_rwkv_shift"])[0, seqs, 0]
        blks = [sh[:, sec * 512 + j * 128: sec * 512 + (j + 1) * 128] for sec in range(3)]
        blks += [sh[:, 1536:1664], sh[:, 1664:1792], np.tile(sh[:, 1792:1824], (1, 4))]
        m["ssh"] = np.ascontiguousarray(np.stack(blks, 0).transpose(2, 0, 1))
        m["sgs"] = np.ascontiguousarray(f(inp["state_gdn"])[0, seqs, j].transpose(1, 0, 2))
        rs = f(inp["state_rwkv"])[0, seqs, 2 * j:2 * j + 2]
        m["srs"] = np.ascontiguousarray(rs.transpose(1, 3, 0, 2).reshape(128, 8, 64))
        fs = f(inp["state_ffn_conv"])[0, 2 * c:2 * c + 2]
        m["sfc"] = np.ascontiguousarray(fs.reshape(2, 2, NFC, 128).transpose(3, 2, 0, 1))
        m["fcw"] = np.ascontiguousarray(f(inp["ffn_conv_w"])[0].reshape(3, NFC, 128).transpose(2, 1, 0))
        m["offs"] = np.array([[j, max(j - 1, 0), 32 * j, 0]], np.int32)
        m["flag"] = np.full((128, 1), 1.0 if j > 0 else 0.0, np.float32)
        in_maps.append(m)
    if "nc" not in _CACHE:
        _CACHE["nc"] = build()
    res = run_bass_kernel_spmd(_CACHE["nc"], in_maps, core_ids=list(range(NCORES)))
    R = res.results
    y_p = np.empty((2, SEQ, D), np.float32); y_s = np.empty((NSAMP, TS, D), np.float32)
    gcp = np.empty((1, 2, 3, 1536), np.float32); gcs = np.empty((1, NSAMP, 3, 1536), np.float32)
    gsp = np.empty((1, 2, 4, 128, 128), np.float32); gss = np.empty((1, NSAMP, 4, 128, 128), np.float32)
    shp = np.empty((1, 2, 1, 1824), np.float32); shs = np.empty((1, NSAMP, 1, 1824), np.float32)
    rsp = np.empty((1, 2, 8, 64, 64), np.float32); rss = np.empty((1, NSAMP, 8, 64, 64), np.float32)
    fcp = np.empty((1, 2, 2, FF), np.float32); fcs = np.empty((1, NSAMP, 2, FF), np.float32)
    for c in range(NCORES):
        b, j = c // 4, c % 4
        r = R[c]
        y = np.asarray(r["yB"])
        y_p[b, NTB * j:NTB * (j + 1)] = y[0:NTB]
        y_s[2 * c:2 * c + 2] = y[NTB:NTB + 32].reshape(2, TS, D)
        og = np.asarray(r["og_p"]).reshape(3, 3, 128)
        ogs_ = np.asarray(r["og_s"]).reshape(3, 8, 3, 128)
        for sec in range(3):
            cs = slice(sec * 512 + j * 128, sec * 512 + (j + 1) * 128)
            gcp[0, b, :, cs] = og[sec]
            gcs[0, 8 * b:8 * b + 8, :, cs] = ogs_[sec]
        gsp[0, b, j] = np.asarray(r["ogs_p"])[0]
        gss[0, 8 * b:8 * b + 8, j] = np.asarray(r["ogs_s"])
        osh = np.asarray(r["osh_p"])
        oshs = np.asarray(r["osh_s"]).reshape(6, 8, 128)
        for sec in range(3):
            cs = slice(sec * 512 + j * 128, sec * 512 + (j + 1) * 128)
            shp[0, b, 0, cs] = osh[sec]
            shs[0, 8 * b:8 * b + 8, 0, cs] = oshs[sec]
        if j == 0:
            shp[0, b, 0, 1536:1664] = osh[3]; shp[0, b, 0, 1664:1792] = osh[4]; shp[0, b, 0, 1792:1824] = osh[5, 0:32]
            shs[0, 8 * b:8 * b + 8, 0, 1536:1664] = oshs[3]; shs[0, 8 * b:8 * b + 8, 0, 1664:1792] = oshs[4]
            shs[0, 8 * b:8 * b + 8, 0, 1792:1824] = oshs[5][:, 0:32]
        rsp[0, b, 2 * j:2 * j + 2] = np.asarray(r["ors_p"])[0]
        rss[0, 8 * b:8 * b + 8, 2 * j:2 * j + 2] = np.asarray(r["ors_s"])
        if j == 3:
            fcp[0, b] = np.asarray(r["ofc_p"])
        fcs[0, 2 * c:2 * c + 2] = np.asarray(r["ofc_s"]).reshape(2, 2, FF)
    return (y_p, y_s, gcp, gcs, gsp, gss, shp, shs, rsp, rss, fcp, fcs)
```
